# Optimizing a Trainium2 kernel written in Bass

```python
import math
import jax, jax.numpy as jnp
from jax import lax
import numpy as np

D_MODEL = 1024
BATCH = 4
SEQ = 4096
DEPTH = 2

CTX_LEN = 256
GRID_W = 64
EPS = 1e-6
N_MOD = 6

N_MIXERS = 4
GROUP_WIDTH = D_MODEL // N_MIXERS
MIX_WIDTH = N_MIXERS * GROUP_WIDTH

SSD_HEAD_DIM = 64
SSD_HEADS = GROUP_WIDTH // SSD_HEAD_DIM
SSD_GROUPS = 2
SSD_STATE = 128
SSD_CONV = 4
SSD_CHUNK = 128
SSD_CONV_CH = GROUP_WIDTH + 2 * SSD_GROUPS * SSD_STATE
SSD_IN = GROUP_WIDTH + SSD_CONV_CH + 2 * SSD_HEADS

LRU_WIDTH = GROUP_WIDTH
LRU_BLOCKS = 4
LRU_BLOCK_W = LRU_WIDTH // LRU_BLOCKS
LRU_CONV = 4
LRU_C = 8.0
LRU_IN = 2 * LRU_WIDTH

ATTN_HEAD_DIM = 64
ATTN_Q_HEADS = GROUP_WIDTH // ATTN_HEAD_DIM
ATTN_KV_HEADS = 2
ATTN_GQA = ATTN_Q_HEADS // ATTN_KV_HEADS
ATTN_IN = (ATTN_Q_HEADS + 2 * ATTN_KV_HEADS) * ATTN_HEAD_DIM
Q_BLOCK = 128
WINDOW = 128
ROPE_THETA = 10000.0
ROPE_AXIS_DIM = ATTN_HEAD_DIM // 2

IN_COLS = SSD_IN + LRU_IN + 2 * ATTN_IN
IN_SPLITS = (SSD_IN, SSD_IN + LRU_IN, SSD_IN + LRU_IN + ATTN_IN)
FFN_HIDDEN = 4 * D_MODEL

kernel_name = 'hybrid_ssd_rglru_gqa_swa_dit_block'

F32 = jnp.float32


def rms_norm(x, g):
    xf = x.astype(F32)
    y = xf * lax.rsqrt(jnp.mean(xf * xf, axis=-1, keepdims=True) + EPS)
    return (y * g.astype(F32)).astype(x.dtype)


def modulate(h, shift, scale):
    return h * (1.0 + scale) + shift


def conv_centred(x, w, b):
    k = w.shape[0]
    y = lax.conv_general_dilated(x, w[:, None, :].astype(x.dtype), window_strides=(1,),
                                 padding=[(k // 2, k - 1 - k // 2)],
                                 dimension_numbers=('NWC', 'WIO', 'NWC'),
                                 feature_group_count=x.shape[-1])
    return y + b.astype(x.dtype)


def rope_2d_tables(length):
    rows = length // GRID_W
    row = jnp.repeat(jnp.arange(rows, dtype=F32), GRID_W)
    col = jnp.tile(jnp.arange(GRID_W, dtype=F32), rows)
    inv = ROPE_THETA ** (-jnp.arange(0, ROPE_AXIS_DIM, 2, dtype=F32) / ROPE_AXIS_DIM)
    ang = jnp.stack([row, col], axis=-1)[:, :, None] * inv
    return jnp.cos(ang), jnp.sin(ang)


def apply_rope_2d(x, cos, sin):
    bsz, length, h, d = x.shape
    xr = x.astype(F32).reshape(bsz, length, h, 2, 2, ROPE_AXIS_DIM // 2)
    x1, x2 = xr[..., 0, :], xr[..., 1, :]
    cs, sn = cos[None, :, None], sin[None, :, None]
    out = jnp.stack([x1 * cs - x2 * sn, x2 * cs + x1 * sn], axis=-2)
    return out.reshape(bsz, length, h, d).astype(x.dtype)


def ssd_chunked(x, dt, a, bm, cm, h0):
    bsz, length, h, p = x.shape
    nc = length // SSD_CHUNK
    shp = (bsz, nc, SSD_CHUNK)
    xc = x.astype(F32).reshape(shp + (h, p))
    bc = bm.astype(F32).reshape(shp + (h, SSD_STATE))
    cc = cm.astype(F32).reshape(shp + (h, SSD_STATE))
    dtc = dt.reshape(shp + (h,))
    a_cum = jnp.cumsum(dtc * a, axis=2)
    seg = a_cum[:, :, :, None, :] - a_cum[:, :, None, :, :]
    tri = jnp.tril(jnp.ones((SSD_CHUNK, SSD_CHUNK), dtype=bool))
    decay = jnp.exp(jnp.where(tri[None, None, :, :, None], seg, -jnp.inf))
    scores = jnp.einsum('bcihn,bcjhn->bcijh', cc, bc) * decay
    y_diag = jnp.einsum('bcijh,bcjhp->bcihp', scores * dtc[:, :, None, :, :], xc)
    w_end = jnp.exp(a_cum[:, :, -1:, :] - a_cum) * dtc
    states = jnp.einsum('bcjh,bcjhn,bcjhp->bchpn', w_end, bc, xc)
    chunk_decay = jnp.exp(a_cum[:, :, -1, :])

    def step(hs, inp):
        s, dcy = inp
        return hs * dcy[:, :, None, None] + s, hs

    h_last, h_start = lax.scan(step, h0, (jnp.moveaxis(states, 1, 0), jnp.moveaxis(chunk_decay, 1, 0)))
    h_start = jnp.moveaxis(h_start, 0, 1)
    y_off = jnp.einsum('bcihn,bchpn->bcihp', cc, h_start) * jnp.exp(a_cum)[..., None]
    return (y_diag + y_off).reshape(bsz, length, h, p), h_last


def _ssd_inputs(p, conv_w, conv_b, dt_bias):
    bsz, length, _ = p.shape
    z, xbc, dt = jnp.split(p, [GROUP_WIDTH, GROUP_WIDTH + SSD_CONV_CH], axis=-1)
    xbc = jax.nn.silu(conv_centred(xbc, conv_w, conv_b))
    xs, bm, cm = jnp.split(xbc, [GROUP_WIDTH, GROUP_WIDTH + SSD_GROUPS * SSD_STATE], axis=-1)
    rep = SSD_HEADS // SSD_GROUPS
    xs = xs.reshape(bsz, length, SSD_HEADS, SSD_HEAD_DIM)
    bm = jnp.repeat(bm.reshape(bsz, length, SSD_GROUPS, SSD_STATE), rep, axis=2)
    cm = jnp.repeat(cm.reshape(bsz, length, SSD_GROUPS, SSD_STATE), rep, axis=2)
    dt = jax.nn.softplus(dt.astype(F32).reshape(bsz, length, 2, SSD_HEADS) + dt_bias.astype(F32))
    return z, xs, bm, cm, dt


def _ssd_bidir(xs, bm, cm, dt, a, h0_f, h0_b):
    y_f, s_f = ssd_chunked(xs, dt[:, :, 0], a[0], bm, cm, h0_f)
    y_b, s_b = ssd_chunked(xs[:, ::-1], dt[:, ::-1, 1], a[1], bm[:, ::-1], cm[:, ::-1], h0_b)
    return y_f + y_b[:, ::-1], s_f, s_b


def _ssd_out(y, xs, z, d_skip, norm_g):
    y = y + d_skip.astype(F32)[:, None] * xs.astype(F32)
    y = y.reshape(z.shape).astype(z.dtype)
    return rms_norm(y * jax.nn.silu(z), norm_g)


def ssd_mixer(pl, pc, conv_w, conv_b, a_log, dt_bias, d_skip, norm_g, need_ctx):
    a = -jnp.exp(a_log.astype(F32))
    zl, xl, bl, cl, dtl = _ssd_inputs(pl, conv_w, conv_b, dt_bias)
    zc, xc, bc, cc, dtc = _ssd_inputs(pc, conv_w, conv_b, dt_bias)
    h0 = jnp.zeros((pc.shape[0], SSD_HEADS, SSD_HEAD_DIM, SSD_STATE), F32)
    yc, s_f, s_b = _ssd_bidir(xc, bc, cc, dtc, a, h0, h0)
    yl, _, _ = _ssd_bidir(xl, bl, cl, dtl, a, s_f, s_b)
    out_l = _ssd_out(yl, xl, zl, d_skip, norm_g)
    out_c = _ssd_out(yc, xc, zc, d_skip, norm_g) if need_ctx else None
    return out_l, out_c


def _linear_combine(left, right):
    a_l, b_l = left
    a_r, b_r = right
    return a_l * a_r, a_r * b_l + b_r


def _rglru_dir(x, lam, w_a, b_a, w_i, b_i, h0):
    bsz, length, _ = x.shape
    xf = x.astype(F32)
    xb = xf.reshape(bsz, length, LRU_BLOCKS, LRU_BLOCK_W)
    r = jax.nn.sigmoid(jnp.einsum('blki,kij->blkj', xb, w_a.astype(F32)).reshape(bsz, length, LRU_WIDTH) + b_a.astype(F32))
    ig = jax.nn.sigmoid(jnp.einsum('blki,kij->blkj', xb, w_i.astype(F32)).reshape(bsz, length, LRU_WIDTH) + b_i.astype(F32))
    log_a = -LRU_C * r * jax.nn.softplus(-lam.astype(F32))
    a = jnp.exp(log_a)
    b = jnp.sqrt(-jnp.expm1(2.0 * log_a)) * (ig * xf)
    b = b.at[:, 0].add(a[:, 0] * h0)
    _, h = lax.associative_scan(_linear_combine, (a, b), axis=1)
    return h, h[:, -1]


def rglru_mixer(pl, pc, conv_w, conv_b, lam, w_a, b_a, w_i, b_i, need_ctx):
    gl, xl = jnp.split(pl, 2, axis=-1)
    gc, xc = jnp.split(pc, 2, axis=-1)
    xl = conv_centred(xl, conv_w, conv_b)
    xc = conv_centred(xc, conv_w, conv_b)
    h0 = jnp.zeros((pc.shape[0], LRU_WIDTH), F32)
    hc_f, s_f = _rglru_dir(xc, lam[0], w_a[0], b_a[0], w_i[0], b_i[0], h0)
    hc_b, s_b = _rglru_dir(xc[:, ::-1], lam[1], w_a[1], b_a[1], w_i[1], b_i[1], h0)
    hl_f, _ = _rglru_dir(xl, lam[0], w_a[0], b_a[0], w_i[0], b_i[0], s_f)
    hl_b, _ = _rglru_dir(xl[:, ::-1], lam[1], w_a[1], b_a[1], w_i[1], b_i[1], s_b)
    out_l = (hl_f + hl_b[:, ::-1]).astype(pl.dtype) * jax.nn.gelu(gl)
    out_c = (hc_f + hc_b[:, ::-1]).astype(pc.dtype) * jax.nn.gelu(gc) if need_ctx else None
    return out_l, out_c


def attn_heads(p, q_norm, k_norm, rope):
    bsz, length, _ = p.shape
    q, k, v = jnp.split(p, [ATTN_Q_HEADS * ATTN_HEAD_DIM, (ATTN_Q_HEADS + ATTN_KV_HEADS) * ATTN_HEAD_DIM], axis=-1)
    q = rms_norm(q.reshape(bsz, length, ATTN_Q_HEADS, ATTN_HEAD_DIM), q_norm)
    k = rms_norm(k.reshape(bsz, length, ATTN_KV_HEADS, ATTN_HEAD_DIM), k_norm)
    v = v.reshape(bsz, length, ATTN_KV_HEADS, ATTN_HEAD_DIM)
    if rope is not None:
        q = apply_rope_2d(q, rope[0], rope[1])
        k = apply_rope_2d(k, rope[0], rope[1])
    return q.reshape(bsz, length, ATTN_KV_HEADS, ATTN_GQA, ATTN_HEAD_DIM), k, v


def global_attention(q, k, v, kc, vc):
    bsz, length = q.shape[:2]
    nb = length // Q_BLOCK
    scale = ATTN_HEAD_DIM ** -0.5
    k_all = jnp.concatenate([k, kc], axis=1)
    v_all = jnp.concatenate([v, vc], axis=1)
    qb = jnp.moveaxis(q.reshape(bsz, nb, Q_BLOCK, ATTN_KV_HEADS, ATTN_GQA, ATTN_HEAD_DIM), 1, 0)

    def block(qblk):
        s = jnp.einsum('bqhgd,bkhd->bhgqk', qblk, k_all).astype(F32) * scale
        pr = jax.nn.softmax(s, axis=-1).astype(v_all.dtype)
        return jnp.einsum('bhgqk,bkhd->bqhgd', pr, v_all)

    o = lax.map(block, qb)
    return jnp.moveaxis(o, 0, 1).reshape(bsz, length, ATTN_Q_HEADS * ATTN_HEAD_DIM)


def context_attention(qc, kc, vc, sink):
    bsz, lc = qc.shape[:2]
    s = jnp.einsum('bqhgd,bkhd->bhgqk', qc, kc).astype(F32) * ATTN_HEAD_DIM ** -0.5
    if sink is not None:
        s_sink = jnp.broadcast_to(sink.astype(F32).reshape(1, ATTN_KV_HEADS, ATTN_GQA, 1, 1), s.shape[:-1] + (1,))
        s = jnp.concatenate([s, s_sink], axis=-1)
    pr = jax.nn.softmax(s, axis=-1)[..., :lc].astype(vc.dtype)
    o = jnp.einsum('bhgqk,bkhd->bqhgd', pr, vc)
    return o.reshape(bsz, lc, ATTN_Q_HEADS * ATTN_HEAD_DIM)


def window_attention(q, k, v, kc, vc, sink):
    bsz, length = q.shape[:2]
    nb = length // WINDOW
    scale = ATTN_HEAD_DIM ** -0.5
    pad = ((0, 0), (WINDOW, WINDOW), (0, 0), (0, 0))
    kp = jnp.pad(k, pad).reshape(bsz, nb + 2, WINDOW, ATTN_KV_HEADS, ATTN_HEAD_DIM)
    vp = jnp.pad(v, pad).reshape(bsz, nb + 2, WINDOW, ATTN_KV_HEADS, ATTN_HEAD_DIM)
    kw = jnp.concatenate([kp[:, :-2], kp[:, 1:-1], kp[:, 2:]], axis=2)
    vw = jnp.concatenate([vp[:, :-2], vp[:, 1:-1], vp[:, 2:]], axis=2)
    qb = q.reshape(bsz, nb, WINDOW, ATTN_KV_HEADS, ATTN_GQA, ATTN_HEAD_DIM)
    s_band = jnp.einsum('bnqhgd,bnkhd->bnhgqk', qb, kw).astype(F32) * scale
    qi = jnp.arange(WINDOW)[:, None]
    kj = jnp.arange(3 * WINDOW)[None, :]
    kpos = jnp.arange(nb)[:, None, None] * WINDOW + kj[None] - WINDOW
    mask = (jnp.abs(kj - WINDOW - qi) <= WINDOW)[None] & (kpos >= 0) & (kpos < length)
    s_band = jnp.where(mask[None, :, None, None], s_band, -jnp.inf)
    s_ctx = jnp.einsum('bnqhgd,bkhd->bnhgqk', qb, kc).astype(F32) * scale
    s_sink = jnp.broadcast_to(sink.astype(F32).reshape(1, 1, ATTN_KV_HEADS, ATTN_GQA, 1, 1), s_band.shape[:-1] + (1,))
    pr = jax.nn.softmax(jnp.concatenate([s_band, s_ctx, s_sink], axis=-1), axis=-1).astype(v.dtype)
    nk = 3 * WINDOW
    o = (jnp.einsum('bnhgqk,bnkhd->bnqhgd', pr[..., :nk], vw)
         + jnp.einsum('bnhgqk,bkhd->bnqhgd', pr[..., nk:nk + kc.shape[1]], vc))
    return o.reshape(bsz, length, ATTN_Q_HEADS * ATTN_HEAD_DIM)


def squared_relu_mlp(h, w1, w2):
    return jnp.square(jax.nn.relu(h @ w1)) @ w2


def setup_inputs(seed: int = 0) -> dict:
    key = jax.random.key(seed)
    ks = iter(jax.random.split(key, 40))

    def nrm(shape, scale):
        return jax.random.normal(next(ks), shape, F32) * scale

    L = DEPTH
    x = nrm((BATCH, SEQ, D_MODEL), 1.0)
    c = nrm((BATCH, D_MODEL), 1.0)
    ctx = nrm((BATCH, CTX_LEN, D_MODEL), 1.0)
    c_ctx = nrm((D_MODEL,), 1.0)
    w_mod = nrm((L, D_MODEL, N_MOD * D_MODEL), 0.02)
    b_mod = nrm((L, N_MOD * D_MODEL), 0.02)
    g_mix = 1.0 + nrm((L, D_MODEL), 0.02)
    w_in = nrm((L, D_MODEL, IN_COLS), D_MODEL ** -0.5)
    ssd_conv_w = nrm((L, SSD_CONV, SSD_CONV_CH), SSD_CONV ** -0.5)
    ssd_conv_b = nrm((L, SSD_CONV_CH), 0.02)
    ssd_a_log = jnp.log(jax.random.uniform(next(ks), (L, 2, SSD_HEADS), F32, 1.0, 16.0))
    dt0 = jnp.exp(jax.random.uniform(next(ks), (L, 2, SSD_HEADS), F32, math.log(1e-3), math.log(1e-1)))
    ssd_dt_bias = dt0 + jnp.log(-jnp.expm1(-dt0))
    ssd_d = 1.0 + nrm((L, SSD_HEADS), 0.02)
    ssd_norm_g = 1.0 + nrm((L, GROUP_WIDTH), 0.02)
    lru_conv_w = nrm((L, LRU_CONV, LRU_WIDTH), LRU_CONV ** -0.5)
    lru_conv_b = nrm((L, LRU_WIDTH), 0.02)
    a0 = jax.random.uniform(next(ks), (L, 2, LRU_WIDTH), F32, 0.9, 0.999)
    s0 = a0 ** (1.0 / LRU_C)
    lru_lambda = jnp.log(s0) - jnp.log1p(-s0)
    lru_w_a = nrm((L, 2, LRU_BLOCKS, LRU_BLOCK_W, LRU_BLOCK_W), LRU_BLOCK_W ** -0.5)
    lru_b_a = nrm((L, 2, LRU_WIDTH), 0.02)
    lru_w_i = nrm((L, 2, LRU_BLOCKS, LRU_BLOCK_W, LRU_BLOCK_W), LRU_BLOCK_W ** -0.5)
    lru_b_i = nrm((L, 2, LRU_WIDTH), 0.02)
    gqa_q_norm = 1.0 + nrm((L, ATTN_HEAD_DIM), 0.02)
    gqa_k_norm = 1.0 + nrm((L, ATTN_HEAD_DIM), 0.02)
    swa_q_norm = 1.0 + nrm((L, ATTN_HEAD_DIM), 0.02)
    swa_k_norm = 1.0 + nrm((L, ATTN_HEAD_DIM), 0.02)
    swa_sink = nrm((L, ATTN_Q_HEADS), 0.5)
    w_out = nrm((L, MIX_WIDTH, D_MODEL), MIX_WIDTH ** -0.5)
    g_ffn = 1.0 + nrm((L, D_MODEL), 0.02)
    w_ffn1 = nrm((L, D_MODEL, FFN_HIDDEN), D_MODEL ** -0.5)
    w_ffn2 = nrm((L, FFN_HIDDEN, D_MODEL), FFN_HIDDEN ** -0.5)
    return {'x': x, 'c': c, 'ctx': ctx, 'c_ctx': c_ctx, 'w_mod': w_mod, 'b_mod': b_mod,
            'g_mix': g_mix, 'w_in': w_in, 'ssd_conv_w': ssd_conv_w, 'ssd_conv_b': ssd_conv_b,
            'ssd_a_log': ssd_a_log, 'ssd_dt_bias': ssd_dt_bias, 'ssd_d': ssd_d, 'ssd_norm_g': ssd_norm_g,
            'lru_conv_w': lru_conv_w, 'lru_conv_b': lru_conv_b, 'lru_lambda': lru_lambda,
            'lru_w_a': lru_w_a, 'lru_b_a': lru_b_a, 'lru_w_i': lru_w_i, 'lru_b_i': lru_b_i,
            'gqa_q_norm': gqa_q_norm, 'gqa_k_norm': gqa_k_norm, 'swa_q_norm': swa_q_norm,
            'swa_k_norm': swa_k_norm, 'swa_sink': swa_sink, 'w_out': w_out, 'g_ffn': g_ffn,
            'w_ffn1': w_ffn1, 'w_ffn2': w_ffn2}


def reference(x, c, ctx, c_ctx, w_mod, b_mod, g_mix, w_in, ssd_conv_w, ssd_conv_b, ssd_a_log,
              ssd_dt_bias, ssd_d, ssd_norm_g, lru_conv_w, lru_conv_b, lru_lambda, lru_w_a, lru_b_a,
              lru_w_i, lru_b_i, gqa_q_norm, gqa_k_norm, swa_q_norm, swa_k_norm, swa_sink, w_out,
              g_ffn, w_ffn1, w_ffn2):
    rope = rope_2d_tables(x.shape[1])
    for l in range(DEPTH):
        need_ctx = l < DEPTH - 1
        m_lat = (jax.nn.silu(c) @ w_mod[l] + b_mod[l])[:, None, :]
        m_ctx = (jax.nn.silu(c_ctx) @ w_mod[l] + b_mod[l])[None, None, :]
        sh1, sc1, ga1, sh2, sc2, ga2 = jnp.split(m_lat, N_MOD, axis=-1)
        csh1, csc1, cga1, csh2, csc2, cga2 = jnp.split(m_ctx, N_MOD, axis=-1)

        pl = modulate(rms_norm(x, g_mix[l]), sh1, sc1) @ w_in[l]
        pc = modulate(rms_norm(ctx, g_mix[l]), csh1, csc1) @ w_in[l]
        pl_a, pl_b, pl_c, pl_d = jnp.split(pl, IN_SPLITS, axis=-1)
        pc_a, pc_b, pc_c, pc_d = jnp.split(pc, IN_SPLITS, axis=-1)

        ya_l, ya_c = ssd_mixer(pl_a, pc_a, ssd_conv_w[l], ssd_conv_b[l], ssd_a_log[l],
                               ssd_dt_bias[l], ssd_d[l], ssd_norm_g[l], need_ctx)
        yb_l, yb_c = rglru_mixer(pl_b, pc_b, lru_conv_w[l], lru_conv_b[l], lru_lambda[l],
                                 lru_w_a[l], lru_b_a[l], lru_w_i[l], lru_b_i[l], need_ctx)
        qcl, kcl, vcl = attn_heads(pl_c, gqa_q_norm[l], gqa_k_norm[l], rope)
        qcc, kcc, vcc = attn_heads(pc_c, gqa_q_norm[l], gqa_k_norm[l], None)
        yc_l = global_attention(qcl, kcl, vcl, kcc, vcc)
        qdl, kdl, vdl = attn_heads(pl_d, swa_q_norm[l], swa_k_norm[l], rope)
        qdc, kdc, vdc = attn_heads(pc_d, swa_q_norm[l], swa_k_norm[l], None)
        yd_l = window_attention(qdl, kdl, vdl, kdc, vdc, swa_sink[l])

        x = x + ga1 * (jnp.concatenate([ya_l, yb_l, yc_l, yd_l], axis=-1) @ w_out[l])
        x = x + ga2 * squared_relu_mlp(modulate(rms_norm(x, g_ffn[l]), sh2, sc2), w_ffn1[l], w_ffn2[l])

        if need_ctx:
            yc_c = context_attention(qcc, kcc, vcc, None)
            yd_c = context_attention(qdc, kdc, vdc, swa_sink[l])
            ctx = ctx + cga1 * (jnp.concatenate([ya_c, yb_c, yc_c, yd_c], axis=-1) @ w_out[l])
            ctx = ctx + cga2 * squared_relu_mlp(modulate(rms_norm(ctx, g_ffn[l]), csh2, csc2), w_ffn1[l], w_ffn2[l])
    return x
```

```python
import numpy as np
import concourse.bass as bass
import concourse.mybir as mybir
from concourse.bass_utils import run_bass_kernel_spmd
from contextlib import ExitStack

F32 = mybir.dt.float32
BF16 = mybir.dt.bfloat16
ALU = mybir.AluOpType
AF = mybir.ActivationFunctionType
AX = mybir.AxisListType

T = 4352
LC = 256
LL = 4096
D = 1024
NCH = 34
EPS = 1e-6
NEG = -30000.0
GROUPS = [(0, 256, 1)] + [(256 + i * 512, 512, 0) for i in range(8)]

FM = [(256, 128, 0), (384, 128, 0), (512, 128, 0), (640, 128, 0), (768, 128, 0), (896, 128, 0),
      (1032, 128, 0), (1160, 128, 0), (1288, 128, 0), (1416, 128, 0),
      (1544, 128, 0), (1672, 128, 0), (1800, 64, 1), (1864, 64, 1),
      (2056, 128, 0), (2184, 128, 0), (2312, 64, 1), (2376, 64, 1)]
I_X, I_B, I_C, I_G, I_R, I_CQ, I_CK, I_DQ, I_DK = 0, 2, 4, 6, 8, 10, 12, 14, 16
TMC = [(0, 256), (1024, 8), (1928, 128), (2440, 128)]
NTM = 520
NPP = 120
NRP = 280
(C_ID, C_LT, C_UT, C_LE, C_GE, C_ONE, C_BD, C_PERM, C_NEGF, C_NEGB) = range(10)
NCST = 10


class Buf:
    __slots__ = ("name", "w", "r")

    def __init__(self, name):
        self.name = name
        self.w = None
        self.r = {}


class Sched:
    COMPUTE = ("pe", "act", "dve", "pool")
    ALL = ("pe", "act", "dve", "pool", "sp")

    def __init__(self, nc, es, n_dma_sems=32, same_engine_sync=True):
        self.nc = nc
        self.streams = {e: [] for e in self.ALL}
        self.sems = {}
        for e in self.COMPUTE:
            self.sems[e] = es.enter_context(nc.semaphore("s_" + e))
        self.cnt = {e: 0 for e in self.COMPUTE}
        self.pending = {e: False for e in self.COMPUTE}
        self.dsem = [es.enter_context(nc.semaphore("d%d" % i)) for i in range(n_dma_sems)]
        self.dcnt = [0] * n_dma_sems
        self.dnext = 0
        self.waited = {e: {} for e in self.ALL}
        self.same_engine_sync = same_engine_sync
        self.nwaits = 0

    def _semof(self, k):
        if isinstance(k, tuple):
            return self.dsem[k[1]]
        return self.sems[k]

    def _deps(self, eng, reads, writes, extra=()):
        deps = {}

        def add(tok):
            if tok is None:
                return
            k, v = tok
            if deps.get(k, 0) < v:
                deps[k] = v

        for b in reads:
            add(b.w)
        for b in writes:
            add(b.w)
            for k, v in b.r.items():
                add((k, v))
        for t in extra:
            add(t)
        out = []
        for k, v in deps.items():
            if k == eng:
                if eng == "pe" or not self.same_engine_sync:
                    continue
                if v > self.cnt[eng]:
                    continue
            if self.waited[eng].get(k, 0) >= v:
                continue
            self.waited[eng][k] = v
            out.append((k, v))
        self.nwaits += len(out)
        return out

    def op(self, eng, fn, reads=(), writes=(), inc=True):
        waits = self._deps(eng, reads, writes)
        if inc:
            self.cnt[eng] += 1
            tok = (eng, self.cnt[eng])
            self.pending[eng] = False
        else:
            tok = (eng, self.cnt[eng] + 1)
            self.pending[eng] = True
        self.streams[eng].append((waits, fn, tok, 1 if inc else 0))
        for b in writes:
            b.w = tok
            b.r = {}
        for b in reads:
            if b.w is not tok:
                if b.r.get(eng, 0) < tok[1]:
                    b.r[eng] = tok[1]
        return tok

    def dma(self, fn, reads=(), writes=(), q="sp"):
        j = self.dnext
        self.dnext = (self.dnext + 1) % len(self.dsem)
        prev = (("d", j), 16 * self.dcnt[j])
        waits = self._deps(q, reads, writes, extra=(prev,) if self.dcnt[j] else ())
        self.dcnt[j] += 1
        tok = (("d", j), 16 * self.dcnt[j])
        self.streams[q].append((waits, fn, tok, 16))
        for b in writes:
            b.w = tok
            b.r = {}
        for b in reads:
            if b.w is not tok:
                b.r[tok[0]] = tok[1]
        return tok

    def wait_all(self, eng, toks):
        waits = self._deps(eng, (), (), extra=toks)
        if waits:
            self.streams[eng].append((waits, None, None, 0))

    def barrier(self):
        for e in self.COMPUTE:
            assert not self.pending[e], "pending un-incremented op on " + e
        toks = [(e, self.cnt[e]) for e in self.COMPUTE if self.cnt[e]]
        toks += [(("d", j), 16 * self.dcnt[j]) for j in range(len(self.dsem)) if self.dcnt[j]]
        for e in self.ALL:
            self.wait_all(e, toks)

    def emit(self, block):
        def mk(ename):
            def body(eng):
                for waits, fn, tok, inc in self.streams[ename]:
                    for k, v in waits:
                        eng.wait_ge(self._semof(k), v)
                    if fn is None:
                        continue
                    inst = fn(eng)
                    if inc:
                        inst.then_inc(self._semof(tok[0]), inc)
            return body

        block.tensor(mk("pe"))
        block.scalar(mk("act"))
        block.vector(mk("dve"))
        block.gpsimd(mk("pool"))
        block.sync(mk("sp"))


def rev_ap(ap):
    dims = [list(d) for d in ap.ap]
    fs, fc = dims[-1]
    dims[-1] = [-fs, fc]
    return bass.AP(ap.tensor, ap.offset + (fc - 1) * fs, dims)


def build(dbg=(), stop_after=None, n_layers=2):
    nc = bass.Bass("TRN2", target_bir_lowering=False)

    def din(name, shape, dt=F32):
        return nc.dram_tensor(name, list(shape), dt, kind="ExternalInput").ap()

    x_d = din("x", [LL, D])
    ctx_d = din("ctx", [LC, D])
    c2_d = din("c2", [128, 8, 2])
    wmod_d = din("w_mod", [2, D, 6144])
    win_d = din("w_in", [2, D, 2568])
    wout_d = din("w_out", [2, D, D])
    w1_d = din("w_ffn1", [2, D, 4096])
    w2_d = din("w_ffn2", [2, 4096, D])
    pp_d = din("pp", [2, 128, NPP])
    rp_d = din("rp", [2, NRP])
    lruw_d = din("lruw", [2, 2, 2, 2, 128, 128])
    cst_d = din("cst", [NCST, 128, 128])
    rope_d = din("rope", [2, 128, LL])
    out_d = nc.dram_tensor("out", [LL, D], F32, kind="ExternalOutput").ap()

    def scratch(name, shape, dt):
        kind = "ExternalOutput" if name in dbg else "Internal"
        return nc.dram_tensor(name, list(shape), dt, kind=kind).ap()

    xT_d = scratch("xT", [8, 128, T], F32)
    PTf_d = scratch("PTf", [18, 128, T], F32)
    PTt_d = scratch("PTt", [T, NTM], F32)
    YT_d = scratch("YT", [8, 128, T], BF16)
    MOD_d = scratch("MODd", [2, 128, 96], F32) if "MODd" in dbg else None

    es = ExitStack()
    with es:
        S = Sched(nc, es)

        _uid = [0]

        def sb(st, name, shape, dt=F32):
            _uid[0] += 1
            return st.enter_context(nc.sbuf_tensor("%s_%d" % (name, _uid[0]), list(shape), dt))

        def MM(out, lhsT, rhs, start, stop, r, w, inc=True):
            S.op("pe", lambda e: e.matmul(out, lhsT, rhs, start=start, stop=stop), r, w, inc=inc)

        def TR(out, in_, ident, r, w):
            S.op("pe", lambda e: e.transpose(out, in_, ident), r, w)

        def ACT(out, in_, func, r, w, bias=None, scale=None):
            kw = {}
            if bias is not None:
                kw["bias"] = bias
            if scale is not None:
                kw["scale"] = scale
            S.op("act", lambda e: e.activation(out=out, in_=in_, func=func, **kw), r, w)

        def TT(eng, out, in0, in1, op, r, w):
            S.op(eng, lambda e: e.tensor_tensor(out=out, in0=in0, in1=in1, op=op), r, w)

        def TS(eng, out, in0, s1, s2, op0, op1, r, w):
            if s2 is None:
                S.op(eng, lambda e: e.tensor_scalar(out=out, in0=in0, scalar1=s1, scalar2=None, op0=op0), r, w)
            else:
                S.op(eng, lambda e: e.tensor_scalar(out=out, in0=in0, scalar1=s1, scalar2=s2, op0=op0, op1=op1), r, w)

        def STT(out, in0, scalar, in1, op0, op1, r, w):
            S.op("dve", lambda e: e.scalar_tensor_tensor(out=out, in0=in0, scalar=scalar, in1=in1, op0=op0, op1=op1), r, w)

        def CP(eng, out, in_, r, w):
            if eng == "act":
                S.op("act", lambda e: e.activation(out=out, in_=in_, func=AF.Copy), r, w)
            else:
                S.op(eng, lambda e: e.tensor_copy(out=out, in_=in_), r, w)

        def RECIP(out, in_, r, w):
            S.op("dve", lambda e: e.reciprocal(out=out, in_=in_), r, w)

        def MEMSET(eng, ap, val, w):
            S.op(eng, lambda e: e.memset(ap, val), (), w)

        def DMA(out, in_, r, w, q="sp"):
            return S.dma(lambda e: e.dma_start(out=out, in_=in_), r, w, q=q)

        PS = [es.enter_context(nc.psum_tensor("ps%d" % i, [128, 512], F32)) for i in range(8)]
        bPS = [Buf("ps%d" % i) for i in range(8)]
        CST = sb(es, "CST", [128, NCST, 128])
        bCST = Buf("CST")
        DMA(CST[:], cst_d.rearrange("c p f -> p c f"), (), [bCST])
        CSTb = sb(es, "CSTb", [128, NCST, 128], BF16)
        bCSTb = Buf("CSTb")
        CP("dve", CSTb[:], CST[:], [bCST], [bCSTb])
        ident = CST[:, C_ID, :]
        identb = CSTb[:, C_ID, :]
        CS = sb(es, "CS", [128, 8, 2])
        bCS = Buf("CS")
        DMA(CS[:], c2_d, (), [bCS])
        ACT(CS[:], CS[:], AF.Silu, [bCS], [bCS])
        MODS = sb(es, "MODS", [128, 48, 2])
        bMODS = Buf("MODS")
        G1 = sb(es, "G1", [128, 8, 2])
        G2 = sb(es, "G2", [128, 8, 2])
        bG = Buf("G12")
        PP = sb(es, "PP", [128, NPP])
        bPP = Buf("PP")
        RP = sb(es, "RP", [128, NRP])
        bRP = Buf("RP")
        final_toks = []

        def phase_T():
            with ExitStack() as st:
                XK = [sb(st, "XK%d" % i, [128, D]) for i in range(2)]
                bXK = [Buf("XK%d" % i) for i in range(2)]
                XS = [sb(st, "XS%d" % i, [128, 8, 512]) for i in range(2)]
                bXS = [Buf("XSs%d" % i) for i in range(2)]
                it = 0
                for gi, (t0, n, which) in enumerate(GROUPS):
                    xs, bxs = XS[gi % 2], bXS[gi % 2]
                    for s in range(n // 128):
                        xk, bxk = XK[it % 2], bXK[it % 2]
                        tt = t0 + s * 128
                        src = ctx_d[tt:tt + 128, :] if which else x_d[tt - LC:tt - LC + 128, :]
                        DMA(xk[:], src, (), [bxk])
                        for half in range(2):
                            p, bp = PS[(it * 2 + half) % 8], bPS[(it * 2 + half) % 8]
                            for jj in range(4):
                                j = half * 4 + jj
                                TR(p[:, jj * 128:(jj + 1) * 128], xk[:, j * 128:(j + 1) * 128], ident, [bxk, bCST], [bp])
                            CP("act" if half else "dve", xs[:, half * 4:half * 4 + 4, s * 128:(s + 1) * 128],
                               p[:].rearrange("p (j t) -> p j t", j=4), [bp], [bxs])
                        it += 1
                    DMA(xT_d[:, :, t0:t0 + n].rearrange("j p t -> p j t"), xs[:, :, 0:n], [bxs], ())
            S.barrier()

        def phase_0(l):
            with ExitStack() as st:
                DMA(PP[:], pp_d[l], (), [bPP])
                DMA(RP[:], rp_d[l:l + 1, :].partition_broadcast(128) if False else rp_d[l].partition_broadcast(128), (), [bRP])
                WM = [sb(st, "WM%d" % i, [128, 8, 1024]) for i in range(2)]
                bWM = [Buf("WM%d" % i) for i in range(2)]
                pm, bpm = PS[0], bPS[0]
                for sl in range(6):
                    wm, bwm = WM[sl % 2], bWM[sl % 2]
                    DMA(wm[:], wmod_d[l, :, sl * 1024:(sl + 1) * 1024].rearrange("(j p) c -> p j c", p=128), (), [bwm])
                    for oc8 in range(8):
                        oc = sl * 8 + oc8
                        for j in range(8):
                            MM(pm[:, oc * 2:oc * 2 + 2], wm[:, j, oc8 * 128:(oc8 + 1) * 128], CS[:, j, :],
                               j == 0, j == 7, [bwm, bCS], [bpm], inc=(j == 7))
                TT("dve", MODS[:], pm[:, 0:96].rearrange("p (o w) -> p o w", w=2),
                   PP[:, 16:64].unsqueeze(2).to_broadcast([128, 48, 2]), ALU.add, [bpm, bPP], [bMODS])
                for (G, gcol, sccol) in ((G1, 0, 8), (G2, 8, 32)):
                    TS("dve", G[:], MODS[:, sccol:sccol + 8, :], 1.0, None, ALU.add, None, [bMODS], [bG])
                    TT("dve", G[:], G[:], PP[:, gcol:gcol + 8].unsqueeze(2).to_broadcast([128, 8, 2]), ALU.mult, [bG, bPP], [bG])
                if MOD_d is not None:
                    final_toks.append(DMA(MOD_d[l], MODS[:].rearrange("p o w -> p (o w)"), [bMODS], ()))
            S.barrier()

        def phase_A(l):
            with ExitStack() as st:
                WF = sb(st, "WF", [128, 8, 18 * 128], BF16)
                WK = sb(st, "WK", [128, 8, NTM], BF16)
                bW = Buf("Win")
                for ci, (c0, ncol, dup) in enumerate(FM):
                    src = win_d[l, :, c0:c0 + ncol].rearrange("(j p) c -> p j c", p=128)
                    DMA(WF[:, :, ci * 128:ci * 128 + ncol], src, (), [bW], q="pool")
                    if dup:
                        DMA(WF[:, :, ci * 128 + 64:ci * 128 + 128], src, (), [bW], q="pool")
                off = 0
                for (c0, ncol) in TMC:
                    DMA(WK[:, :, off:off + ncol], win_d[l, :, c0:c0 + ncol].rearrange("(j p) c -> p j c", p=128), (), [bW], q="pool")
                    off += ncol
                XT = [sb(st, "XT%d" % i, [128, 8, 512]) for i in range(2)]
                bXT = [Buf("XT%d" % i) for i in range(2)]
                SQ = sb(st, "SQ", [128, 8, 512], BF16)
                bSQ = Buf("SQ")
                RS = sb(st, "RS", [128, 512])
                bRS = Buf("RS")
                HT = [sb(st, "HT%d" % i, [128, 8, 512], BF16) for i in range(2)]
                bHT = [Buf("HT%d" % i) for i in range(2)]
                FS = [sb(st, "FS%d" % i, [128, 512]) for i in range(4)]
                bFS = [Buf("FS%d" % i) for i in range(4)]
                ZS = [sb(st, "ZS%d" % i, [128, NTM]) for i in range(2)]
                bZS = [Buf("ZS%d" % i) for i in range(2)]
                fsi = 0
                zsi = 0
                pi = 0
                for gi, (t0, n, which) in enumerate(GROUPS):
                    xt, bxt = XT[gi % 2], bXT[gi % 2]
                    ht, bht = HT[gi % 2], bHT[gi % 2]
                    DMA(xt[:, :, 0:n], xT_d[:, :, t0:t0 + n].rearrange("j p t -> p j t"), (), [bxt])
                    ACT(SQ[:, :, 0:n], xt[:, :, 0:n], AF.Square, [bxt], [bSQ])
                    pss, bpss = PS[pi % 8], bPS[pi % 8]
                    pi += 1
                    for j in range(8):
                        MM(pss[:, 0:n], CSTb[:, C_ONE, :], SQ[:, j, 0:n], j == 0, j == 7, [bSQ, bCSTb], [bpss], inc=(j == 7))
                    ACT(RS[:, 0:n], pss[:, 0:n], AF.Sqrt, [bpss], [bRS], bias=EPS, scale=1.0 / D)
                    RECIP(RS[:, 0:n], RS[:, 0:n], [bRS], [bRS])
                    TT("dve", xt[:, :, 0:n], xt[:, :, 0:n], RS[:, 0:n].unsqueeze(1).to_broadcast([128, 8, n]), ALU.mult, [bxt, bRS], [bxt])
                    for j in range(8):
                        ACT(ht[:, j, 0:n], xt[:, j, 0:n], AF.Identity, [bxt, bG, bMODS], [bht],
                            bias=MODS[:, j, which:which + 1], scale=G1[:, j, which:which + 1])
                    for ci in range(18):
                        p, bp = PS[pi % 8], bPS[pi % 8]
                        pi += 1
                        for j in range(8):
                            MM(p[:, 0:n], WF[:, j, ci * 128:(ci + 1) * 128], ht[:, j, 0:n], j == 0, j == 7, [bW, bht], [bp], inc=(j == 7))
                        fs, bfs = FS[fsi % 4], bFS[fsi % 4]
                        CP("act" if fsi % 2 else "dve", fs[:, 0:n], p[:, 0:n], [bp], [bfs])
                        fsi += 1
                        DMA(PTf_d[ci, :, t0:t0 + n], fs[:, 0:n], [bfs], ())
                    for s in range(n // 128):
                        p, bp = PS[pi % 8], bPS[pi % 8]
                        p2, bp2 = PS[(pi + 1) % 8], bPS[(pi + 1) % 8]
                        pi += 2
                        for j in range(8):
                            MM(p[:, 0:512], ht[:, j, s * 128:(s + 1) * 128], WK[:, j, 0:512], j == 0, j == 7, [bW, bht], [bp], inc=(j == 7))
                        for j in range(8):
                            MM(p2[:, 0:8], ht[:, j, s * 128:(s + 1) * 128], WK[:, j, 512:520], j == 0, j == 7, [bW, bht], [bp2], inc=(j == 7))
                        zs, bzs = ZS[zsi % 2], bZS[zsi % 2]
                        zsi += 1
                        CP("act", zs[:, 0:512], p[:, 0:512], [bp], [bzs])
                        CP("dve", zs[:, 512:520], p2[:, 0:8], [bp2], [bzs])
                        tt = t0 + s * 128
                        DMA(PTt_d[tt:tt + 128, :], zs[:], [bzs], ())
            S.barrier()


        def conv_seg(out, inp, wc, bc, s0, L, r, w):
            S.op("dve", lambda e: e.tensor_scalar(out=out[:, s0:s0 + L], in0=inp[:, s0:s0 + L], scalar1=PP[:, wc + 2:wc + 3],
                                                   scalar2=PP[:, bc:bc + 1], op0=ALU.mult, op1=ALU.add), r, w)
            STT(out[:, s0 + 2:s0 + L], inp[:, s0:s0 + L - 2], PP[:, wc:wc + 1], out[:, s0 + 2:s0 + L], ALU.mult, ALU.add, r + w, w)
            STT(out[:, s0 + 1:s0 + L], inp[:, s0:s0 + L - 1], PP[:, wc + 1:wc + 2], out[:, s0 + 1:s0 + L], ALU.mult, ALU.add, r + w, w)
            STT(out[:, s0:s0 + L - 1], inp[:, s0 + 1:s0 + L], PP[:, wc + 3:wc + 4], out[:, s0:s0 + L - 1], ALU.mult, ALU.add, r + w, w)

        def conv_full(out, inp, wc, bc, r, w):
            conv_seg(out, inp, wc, bc, 0, LC, r, w)
            conv_seg(out, inp, wc, bc, LC, LL, r, w)

        def phase_B(l):
            with ExitStack() as st:
                LW = sb(st, "LW", [128, 8, 128])
                bLW = Buf("LW")
                DMA(LW[:], lruw_d[l].rearrange("d g c p f -> p (d g c) f"), (), [bLW])
                SC = sb(st, "SCl", [128, 8])
                bSC = Buf("SCl")
                ACT(SC[:, 0:4], PP[:, 104:108], AF.Exp, [bPP], [bSC], scale=-1.0)
                ACT(SC[:, 0:4], SC[:, 0:4], AF.Ln, [bSC], [bSC], bias=1.0)
                TS("dve", SC[:, 4:8], SC[:, 0:4], -16.0, None, ALU.mult, None, [bSC], [bSC])
                TS("dve", SC[:, 0:4], SC[:, 0:4], -8.0, None, ALU.mult, None, [bSC], [bSC])
                names = ["XR", "XC", "Rt", "IGt", "Mt", "HF", "HB"]
                tl = {n_: sb(st, "lru_" + n_, [128, T]) for n_ in names}
                bf = {n_: Buf("lru_" + n_) for n_ in names}
                YB = sb(st, "lru_YB", [128, T], BF16)
                bYB = Buf("lru_YB")
                pi = 0
                for c2 in range(2):
                    DMA(tl["XR"][:], PTf_d[I_R + c2], (), [bf["XR"]])
                    conv_full(tl["XC"], tl["XR"], 94 + c2 * 4, 102 + c2, [bf["XR"], bPP], [bf["XC"]])
                    DMA(tl["XR"][:], PTf_d[I_G + c2], (), [bf["XR"]])
                    for dr in range(2):
                        col = dr * 2 + c2
                        for (t0, n, which) in GROUPS:
                            for gt, dst, bcol in ((0, "Rt", 108), (1, "IGt", 112)):
                                p, bp = PS[pi % 8], bPS[pi % 8]
                                pi += 1
                                MM(p[:, 0:n], LW[:, dr * 4 + gt * 2 + c2, :], tl["XC"][:, t0:t0 + n], True, True, [bLW, bf["XC"]], [bp])
                                ACT(tl[dst][:, t0:t0 + n], p[:, 0:n], AF.Sigmoid, [bp, bPP], [bf[dst]], bias=PP[:, bcol + col:bcol + col + 1])
                        ACT(tl["Mt"][:], tl["Rt"][:], AF.Exp, [bf["Rt"], bSC], [bf["Mt"]], scale=SC[:, 4 + col:5 + col])
                        ACT(tl["Rt"][:], tl["Rt"][:], AF.Exp, [bf["Rt"], bSC], [bf["Rt"]], scale=SC[:, col:col + 1])
                        ACT(tl["Mt"][:], tl["Mt"][:], AF.Sqrt, [bf["Mt"]], [bf["Mt"]], bias=1.0, scale=-1.0)
                        TT("pool", tl["IGt"][:], tl["IGt"][:], tl["Mt"][:], ALU.mult, [bf["IGt"], bf["Mt"]], [bf["IGt"]])
                        TT("pool", tl["IGt"][:], tl["IGt"][:], tl["XC"][:], ALU.mult, [bf["IGt"], bf["XC"]], [bf["IGt"]])
                        H = tl["HF" if dr == 0 else "HB"]
                        bH = bf["HF" if dr == 0 else "HB"]
                        A_, B_ = tl["Rt"], tl["IGt"]
                        rw = ([bf["Rt"], bf["IGt"], bH], [bH])

                        def scan(s0, L, init, reverse):
                            o, a, b = H[:, s0:s0 + L], A_[:, s0:s0 + L], B_[:, s0:s0 + L]
                            if reverse:
                                o, a, b = rev_ap(o), rev_ap(a), rev_ap(b)
                            S.op("dve", lambda e: e.tensor_tensor_scan(out=o, data0=a, data1=b, initial=init, op0=ALU.mult, op1=ALU.add), rw[0], rw[1])

                        PIECE = 1024
                        if dr == 0:
                            scan(0, LC, 0.0, False)
                            for s0 in range(LC, T, PIECE):
                                scan(s0, PIECE, H[:, s0 - 1:s0], False)
                        else:
                            scan(0, LC, 0.0, True)
                            prev = H[:, 0:1]
                            for s0 in range(T - PIECE, LC - 1, -PIECE):
                                scan(s0, PIECE, prev, True)
                                prev = H[:, s0:s0 + 1]
                    TT("pool", tl["HF"][:], tl["HF"][:], tl["HB"][:], ALU.add, [bf["HF"], bf["HB"]], [bf["HF"]])
                    ACT(tl["XR"][:], tl["XR"][:], AF.Gelu, [bf["XR"]], [bf["XR"]])
                    TT("dve", YB[:], tl["HF"][:], tl["XR"][:], ALU.mult, [bf["HF"], bf["XR"]], [bYB])
                    DMA(YT_d[2 + c2], YB[:], [bYB], ())
            S.barrier()

        def phase_C(l):
            with ExitStack() as st:
                BTm = [sb(st, "BTm%d" % g, [128, T], BF16) for g in range(2)]
                CTm = [sb(st, "CTm%d" % g, [128, T], BF16) for g in range(2)]
                bBC = Buf("BCT")
                XTK = sb(st, "XTK", [128, NCH, 256])
                XTKb = sb(st, "XTKb", [128, NCH, 256], BF16)
                BTK = sb(st, "BTK", [128, NCH, 256], BF16)
                bTK = Buf("TK")
                EX = sb(st, "EX", [128, NCH, 40])
                DTt = sb(st, "DTt", [128, NCH, 8])
                LDT = sb(st, "LDT", [128, NCH, 8])
                DTA = sb(st, "DTA", [128, NCH, 8])
                WE = sb(st, "WE", [128, NCH, 8])
                A8 = sb(st, "A8", [128, 8])
                bSM = Buf("ssd_small")
                with ExitStack() as st2:
                    XR = [sb(st2, "cXR%d" % i, [128, T]) for i in range(2)]
                    bXR = [Buf("cXR%d" % i) for i in range(2)]
                    XC = sb(st2, "cXC", [128, T])
                    bXC = Buf("cXC")
                    XS = [sb(st2, "cXS%d" % i, [128, T]) for i in range(2)]
                    bXS = Buf("cXS")
                    for ci in range(6):
                        xr, bxr = XR[ci % 2], bXR[ci % 2]
                        DMA(xr[:], PTf_d[I_X + ci], (), [bxr])
                        conv_full(XC, xr, 64 + ci * 4, 88 + ci, [bxr, bPP], [bXC])
                        if ci < 2:
                            ACT(XS[ci][:], XC[:], AF.Silu, [bXC], [bXS])
                        elif ci < 4:
                            ACT(BTm[ci - 2][:], XC[:], AF.Silu, [bXC], [bBC])
                        else:
                            ACT(CTm[ci - 4][:], XC[:], AF.Silu, [bXC], [bBC])
                    for c in range(NCH):
                        tt = c * 128
                        p, bp = PS[(2 * c) % 8], bPS[(2 * c) % 8]
                        pb, bpb = PS[(2 * c + 1) % 8], bPS[(2 * c + 1) % 8]
                        for ci in range(2):
                            TR(p[:, ci * 128:(ci + 1) * 128], XS[ci][:, tt:tt + 128], ident, [bXS, bCST], [bp])
                        CP("act", XTK[:, c, :], p[:, 0:256], [bp], [bTK])
                        CP("dve", XTKb[:, c, :], p[:, 0:256], [bp], [bTK])
                        pbv = pb[:].bitcast(BF16)
                        for g in range(2):
                            TR(pbv[:, g * 128:(g + 1) * 128], BTm[g][:, tt:tt + 128], identb, [bBC, bCSTb], [bpb])
                        CP("dve", BTK[:, c, :], pbv[:, 0:256], [bpb], [bTK])
                        DMA(DTt[:, c, :], PTt_d[tt:tt + 128, 256:264], (), [bSM])
                S.barrier()
                TT("dve", DTt[:], DTt[:], RP[:, 260:268].unsqueeze(1).to_broadcast([128, NCH, 8]), ALU.add, [bSM, bRP], [bSM])
                ACT(DTt[:], DTt[:], AF.Exp, [bSM], [bSM])
                ACT(DTt[:], DTt[:], AF.Ln, [bSM], [bSM], bias=1.0)
                ACT(LDT[:], DTt[:], AF.Ln, [bSM], [bSM])
                ACT(A8[:], RP[:, 268:276], AF.Exp, [bRP], [bSM])
                STT(DTA[:], DTt[:], -1.0, A8[:].unsqueeze(1).to_broadcast([128, NCH, 8]), ALU.mult, ALU.mult, [bSM], [bSM])
                for c0 in range(0, NCH, 12):
                    nb = min(12, NCH - c0)
                    p, bp = PS[(c0 // 12) % 8], bPS[(c0 // 12) % 8]
                    for cc in range(nb):
                        for k, cm in enumerate((C_LT, C_UT, C_LE, C_GE, C_ONE)):
                            MM(p[:, cc * 40 + k * 8:cc * 40 + k * 8 + 8], CST[:, cm, :], DTA[:, c0 + cc, :], True, True,
                               [bCST, bSM], [bp], inc=(cc == nb - 1 and k == 4))
                    ACT(EX[:, c0:c0 + nb, :], p[:, 0:nb * 40].rearrange("p (c k) -> p c k", k=40), AF.Exp, [bp], [bSM])
                TT("dve", WE[:, :, 0:4], EX[:, :, 0:4], DTt[:, :, 0:4], ALU.mult, [bSM], [bSM])
                TT("dve", WE[:, :, 4:8], EX[:, :, 12:16], DTt[:, :, 4:8], ALU.mult, [bSM], [bSM])

                def bc4(ap):
                    return ap.unsqueeze(2).to_broadcast([128, 4, 64])

                def v4(ap):
                    return ap.rearrange("p (h d) -> p h d", h=4)

                HBall = sb(st, "HBall", [128, NCH, 256], BF16)
                bHBall = Buf("HBall")
                Hs = sb(st, "Hs", [128, 256])
                bHs = Buf("Hs")
                XSB = [sb(st, "XSB%d" % i, [128, 256], BF16) for i in range(2)]
                bXSB = [Buf("XSB%d" % i) for i in range(2)]
                MEMSET("pool", Hs[:], 0.0, [bHs])
                order = [1, 0] + list(range(NCH - 1, 1, -1))
                for it, c in enumerate(order):
                    xsb, bxsb = XSB[it % 2], bXSB[it % 2]
                    CP("pool", HBall[:, c, :], Hs[:], [bHs], [bHBall])
                    TT("dve", v4(xsb[:]), v4(XTK[:, c, :]), bc4(WE[:, c, 4:8]), ALU.mult, [bTK, bSM], [bxsb])
                    p, bp = PS[it % 4], bPS[it % 4]
                    for g in range(2):
                        MM(p[:, g * 128:(g + 1) * 128], BTK[:, c, g * 128:(g + 1) * 128], xsb[:, g * 128:(g + 1) * 128], True, True,
                           [bTK, bxsb], [bp], inc=(g == 1))
                    TT("pool", v4(Hs[:]), v4(Hs[:]), bc4(EX[:, c, 36:40]), ALU.mult, [bHs, bSM], [bHs])
                    TT("dve", Hs[:], Hs[:], p[:, 0:256], ALU.add, [bHs, bp], [bHs])
                RFB = [sb(st, "RFB%d" % i, [128, 8, 128]) for i in range(2)]
                bRFB = [Buf("RFB%d" % i) for i in range(2)]
                Dx = sb(st, "Dx", [128, 8, 128])
                bDx = Buf("Dx")
                GTs = sb(st, "GTs", [128, 2, 128])
                bGTs = Buf("GTs")
                DS = sb(st, "DS", [128, 4, 128])
                bDS = Buf("DS")
                MTb = sb(st, "MTb", [128, 4, 128], BF16)
                bMTb = Buf("MTb")
                Hbf = sb(st, "Hbf", [128, 256], BF16)
                bHbf = Buf("Hbf")
                t1 = sb(st, "ct1", [128, 256])
                t2 = sb(st, "ct2", [128, 256])
                t3 = sb(st, "ct3", [128, 256])
                bt1, bt2, bt3 = Buf("ct1"), Buf("ct2"), Buf("ct3")
                ZD = [sb(st, "ZD%d" % i, [128, 256]) for i in range(2)]
                bZD = [Buf("ZD%d" % i) for i in range(2)]
                ssum = sb(st, "ssum", [128, 2])
                bss = Buf("ssum")
                YN = sb(st, "YN", [128, 256])
                bYN = Buf("YN")
                YAT = sb(st, "YAT", [128, 2, T], BF16)
                bYAT = Buf("YAT")
                MEMSET("pool", Hs[:], 0.0, [bHs])
                for c in range(NCH):
                    tt = c * 128
                    rfb, brfb = RFB[c % 2], bRFB[c % 2]
                    zd, bzd = ZD[c % 2], bZD[c % 2]
                    DMA(zd[:], PTt_d[tt:tt + 128, 0:256], (), [bzd])
                    CP("pool", Hbf[:], Hs[:], [bHs], [bHbf])
                    for g in range(2):
                        MM(PS[0][:, g * 128:(g + 1) * 128], BTm[g][:, tt:tt + 128], CTm[g][:, tt:tt + 128], True, True, [bBC], [bPS[0]], inc=(g == 1))
                    CP("act", GTs[:], PS[0][:, 0:256].rearrange("p (g i) -> p g i", g=2), [bPS[0]], [bGTs])
                    for h in range(4):
                        TS("pool", rfb[:, h, :], CST[:, C_LE, :], DTA[:, c, h:h + 1], 1.0, ALU.mult, ALU.mult, [bCST, bSM], [brfb])
                        TS("pool", rfb[:, 4 + h, :], CST[:, C_GE, :], DTA[:, c, 4 + h:5 + h], 1.0, ALU.mult, ALU.mult, [bCST, bSM], [brfb])
                    for h in range(4):
                        MM(PS[1][:, h * 128:(h + 1) * 128], CST[:, C_LT, :], rfb[:, h, :], True, False, [bCST, brfb], [bPS[1]], inc=False)
                        MM(PS[1][:, h * 128:(h + 1) * 128], ident, CST[:, C_NEGF, :], False, True, [bCST], [bPS[1]], inc=(h == 3))
                    for h in range(4):
                        MM(PS[2][:, h * 128:(h + 1) * 128], CST[:, C_UT, :], rfb[:, 4 + h, :], True, False, [bCST, brfb], [bPS[2]], inc=False)
                        MM(PS[2][:, h * 128:(h + 1) * 128], ident, CST[:, C_NEGB, :], False, True, [bCST], [bPS[2]], inc=(h == 3))
                    for dh in range(8):
                        src = PS[1 + dh // 4]
                        h = dh % 4
                        ACT(Dx[:, dh, :], src[:, h * 128:(h + 1) * 128], AF.Exp, [bPS[1 + dh // 4], bSM], [bDx], bias=LDT[:, c, dh:dh + 1])
                    TT("pool", DS[:], Dx[:, 0:4, :], Dx[:, 4:8, :], ALU.add, [bDx], [bDS])
                    for g in range(2):
                        TT("dve", MTb[:, 2 * g:2 * g + 2, :], DS[:, 2 * g:2 * g + 2, :], GTs[:, g, :].unsqueeze(1).to_broadcast([128, 2, 128]),
                           ALU.mult, [bDS, bGTs], [bMTb])
                    for h in range(4):
                        MM(PS[3][:, h * 64:(h + 1) * 64], MTb[:, h, :], XTKb[:, c, h * 64:(h + 1) * 64], True, True, [bMTb, bTK], [bPS[3]], inc=(h == 3))
                    for g in range(2):
                        MM(PS[4][:, g * 128:(g + 1) * 128], CTm[g][:, tt:tt + 128], Hbf[:, g * 128:(g + 1) * 128], True, True, [bBC, bHbf], [bPS[4]], inc=False)
                        MM(PS[4][:, 256 + g * 128:256 + (g + 1) * 128], CTm[g][:, tt:tt + 128], HBall[:, c, g * 128:(g + 1) * 128], True, True,
                           [bBC, bHBall], [bPS[4]], inc=(g == 1))
                    TT("dve", v4(t1[:]), v4(PS[4][:, 0:256]), bc4(EX[:, c, 16:20]), ALU.mult, [bPS[4], bSM], [bt1])
                    TT("dve", v4(t2[:]), v4(PS[4][:, 256:512]), bc4(EX[:, c, 28:32]), ALU.mult, [bPS[4], bSM], [bt2])
                    TT("pool", t1[:], t1[:], t2[:], ALU.add, [bt1, bt2], [bt1])
                    TT("dve", t1[:], t1[:], PS[3][:, 0:256], ALU.add, [bt1, bPS[3]], [bt1])
                    TT("pool", v4(t3[:]), v4(XTK[:, c, :]), bc4(RP[:, 256:260]), ALU.mult, [bTK, bRP], [bt3])
                    TT("pool", t1[:], t1[:], t3[:], ALU.add, [bt1, bt3], [bt1])
                    ACT(t2[:], zd[:], AF.Silu, [bzd], [bt2])
                    TT("dve", t1[:], t1[:], t2[:], ALU.mult, [bt1, bt2], [bt1])
                    TT("pool", t3[:], t1[:], t1[:], ALU.mult, [bt1], [bt3])
                    S.op("dve", lambda e: e.tensor_reduce(out=ssum[:, 0:1], in_=t3[:], axis=AX.X, op=ALU.add), [bt3], [bss])
                    ACT(ssum[:, 1:2], ssum[:, 0:1], AF.Sqrt, [bss], [bss], bias=EPS, scale=1.0 / 256)
                    RECIP(ssum[:, 1:2], ssum[:, 1:2], [bss], [bss])
                    STT(YN[:], t1[:], ssum[:, 1:2], RP[:, 0:256], ALU.mult, ALU.mult, [bt1, bss, bRP], [bYN])
                    for ci in range(2):
                        TR(PS[6][:, ci * 128:(ci + 1) * 128], YN[:, ci * 128:(ci + 1) * 128], ident, [bYN, bCST], [bPS[6]])
                    CP("act", YAT[:, :, tt:tt + 128], PS[6][:, 0:256].rearrange("p (j t) -> p j t", j=2), [bPS[6]], [bYAT])
                    xsb, bxsb = XSB[c % 2], bXSB[c % 2]
                    TT("dve", v4(xsb[:]), v4(XTK[:, c, :]), bc4(WE[:, c, 0:4]), ALU.mult, [bTK, bSM], [bxsb])
                    for g in range(2):
                        MM(PS[5][:, g * 128:(g + 1) * 128], BTK[:, c, g * 128:(g + 1) * 128], xsb[:, g * 128:(g + 1) * 128], True, True,
                           [bTK, bxsb], [bPS[5]], inc=(g == 1))
                    TT("pool", v4(Hs[:]), v4(Hs[:]), bc4(EX[:, c, 32:36]), ALU.mult, [bHs, bSM], [bHs])
                    TT("dve", Hs[:], Hs[:], PS[5][:, 0:256], ALU.add, [bHs, bPS[5]], [bHs])
                DMA(YT_d[0:2].rearrange("j p t -> p j t"), YAT[:], [bYAT], ())
            S.barrier()


        def phase_D(l, need_ctx):
            with ExitStack() as st:
                QT = [[sb(st, "QT%d%d" % (ty, k), [128, T], BF16) for k in range(2)] for ty in range(2)]
                KT = [[sb(st, "KT%d%d" % (ty, k), [128, T], BF16) for k in range(2)] for ty in range(2)]
                bQK = Buf("QK")
                VA = [[sb(st, "VA%d%d" % (ty, k), [128, NCH, 128], BF16) for k in range(2)] for ty in range(2)]
                bVA = Buf("VA")
                QG = sb(st, "QG", [128, 4])
                ESK = sb(st, "ESK", [128, 4])
                bQG = Buf("QG")
                TS("dve", QG[:, 0:1], PP[:, 116:117], 0.125, None, ALU.mult, None, [bPP], [bQG])
                TS("dve", QG[:, 2:3], PP[:, 118:119], 0.125, None, ALU.mult, None, [bPP], [bQG])
                CP("dve", QG[:, 1:2], PP[:, 117:118], [bPP], [bQG])
                CP("dve", QG[:, 3:4], PP[:, 119:120], [bPP], [bQG])
                ACT(ESK[:], RP[:, 276:280], AF.Exp, [bRP], [bQG])
                pi = 0
                with ExitStack() as st2:
                    COS = sb(st2, "COS", [128, LL])
                    SIN = sb(st2, "SIN", [128, LL])
                    bROPE = Buf("ROPE")
                    DMA(COS[:], rope_d[0], (), [bROPE])
                    DMA(SIN[:], rope_d[1], (), [bROPE])
                    XR = [sb(st2, "dXR%d" % i, [128, T]) for i in range(2)]
                    bXR = [Buf("dXR%d" % i) for i in range(2)]
                    SQ = sb(st2, "dSQ", [128, 512], BF16)
                    bSQ = Buf("dSQ")
                    RSq = sb(st2, "dRS", [128, 512])
                    bRSq = Buf("dRS")
                    QN = sb(st2, "dQN", [128, 512])
                    bQN = Buf("dQN")
                    TA = sb(st2, "dTA", [128, 512])
                    TB = sb(st2, "dTB", [128, 512])
                    bTA, bTB = Buf("dTA"), Buf("dTB")
                    VS = sb(st2, "dVS", [128, NCH, 64])
                    bVS = Buf("dVS")
                    it = 0
                    for ty in range(2):
                        base = I_CQ if ty == 0 else I_DQ
                        for idx in range(4):
                            dest = (QT[ty][idx] if idx < 2 else KT[ty][idx - 2])
                            gcol = ty * 2 + (0 if idx < 2 else 1)
                            xr, bxr = XR[it % 2], bXR[it % 2]
                            it += 1
                            DMA(xr[:], PTf_d[base + idx], (), [bxr])
                            for (t0, n, which) in GROUPS:
                                ACT(SQ[:, 0:n], xr[:, t0:t0 + n], AF.Square, [bxr], [bSQ])
                                p, bp = PS[pi % 8], bPS[pi % 8]
                                pi += 1
                                MM(p[:, 0:n], CSTb[:, C_BD, :], SQ[:, 0:n], True, True, [bCSTb, bSQ], [bp])
                                ACT(RSq[:, 0:n], p[:, 0:n], AF.Sqrt, [bp], [bRSq], bias=EPS, scale=1.0 / 64)
                                RECIP(RSq[:, 0:n], RSq[:, 0:n], [bRSq], [bRSq])
                                if which:
                                    STT(dest[:, t0:t0 + n], xr[:, t0:t0 + n], QG[:, gcol:gcol + 1], RSq[:, 0:n], ALU.mult, ALU.mult,
                                        [bxr, bQG, bRSq], [bQK])
                                else:
                                    lt0 = t0 - LC
                                    STT(QN[:, 0:n], xr[:, t0:t0 + n], QG[:, gcol:gcol + 1], RSq[:, 0:n], ALU.mult, ALU.mult,
                                        [bxr, bQG, bRSq], [bQN])
                                    p2, bp2 = PS[pi % 8], bPS[pi % 8]
                                    pi += 1
                                    MM(p2[:, 0:n], CST[:, C_PERM, :], QN[:, 0:n], True, True, [bCST, bQN], [bp2])
                                    TT("pool", TA[:, 0:n], QN[:, 0:n], COS[:, lt0:lt0 + n], ALU.mult, [bQN, bROPE], [bTA])
                                    TT("dve", TB[:, 0:n], p2[:, 0:n], SIN[:, lt0:lt0 + n], ALU.mult, [bp2, bROPE], [bTB])
                                    TT("pool", dest[:, t0:t0 + n], TA[:, 0:n], TB[:, 0:n], ALU.add, [bTA, bTB], [bQK])
                        for hk in range(2):
                            col0 = 264 + ty * 128 + hk * 64
                            src = PTt_d[:, col0:col0 + 64].rearrange("(c p) f -> p c f", p=128)
                            DMA(VS[:, 0:17, :], src[:, 0:17, :], (), [bVS])
                            DMA(VS[:, 17:34, :], src[:, 17:34, :], (), [bVS])
                            CP("dve", VA[ty][hk][:, :, 0:64], VS[:], [bVS], [bVA])
                            MEMSET("pool", VA[ty][hk][:, :, 64:128], 1.0, [bVA])
                S.barrier()
                YO = [sb(st, "YO%d" % ty, [128, 2, T], BF16) for ty in range(2)]
                bYO = [Buf("YO0"), Buf("YO1")]
                PTs = [sb(st, "PTs%d" % i, [128, 640], BF16) for i in range(3)]
                bPTs = [Buf("PTs%d" % i) for i in range(3)]
                RD = sb(st, "RD", [128, 512])
                bRD = Buf("RD")
                si = 0
                oi = 0
                qgroups = ([(0, LC, [0, 1])] if need_ctx else []) + [(t0, n, list(range(NCH))) for (t0, n, w_) in GROUPS[1:]]
                for hk in range(2):
                    for g in range(2):
                        rows = slice(g * 64, (g + 1) * 64)
                        for (q0, nq, keys) in qgroups:
                            O, bO = PS[6 + oi % 2], bPS[6 + oi % 2]
                            oi += 1
                            for ki, kc in enumerate(keys):
                                Sp, bSp = PS[si % 3], bPS[si % 3]
                                pt, bpt = PTs[si % 3], bPTs[si % 3]
                                si += 1
                                MM(Sp[:, 0:nq], KT[0][hk][rows, kc * 128:(kc + 1) * 128], QT[0][hk][rows, q0:q0 + nq], True, True, [bQK], [bSp])
                                ACT(pt[:, 0:nq], Sp[:, 0:nq], AF.Exp, [bSp], [bpt])
                                MM(O[:, 0:nq], VA[0][hk][:, kc, :], pt[:, 0:nq], ki == 0, ki == len(keys) - 1, [bVA, bpt], [bO],
                                   inc=(ki == len(keys) - 1))
                            RECIP(RD[64:128, 0:nq], O[64:128, 0:nq], [bO], [bRD])
                            TT("dve", YO[0][rows, hk, q0:q0 + nq], O[0:64, 0:nq], RD[64:128, 0:nq], ALU.mult, [bO, bRD], [bYO[0]])
                DMA(YT_d[4:6].rearrange("j p t -> p j t"), YO[0][:], [bYO[0]], ())
                def norm_d(O, bO, rows, hk, h, q0, nq):
                    TS("dve", RD[64:128, 0:nq], O[64:128, 0:nq], ESK[64:128, h:h + 1], None, ALU.add, None, [bO, bQG], [bRD])
                    RECIP(RD[64:128, 0:nq], RD[64:128, 0:nq], [bRD], [bRD])
                    TT("dve", YO[1][rows, hk, q0:q0 + nq], O[0:64, 0:nq], RD[64:128, 0:nq], ALU.mult, [bO, bRD], [bYO[1]])

                for hk in range(2):
                    for g in range(2):
                        h = 2 * hk + g
                        rows = slice(g * 64, (g + 1) * 64)
                        if need_ctx:
                            O, bO = PS[6 + oi % 2], bPS[6 + oi % 2]
                            oi += 1
                            pt, bpt = PTs[si % 3], bPTs[si % 3]
                            for kc in range(2):
                                Sp, bSp = PS[si % 3], bPS[si % 3]
                                si += 1
                                MM(Sp[:, 0:LC], KT[1][hk][rows, kc * 128:(kc + 1) * 128], QT[1][hk][rows, 0:LC], True, True, [bQK], [bSp])
                                ACT(pt[:, kc * 256:(kc + 1) * 256], Sp[:, 0:LC], AF.Exp, [bSp], [bpt])
                            for kc in range(2):
                                MM(O[:, 0:LC], VA[1][hk][:, kc, :], pt[:, kc * 256:(kc + 1) * 256], kc == 0, kc == 1, [bVA, bpt], [bO], inc=(kc == 1))
                            norm_d(O, bO, rows, hk, h, 0, LC)
                        for (t0, n, w_) in GROUPS[1:]:
                            O, bO = PS[6 + oi % 2], bPS[6 + oi % 2]
                            oi += 1
                            for blk in range(4):
                                nb = (t0 - LC) // 128 + blk
                                qb = t0 + blk * 128
                                lat = []
                                if nb > 0:
                                    lat.append((2 + nb - 1, C_NEGB))
                                lat.append((2 + nb, None))
                                if nb < 31:
                                    lat.append((2 + nb + 1, C_NEGF))
                                SA, bSA = PS[(2 * si) % 6], bPS[(2 * si) % 6]
                                SB, bSB = PS[(2 * si + 1) % 6], bPS[(2 * si + 1) % 6]
                                pt, bpt = PTs[si % 3], bPTs[si % 3]
                                si += 1
                                for ii, (kc, mk) in enumerate(lat):
                                    last = (ii == len(lat) - 1)
                                    MM(SA[:, ii * 128:(ii + 1) * 128], KT[1][hk][rows, kc * 128:(kc + 1) * 128], QT[1][hk][rows, qb:qb + 128],
                                       True, mk is None, [bQK], [bSA], inc=(last and mk is None))
                                    if mk is not None:
                                        MM(SA[:, ii * 128:(ii + 1) * 128], identb, CSTb[:, mk, :], False, True, [bCSTb], [bSA], inc=last)
                                for kc in range(2):
                                    MM(SB[:, kc * 128:(kc + 1) * 128], KT[1][hk][rows, kc * 128:(kc + 1) * 128], QT[1][hk][rows, qb:qb + 128],
                                       True, True, [bQK], [bSB], inc=(kc == 1))
                                nl = len(lat)
                                ACT(pt[:, 0:nl * 128], SA[:, 0:nl * 128], AF.Exp, [bSA], [bpt])
                                ACT(pt[:, 384:640], SB[:, 0:256], AF.Exp, [bSB], [bpt])
                                kvs = [(kc, ii * 128) for ii, (kc, mk) in enumerate(lat)] + [(0, 384), (1, 512)]
                                for ii, (kc, off) in enumerate(kvs):
                                    MM(O[:, blk * 128:(blk + 1) * 128], VA[1][hk][:, kc, :], pt[:, off:off + 128], ii == 0, ii == len(kvs) - 1,
                                       [bVA, bpt], [bO], inc=(ii == len(kvs) - 1))
                            norm_d(O, bO, rows, hk, h, t0, n)
                DMA(YT_d[6:8].rearrange("j p t -> p j t"), YO[1][:], [bYO[1]], ())
            S.barrier()

        def phase_E(l, need_ctx):
            with ExitStack() as st:
                WO = sb(st, "WO", [128, 8, 1024], BF16)
                W1 = sb(st, "W1", [128, 8, 4096], BF16)
                W2 = sb(st, "W2", [128, 32, 1024], BF16)
                bWO, bW1, bW2 = Buf("WO"), Buf("W1"), Buf("W2")
                DMA(WO[:], wout_d[l].rearrange("(j p) c -> p j c", p=128), (), [bWO], q="pool")
                for q4 in range(4):
                    DMA(W1[:, :, q4 * 1024:(q4 + 1) * 1024], w1_d[l, :, q4 * 1024:(q4 + 1) * 1024].rearrange("(j p) c -> p j c", p=128), (), [bW1], q="pool")
                for q4 in range(4):
                    DMA(W2[:, q4 * 8:(q4 + 1) * 8, :], w2_d[l, q4 * 1024:(q4 + 1) * 1024, :].rearrange("(k p) c -> p k c", p=128), (), [bW2], q="pool")
                NE = 256
                XT = [sb(st, "eXT%d" % i, [128, 8, NE]) for i in range(2)]
                bXT = [Buf("eXT%d" % i) for i in range(2)]
                YTs = [sb(st, "eYT%d" % i, [128, 8, NE], BF16) for i in range(2)]
                bYTs = [Buf("eYT%d" % i) for i in range(2)]
                X1 = [sb(st, "eX1%d" % i, [128, 8, NE]) for i in range(2)]
                bX1 = [Buf("eX1%d" % i) for i in range(2)]
                SQ = sb(st, "eSQ", [128, 8, NE], BF16)
                bSQ = Buf("eSQ")
                RS = sb(st, "eRS", [128, NE])
                bRS = Buf("eRS")
                H2 = sb(st, "eH2", [128, 8, NE], BF16)
                bH2 = Buf("eH2")
                RL = [sb(st, "eRL%d" % i, [128, NE]) for i in range(3)]
                bRL = [Buf("eRL%d" % i) for i in range(3)]
                AK = [sb(st, "eAK%d" % i, [128, NE], BF16) for i in range(3)]
                bAK = [Buf("eAK%d" % i) for i in range(3)]
                ki = 0
                starts = list(range(0 if need_ctx else LC, T, NE))
                for gi, t0 in enumerate(starts):
                    w = 1 if t0 < LC else 0
                    xt, bxt = XT[gi % 2], bXT[gi % 2]
                    yt, byt = YTs[gi % 2], bYTs[gi % 2]
                    x1, bx1 = X1[gi % 2], bX1[gi % 2]
                    DMA(xt[:], xT_d[:, :, t0:t0 + NE].rearrange("j p t -> p j t"), (), [bxt])
                    DMA(yt[:], YT_d[:, :, t0:t0 + NE].rearrange("j p t -> p j t"), (), [byt])
                    for fo in range(8):
                        p, bp = PS[fo % 3], bPS[fo % 3]
                        for k in range(8):
                            MM(p[:, 0:NE], WO[:, k, fo * 128:(fo + 1) * 128], yt[:, k, :], k == 0, k == 7, [bWO, byt], [bp], inc=(k == 7))
                        STT(x1[:, fo, :], p[:, 0:NE], MODS[:, 16 + fo, w:w + 1], xt[:, fo, :], ALU.mult, ALU.add, [bp, bMODS, bxt], [bx1])
                    ACT(SQ[:], x1[:], AF.Square, [bx1], [bSQ])
                    pss, bpss = PS[3], bPS[3]
                    for j in range(8):
                        MM(pss[:, 0:NE], CSTb[:, C_ONE, :], SQ[:, j, :], j == 0, j == 7, [bSQ, bCSTb], [bpss], inc=(j == 7))
                    ACT(RS[:], pss[:, 0:NE], AF.Sqrt, [bpss], [bRS], bias=EPS, scale=1.0 / D)
                    RECIP(RS[:], RS[:], [bRS], [bRS])
                    TT("dve", xt[:], x1[:], RS[:].unsqueeze(1).to_broadcast([128, 8, NE]), ALU.mult, [bx1, bRS], [bxt])
                    for j in range(8):
                        ACT(H2[:, j, :], xt[:, j, :], AF.Identity, [bxt, bG, bMODS], [bH2],
                            bias=MODS[:, 24 + j, w:w + 1], scale=G2[:, j, w:w + 1])
                    for kh in range(32):
                        pu, bpu = PS[ki % 3], bPS[ki % 3]
                        rl, brl = RL[ki % 3], bRL[ki % 3]
                        ak, bak = AK[ki % 3], bAK[ki % 3]
                        ki += 1
                        for j in range(8):
                            MM(pu[:, 0:NE], W1[:, j, kh * 128:(kh + 1) * 128], H2[:, j, :], j == 0, j == 7, [bW1, bH2], [bpu], inc=(j == 7))
                        ACT(rl[:], pu[:, 0:NE], AF.Relu, [bpu], [brl])
                        TT("dve", ak[:], rl[:], pu[:, 0:NE], ALU.mult, [brl, bpu], [bak])
                        for fo in range(8):
                            pd, bpd = PS[4 + fo // 2], bPS[4 + fo // 2]
                            MM(pd[:, (fo % 2) * NE:(fo % 2 + 1) * NE], W2[:, kh, fo * 128:(fo + 1) * 128], ak[:], (kh == 0 and fo % 2 == 0), kh == 31,
                               [bW2, bak], [bpd], inc=(kh == 31 or fo == 7))
                    for fo in range(8):
                        pd, bpd = PS[4 + fo // 2], bPS[4 + fo // 2]
                        STT(x1[:, fo, :], pd[:, (fo % 2) * NE:(fo % 2 + 1) * NE], MODS[:, 40 + fo, w:w + 1], x1[:, fo, :], ALU.mult, ALU.add,
                            [bpd, bMODS, bx1], [bx1])
                    DMA(xT_d[:, :, t0:t0 + NE].rearrange("j p t -> p j t"), x1[:], [bx1], ())
            S.barrier()

        def phase_U():
            with ExitStack() as st:
                XT = [sb(st, "uXT%d" % i, [128, 8, 512]) for i in range(2)]
                bXT = [Buf("uXT%d" % i) for i in range(2)]
                OS = [sb(st, "uOS%d" % i, [128, D]) for i in range(2)]
                bOS = [Buf("uOS%d" % i) for i in range(2)]
                it = 0
                for gi, (t0, n, which) in enumerate(GROUPS[1:]):
                    xt, bxt = XT[gi % 2], bXT[gi % 2]
                    DMA(xt[:], xT_d[:, :, t0:t0 + n].rearrange("j p t -> p j t"), (), [bxt])
                    for s in range(4):
                        os_, bos = OS[it % 2], bOS[it % 2]
                        for half in range(2):
                            p, bp = PS[(it * 2 + half) % 8], bPS[(it * 2 + half) % 8]
                            for jj in range(4):
                                j = half * 4 + jj
                                TR(p[:, jj * 128:(jj + 1) * 128], xt[:, j, s * 128:(s + 1) * 128], ident, [bxt, bCST], [bp])
                            CP("act" if half else "dve", os_[:, half * 512:(half + 1) * 512], p[:, 0:512], [bp], [bos])
                        it += 1
                        tt = t0 - LC + s * 128
                        final_toks.append(DMA(out_d[tt:tt + 128, :], os_[:], [bos], ()))
            S.barrier()

        phase_T()
        done = False
        for l in range(n_layers):
            need_ctx = l < n_layers - 1
            for nm, fn in (("0", lambda: phase_0(l)), ("A", lambda: phase_A(l)), ("B", lambda: phase_B(l)), ("C", lambda: phase_C(l)),
                           ("D", lambda: phase_D(l, need_ctx)), ("E", lambda: phase_E(l, need_ctx))):
                if stop_after is not None and isinstance(stop_after[0], tuple):
                    if (nm, l) not in stop_after:
                        continue
                fn()
                if stop_after == (nm, l):
                    done = True
                    break
            if done:
                break
        if stop_after is None:
            phase_U()
        else:
            with ExitStack() as st:
                Z = sb(st, "ZZ", [128, D])
                bZ = Buf("ZZ")
                MEMSET("dve", Z[:], 0.0, [bZ])
                final_toks.append(DMA(out_d[0:128, :], Z[:], [bZ], ()))
        S.barrier()
        with nc.Block() as block:
            S.emit(block)
    return nc, S


def _consts():
    p = np.arange(128)[:, None]
    f = np.arange(128)[None, :]
    cst = np.zeros((NCST, 128, 128), np.float32)
    cst[C_ID] = (p == f)
    cst[C_LT] = (p > f)
    cst[C_UT] = (p < f)
    cst[C_LE] = (p <= f)
    cst[C_GE] = (p >= f)
    cst[C_ONE] = 1.0
    cst[C_BD] = (p // 64 == f // 64)
    cst[C_PERM] = (p == (f ^ 16))
    cst[C_NEGF] = NEG * (p > f)
    cst[C_NEGB] = NEG * (p < f)
    return cst


def _rope():
    t = np.arange(LL)
    row = (t // 64).astype(np.float32)
    col = (t % 64).astype(np.float32)
    inv = (np.float32(10000.0) ** (-np.arange(0, 32, 2, dtype=np.float32) / np.float32(32))).astype(np.float32)
    tab = np.zeros((2, 128, LL), np.float32)
    for d in range(128):
        dd = d % 64
        axis, half, fi = dd // 32, (dd % 32) // 16, dd % 16
        ang = ((row if axis == 0 else col) * inv[fi]).astype(np.float32)
        tab[0, d] = np.cos(ang)
        tab[1, d] = (-np.sin(ang)) if half == 0 else np.sin(ang)
    return tab


def _pack(inputs):
    f = lambda a: np.ascontiguousarray(np.asarray(a, dtype=np.float32))
    pp = np.zeros((2, 128, NPP), np.float32)
    rp = np.zeros((2, NRP), np.float32)
    lruw = np.zeros((2, 2, 2, 2, 128, 128), np.float32)
    for l in range(2):
        pp[l, :, 0:8] = f(inputs["g_mix"])[l].reshape(8, 128).T
        pp[l, :, 8:16] = f(inputs["g_ffn"])[l].reshape(8, 128).T
        pp[l, :, 16:64] = f(inputs["b_mod"])[l].reshape(48, 128).T
        cw = f(inputs["ssd_conv_w"])[l]
        for ci in range(6):
            pp[l, :, 64 + ci * 4:64 + ci * 4 + 4] = cw[:, ci * 128:(ci + 1) * 128].T
        pp[l, :, 88:94] = f(inputs["ssd_conv_b"])[l].reshape(6, 128).T
        lw = f(inputs["lru_conv_w"])[l]
        for c2 in range(2):
            pp[l, :, 94 + c2 * 4:94 + c2 * 4 + 4] = lw[:, c2 * 128:(c2 + 1) * 128].T
        pp[l, :, 102:104] = f(inputs["lru_conv_b"])[l].reshape(2, 128).T
        pp[l, :, 104:108] = f(inputs["lru_lambda"])[l].reshape(4, 128).T
        pp[l, :, 108:112] = f(inputs["lru_b_a"])[l].reshape(4, 128).T
        pp[l, :, 112:116] = f(inputs["lru_b_i"])[l].reshape(4, 128).T
        for k, nm in enumerate(("gqa_q_norm", "gqa_k_norm", "swa_q_norm", "swa_k_norm")):
            pp[l, :, 116 + k] = np.tile(f(inputs[nm])[l], 2)
        rp[l, 0:256] = f(inputs["ssd_norm_g"])[l]
        rp[l, 256:260] = f(inputs["ssd_d"])[l]
        rp[l, 260:268] = f(inputs["ssd_dt_bias"])[l].reshape(8)
        rp[l, 268:276] = f(inputs["ssd_a_log"])[l].reshape(8)
        rp[l, 276:280] = f(inputs["swa_sink"])[l]
        for dr in range(2):
            for gt, nm in enumerate(("lru_w_a", "lru_w_i")):
                w = f(inputs[nm])[l, dr]
                for c2 in range(2):
                    for bl in range(2):
                        lruw[l, dr, gt, c2, bl * 64:(bl + 1) * 64, bl * 64:(bl + 1) * 64] = w[c2 * 2 + bl]
    return pp, rp, lruw


def make_in_maps(inputs, cores):
    f = lambda a: np.ascontiguousarray(np.asarray(a, dtype=np.float32))
    pp, rp, lruw = _pack(inputs)
    cst = _consts()
    rope = _rope()
    shared = {"w_mod": f(inputs["w_mod"]), "w_in": f(inputs["w_in"]), "w_out": f(inputs["w_out"]),
              "w_ffn1": f(inputs["w_ffn1"]), "w_ffn2": f(inputs["w_ffn2"]), "pp": pp, "rp": rp, "lruw": lruw,
              "cst": cst, "rope": rope}
    maps = []
    for b in cores:
        c2 = np.stack([f(inputs["c"])[b], f(inputs["c_ctx"])], axis=0)
        c2 = np.ascontiguousarray(c2.reshape(2, 8, 128).transpose(2, 1, 0))
        m = dict(shared)
        m["x"] = f(inputs["x"])[b]
        m["ctx"] = f(inputs["ctx"])[b]
        m["c2"] = c2
        maps.append(m)
    return maps


_NC_CACHE = {}


def kernel(**inputs):
    if "nc" not in _NC_CACHE:
        _NC_CACHE["nc"] = build()[0]
    nc = _NC_CACHE["nc"]
    maps = make_in_maps(inputs, range(4))
    res = run_bass_kernel_spmd(nc, maps, core_ids=list(range(4)))
    return np.stack([np.asarray(r["out"], dtype=np.float32) for r in res.results], axis=0)
```

```python
import numpy as np
import concourse.bass as bass
import concourse.mybir as mybir
from concourse.bass_utils import run_bass_kernel_spmd
from contextlib import ExitStack

F32 = mybir.dt.float32
BF16 = mybir.dt.bfloat16
ALU = mybir.AluOpType
AF = mybir.ActivationFunctionType
AX = mybir.AxisListType

T = 4352
LC = 256
LL = 4096
D = 1024
NCH = 34
EPS = 1e-6
NEG = -30000.0
GROUPS = [(0, 256, 1)] + [(256 + i * 512, 512, 0) for i in range(8)]

FM = [(256, 128, 0), (384, 128, 0), (512, 128, 0), (640, 128, 0), (768, 128, 0), (896, 128, 0),
      (1032, 128, 0), (1160, 128, 0), (1288, 128, 0), (1416, 128, 0),
      (1544, 128, 0), (1672, 128, 0), (1800, 64, 1), (1864, 64, 1),
      (2056, 128, 0), (2184, 128, 0), (2312, 64, 1), (2376, 64, 1)]
I_X, I_B, I_C, I_G, I_R, I_CQ, I_CK, I_DQ, I_DK = 0, 2, 4, 6, 8, 10, 12, 14, 16
TMC = [(0, 256), (1024, 8), (1928, 128), (2440, 128)]
NTM = 520
NPP = 120
NRP = 280
(C_ID, C_LT, C_UT, C_LE, C_GE, C_ONE, C_BD, C_PERM, C_NEGF, C_NEGB) = range(10)
NCST = 10


class Buf:
    __slots__ = ("name", "w", "r")

    def __init__(self, name):
        self.name = name
        self.w = None
        self.r = {}


class Sched:
    COMPUTE = ("pe", "act", "dve", "pool")
    ALL = ("pe", "act", "dve", "pool", "sp")

    def __init__(self, nc, es, n_dma_sems=32, same_engine_sync=True):
        self.nc = nc
        self.streams = {e: [] for e in self.ALL}
        self.sems = {}
        for e in self.COMPUTE:
            self.sems[e] = es.enter_context(nc.semaphore("s_" + e))
        self.cnt = {e: 0 for e in self.COMPUTE}
        self.pending = {e: False for e in self.COMPUTE}
        self.dsem = [es.enter_context(nc.semaphore("d%d" % i)) for i in range(n_dma_sems)]
        self.dcnt = [0] * n_dma_sems
        self.dnext = 0
        self.waited = {e: {} for e in self.ALL}
        self.same_engine_sync = same_engine_sync
        self.nwaits = 0

    def _semof(self, k):
        if isinstance(k, tuple):
            return self.dsem[k[1]]
        return self.sems[k]

    def _deps(self, eng, reads, writes, extra=()):
        deps = {}

        def add(tok):
            if tok is None:
                return
            k, v = tok
            if deps.get(k, 0) < v:
                deps[k] = v

        for b in reads:
            add(b.w)
        for b in writes:
            add(b.w)
            for k, v in b.r.items():
                add((k, v))
        for t in extra:
            add(t)
        out = []
        for k, v in deps.items():
            if k == eng:
                if eng == "pe" or not self.same_engine_sync:
                    continue
                if v > self.cnt[eng]:
                    continue
            if self.waited[eng].get(k, 0) >= v:
                continue
            self.waited[eng][k] = v
            out.append((k, v))
        self.nwaits += len(out)
        return out

    def op(self, eng, fn, reads=(), writes=(), inc=True):
        waits = self._deps(eng, reads, writes)
        if inc:
            self.cnt[eng] += 1
            tok = (eng, self.cnt[eng])
            self.pending[eng] = False
        else:
            tok = (eng, self.cnt[eng] + 1)
            self.pending[eng] = True
        self.streams[eng].append((waits, fn, tok, 1 if inc else 0))
        for b in writes:
            b.w = tok
            b.r = {}
        for b in reads:
            if b.w is not tok:
                if b.r.get(eng, 0) < tok[1]:
                    b.r[eng] = tok[1]
        return tok

    def dma(self, fn, reads=(), writes=(), q="sp"):
        j = self.dnext
        self.dnext = (self.dnext + 1) % len(self.dsem)
        prev = (("d", j), 16 * self.dcnt[j])
        waits = self._deps(q, reads, writes, extra=(prev,) if self.dcnt[j] else ())
        self.dcnt[j] += 1
        tok = (("d", j), 16 * self.dcnt[j])
        self.streams[q].append((waits, fn, tok, 16))
        for b in writes:
            b.w = tok
            b.r = {}
        for b in reads:
            if b.w is not tok:
                b.r[tok[0]] = tok[1]
        return tok

    def wait_all(self, eng, toks):
        waits = self._deps(eng, (), (), extra=toks)
        if waits:
            self.streams[eng].append((waits, None, None, 0))

    def barrier(self):
        for e in self.COMPUTE:
            assert not self.pending[e], "pending un-incremented op on " + e
        toks = [(e, self.cnt[e]) for e in self.COMPUTE if self.cnt[e]]
        toks += [(("d", j), 16 * self.dcnt[j]) for j in range(len(self.dsem)) if self.dcnt[j]]
        for e in self.ALL:
            self.wait_all(e, toks)

    def emit(self, block):
        def mk(ename):
            def body(eng):
                for waits, fn, tok, inc in self.streams[ename]:
                    for k, v in waits:
                        eng.wait_ge(self._semof(k), v)
                    if fn is None:
                        continue
                    inst = fn(eng)
                    if inc:
                        inst.then_inc(self._semof(tok[0]), inc)
            return body

        block.tensor(mk("pe"))
        block.scalar(mk("act"))
        block.vector(mk("dve"))
        block.gpsimd(mk("pool"))
        block.sync(mk("sp"))


def rev_ap(ap):
    dims = [list(d) for d in ap.ap]
    fs, fc = dims[-1]
    dims[-1] = [-fs, fc]
    return bass.AP(ap.tensor, ap.offset + (fc - 1) * fs, dims)


def build(dbg=(), stop_after=None, n_layers=2):
    nc = bass.Bass("TRN2", target_bir_lowering=False)

    def din(name, shape, dt=F32):
        return nc.dram_tensor(name, list(shape), dt, kind="ExternalInput").ap()

    x_d = din("x", [LL, D])
    ctx_d = din("ctx", [LC, D])
    c2_d = din("c2", [128, 8, 2])
    wmod_d = din("w_mod", [2, D, 6144])
    win_d = din("w_in", [2, D, 2568])
    wout_d = din("w_out", [2, D, D])
    w1_d = din("w_ffn1", [2, D, 4096])
    w2_d = din("w_ffn2", [2, 4096, D])
    pp_d = din("pp", [2, 128, NPP])
    rp_d = din("rp", [2, NRP])
    lruw_d = din("lruw", [2, 2, 2, 2, 128, 128])
    cst_d = din("cst", [NCST, 128, 128])
    rope_d = din("rope", [2, 128, LL])
    out_d = nc.dram_tensor("out", [LL, D], F32, kind="ExternalOutput").ap()

    def scratch(name, shape, dt):
        kind = "ExternalOutput" if name in dbg else "Internal"
        return nc.dram_tensor(name, list(shape), dt, kind=kind).ap()

    xT_d = scratch("xT", [8, 128, T], F32)
    PTf_d = scratch("PTf", [18, 128, T], F32)
    PTt_d = scratch("PTt", [T, NTM], F32)
    YT_d = scratch("YT", [8, 128, T], BF16)
    MOD_d = scratch("MODd", [2, 128, 96], F32) if "MODd" in dbg else None

    es = ExitStack()
    with es:
        S = Sched(nc, es)

        _uid = [0]

        def sb(st, name, shape, dt=F32):
            _uid[0] += 1
            return st.enter_context(nc.sbuf_tensor("%s_%d" % (name, _uid[0]), list(shape), dt))

        def MM(out, lhsT, rhs, start, stop, r, w, inc=True):
            S.op("pe", lambda e: e.matmul(out, lhsT, rhs, start=start, stop=stop), r, w, inc=inc)

        def TR(out, in_, ident, r, w):
            S.op("pe", lambda e: e.transpose(out, in_, ident), r, w)

        def ACT(out, in_, func, r, w, bias=None, scale=None):
            kw = {}
            if bias is not None:
                kw["bias"] = bias
            if scale is not None:
                kw["scale"] = scale
            S.op("act", lambda e: e.activation(out=out, in_=in_, func=func, **kw), r, w)

        def TT(eng, out, in0, in1, op, r, w):
            S.op(eng, lambda e: e.tensor_tensor(out=out, in0=in0, in1=in1, op=op), r, w)

        def TS(eng, out, in0, s1, s2, op0, op1, r, w):
            if s2 is None:
                S.op(eng, lambda e: e.tensor_scalar(out=out, in0=in0, scalar1=s1, scalar2=None, op0=op0), r, w)
            else:
                S.op(eng, lambda e: e.tensor_scalar(out=out, in0=in0, scalar1=s1, scalar2=s2, op0=op0, op1=op1), r, w)

        def STT(out, in0, scalar, in1, op0, op1, r, w):
            S.op("dve", lambda e: e.scalar_tensor_tensor(out=out, in0=in0, scalar=scalar, in1=in1, op0=op0, op1=op1), r, w)

        def CP(eng, out, in_, r, w):
            if eng == "act":
                S.op("act", lambda e: e.activation(out=out, in_=in_, func=AF.Copy), r, w)
            else:
                S.op(eng, lambda e: e.tensor_copy(out=out, in_=in_), r, w)

        def RECIP(out, in_, r, w):
            S.op("dve", lambda e: e.reciprocal(out=out, in_=in_), r, w)

        def MEMSET(eng, ap, val, w):
            S.op(eng, lambda e: e.memset(ap, val), (), w)

        def DMA(out, in_, r, w, q="sp"):
            return S.dma(lambda e: e.dma_start(out=out, in_=in_), r, w, q=q)

        PS = [es.enter_context(nc.psum_tensor("ps%d" % i, [128, 512], F32)) for i in range(8)]
        bPS = [Buf("ps%d" % i) for i in range(8)]
        CST = sb(es, "CST", [128, NCST, 128])
        bCST = Buf("CST")
        DMA(CST[:], cst_d.rearrange("c p f -> p c f"), (), [bCST])
        CSTb = sb(es, "CSTb", [128, NCST, 128], BF16)
        bCSTb = Buf("CSTb")
        CP("dve", CSTb[:], CST[:], [bCST], [bCSTb])
        ident = CST[:, C_ID, :]
        identb = CSTb[:, C_ID, :]
        CS = sb(es, "CS", [128, 8, 2])
        bCS = Buf("CS")
        DMA(CS[:], c2_d, (), [bCS])
        ACT(CS[:], CS[:], AF.Silu, [bCS], [bCS])
        MODS = sb(es, "MODS", [128, 48, 2])
        bMODS = Buf("MODS")
        G1 = sb(es, "G1", [128, 8, 2])
        G2 = sb(es, "G2", [128, 8, 2])
        bG = Buf("G12")
        PP = sb(es, "PP", [128, NPP])
        bPP = Buf("PP")
        RP = sb(es, "RP", [128, NRP])
        bRP = Buf("RP")
        final_toks = []

        def phase_T():
            with ExitStack() as st:
                XK = [sb(st, "XK%d" % i, [128, D]) for i in range(2)]
                bXK = [Buf("XK%d" % i) for i in range(2)]
                XS = [sb(st, "XS%d" % i, [128, 8, 512]) for i in range(2)]
                bXS = [Buf("XSs%d" % i) for i in range(2)]
                it = 0
                for gi, (t0, n, which) in enumerate(GROUPS):
                    xs, bxs = XS[gi % 2], bXS[gi % 2]
                    for s in range(n // 128):
                        xk, bxk = XK[it % 2], bXK[it % 2]
                        tt = t0 + s * 128
                        src = ctx_d[tt:tt + 128, :] if which else x_d[tt - LC:tt - LC + 128, :]
                        DMA(xk[:], src, (), [bxk])
                        for half in range(2):
                            p, bp = PS[(it * 2 + half) % 8], bPS[(it * 2 + half) % 8]
                            for jj in range(4):
                                j = half * 4 + jj
                                TR(p[:, jj * 128:(jj + 1) * 128], xk[:, j * 128:(j + 1) * 128], ident, [bxk, bCST], [bp])
                            CP("act" if half else "dve", xs[:, half * 4:half * 4 + 4, s * 128:(s + 1) * 128],
                               p[:].rearrange("p (j t) -> p j t", j=4), [bp], [bxs])
                        it += 1
                    DMA(xT_d[:, :, t0:t0 + n].rearrange("j p t -> p j t"), xs[:, :, 0:n], [bxs], ())
            S.barrier()

        def phase_0(l):
            with ExitStack() as st:
                DMA(PP[:], pp_d[l], (), [bPP])
                DMA(RP[:], rp_d[l:l + 1, :].partition_broadcast(128) if False else rp_d[l].partition_broadcast(128), (), [bRP])
                WM = [sb(st, "WM%d" % i, [128, 8, 1024]) for i in range(2)]
                bWM = [Buf("WM%d" % i) for i in range(2)]
                pm, bpm = PS[0], bPS[0]
                for sl in range(6):
                    wm, bwm = WM[sl % 2], bWM[sl % 2]
                    DMA(wm[:], wmod_d[l, :, sl * 1024:(sl + 1) * 1024].rearrange("(j p) c -> p j c", p=128), (), [bwm])
                    for oc8 in range(8):
                        oc = sl * 8 + oc8
                        for j in range(8):
                            MM(pm[:, oc * 2:oc * 2 + 2], wm[:, j, oc8 * 128:(oc8 + 1) * 128], CS[:, j, :],
                               j == 0, j == 7, [bwm, bCS], [bpm], inc=(j == 7))
                TT("dve", MODS[:], pm[:, 0:96].rearrange("p (o w) -> p o w", w=2),
                   PP[:, 16:64].unsqueeze(2).to_broadcast([128, 48, 2]), ALU.add, [bpm, bPP], [bMODS])
                for (G, gcol, sccol) in ((G1, 0, 8), (G2, 8, 32)):
                    TS("dve", G[:], MODS[:, sccol:sccol + 8, :], 1.0, None, ALU.add, None, [bMODS], [bG])
                    TT("dve", G[:], G[:], PP[:, gcol:gcol + 8].unsqueeze(2).to_broadcast([128, 8, 2]), ALU.mult, [bG, bPP], [bG])
                if MOD_d is not None:
                    final_toks.append(DMA(MOD_d[l], MODS[:].rearrange("p o w -> p (o w)"), [bMODS], ()))
            S.barrier()

        def phase_A(l):
            with ExitStack() as st:
                WF = sb(st, "WF", [128, 8, 18 * 128], BF16)
                WK = sb(st, "WK", [128, 8, NTM], BF16)
                bW = Buf("Win")
                for ci, (c0, ncol, dup) in enumerate(FM):
                    src = win_d[l, :, c0:c0 + ncol].rearrange("(j p) c -> p j c", p=128)
                    DMA(WF[:, :, ci * 128:ci * 128 + ncol], src, (), [bW], q="pool")
                    if dup:
                        DMA(WF[:, :, ci * 128 + 64:ci * 128 + 128], src, (), [bW], q="pool")
                off = 0
                for (c0, ncol) in TMC:
                    DMA(WK[:, :, off:off + ncol], win_d[l, :, c0:c0 + ncol].rearrange("(j p) c -> p j c", p=128), (), [bW], q="pool")
                    off += ncol
                XT = [sb(st, "XT%d" % i, [128, 8, 512]) for i in range(2)]
                bXT = [Buf("XT%d" % i) for i in range(2)]
                SQ = sb(st, "SQ", [128, 8, 512], BF16)
                bSQ = Buf("SQ")
                RS = sb(st, "RS", [128, 512])
                bRS = Buf("RS")
                HT = [sb(st, "HT%d" % i, [128, 8, 512], BF16) for i in range(2)]
                bHT = [Buf("HT%d" % i) for i in range(2)]
                FS = [sb(st, "FS%d" % i, [128, 512]) for i in range(4)]
                bFS = [Buf("FS%d" % i) for i in range(4)]
                ZS = [sb(st, "ZS%d" % i, [128, NTM]) for i in range(2)]
                bZS = [Buf("ZS%d" % i) for i in range(2)]
                fsi = 0
                zsi = 0
                pi = 0
                for gi, (t0, n, which) in enumerate(GROUPS):
                    xt, bxt = XT[gi % 2], bXT[gi % 2]
                    ht, bht = HT[gi % 2], bHT[gi % 2]
                    DMA(xt[:, :, 0:n], xT_d[:, :, t0:t0 + n].rearrange("j p t -> p j t"), (), [bxt])
                    ACT(SQ[:, :, 0:n], xt[:, :, 0:n], AF.Square, [bxt], [bSQ])
                    pss, bpss = PS[pi % 8], bPS[pi % 8]
                    pi += 1
                    for j in range(8):
                        MM(pss[:, 0:n], CSTb[:, C_ONE, :], SQ[:, j, 0:n], j == 0, j == 7, [bSQ, bCSTb], [bpss], inc=(j == 7))
                    ACT(RS[:, 0:n], pss[:, 0:n], AF.Sqrt, [bpss], [bRS], bias=EPS, scale=1.0 / D)
                    RECIP(RS[:, 0:n], RS[:, 0:n], [bRS], [bRS])
                    TT("dve", xt[:, :, 0:n], xt[:, :, 0:n], RS[:, 0:n].unsqueeze(1).to_broadcast([128, 8, n]), ALU.mult, [bxt, bRS], [bxt])
                    for j in range(8):
                        ACT(ht[:, j, 0:n], xt[:, j, 0:n], AF.Identity, [bxt, bG, bMODS], [bht],
                            bias=MODS[:, j, which:which + 1], scale=G1[:, j, which:which + 1])
                    for ci in range(18):
                        p, bp = PS[pi % 8], bPS[pi % 8]
                        pi += 1
                        for j in range(8):
                            MM(p[:, 0:n], WF[:, j, ci * 128:(ci + 1) * 128], ht[:, j, 0:n], j == 0, j == 7, [bW, bht], [bp], inc=(j == 7))
                        fs, bfs = FS[fsi % 4], bFS[fsi % 4]
                        CP("act" if fsi % 2 else "dve", fs[:, 0:n], p[:, 0:n], [bp], [bfs])
                        fsi += 1
                        DMA(PTf_d[ci, :, t0:t0 + n], fs[:, 0:n], [bfs], ())
                    for s in range(n // 128):
                        p, bp = PS[pi % 8], bPS[pi % 8]
                        p2, bp2 = PS[(pi + 1) % 8], bPS[(pi + 1) % 8]
                        pi += 2
                        for j in range(8):
                            MM(p[:, 0:512], ht[:, j, s * 128:(s + 1) * 128], WK[:, j, 0:512], j == 0, j == 7, [bW, bht], [bp], inc=(j == 7))
                        for j in range(8):
                            MM(p2[:, 0:8], ht[:, j, s * 128:(s + 1) * 128], WK[:, j, 512:520], j == 0, j == 7, [bW, bht], [bp2], inc=(j == 7))
                        zs, bzs = ZS[zsi % 2], bZS[zsi % 2]
                        zsi += 1
                        CP("act", zs[:, 0:512], p[:, 0:512], [bp], [bzs])
                        CP("dve", zs[:, 512:520], p2[:, 0:8], [bp2], [bzs])
                        tt = t0 + s * 128
                        DMA(PTt_d[tt:tt + 128, :], zs[:], [bzs], ())
            S.barrier()


        def conv_seg(out, inp, wc, bc, s0, L, r, w):
            S.op("dve", lambda e: e.tensor_scalar(out=out[:, s0:s0 + L], in0=inp[:, s0:s0 + L], scalar1=PP[:, wc + 2:wc + 3],
                                                   scalar2=PP[:, bc:bc + 1], op0=ALU.mult, op1=ALU.add), r, w)
            STT(out[:, s0 + 2:s0 + L], inp[:, s0:s0 + L - 2], PP[:, wc:wc + 1], out[:, s0 + 2:s0 + L], ALU.mult, ALU.add, r + w, w)
            STT(out[:, s0 + 1:s0 + L], inp[:, s0:s0 + L - 1], PP[:, wc + 1:wc + 2], out[:, s0 + 1:s0 + L], ALU.mult, ALU.add, r + w, w)
            STT(out[:, s0:s0 + L - 1], inp[:, s0 + 1:s0 + L], PP[:, wc + 3:wc + 4], out[:, s0:s0 + L - 1], ALU.mult, ALU.add, r + w, w)

        def conv_full(out, inp, wc, bc, r, w):
            conv_seg(out, inp, wc, bc, 0, LC, r, w)
            conv_seg(out, inp, wc, bc, LC, LL, r, w)

        def phase_B(l):
            with ExitStack() as st:
                LW = sb(st, "LW", [128, 8, 128])
                bLW = Buf("LW")
                DMA(LW[:], lruw_d[l].rearrange("d g c p f -> p (d g c) f"), (), [bLW])
                SC = sb(st, "SCl", [128, 8])
                bSC = Buf("SCl")
                ACT(SC[:, 0:4], PP[:, 104:108], AF.Exp, [bPP], [bSC], scale=-1.0)
                ACT(SC[:, 0:4], SC[:, 0:4], AF.Ln, [bSC], [bSC], bias=1.0)
                TS("dve", SC[:, 4:8], SC[:, 0:4], -16.0, None, ALU.mult, None, [bSC], [bSC])
                TS("dve", SC[:, 0:4], SC[:, 0:4], -8.0, None, ALU.mult, None, [bSC], [bSC])
                names = ["XR", "XC", "Rt", "IGt", "Mt", "HF", "HB"]
                tl = {n_: sb(st, "lru_" + n_, [128, T]) for n_ in names}
                bf = {n_: Buf("lru_" + n_) for n_ in names}
                YB = sb(st, "lru_YB", [128, T], BF16)
                bYB = Buf("lru_YB")
                pi = 0
                for c2 in range(2):
                    DMA(tl["XR"][:], PTf_d[I_R + c2], (), [bf["XR"]])
                    conv_full(tl["XC"], tl["XR"], 94 + c2 * 4, 102 + c2, [bf["XR"], bPP], [bf["XC"]])
                    DMA(tl["XR"][:], PTf_d[I_G + c2], (), [bf["XR"]])
                    for dr in range(2):
                        col = dr * 2 + c2
                        for (t0, n, which) in GROUPS:
                            for gt, dst, bcol in ((0, "Rt", 108), (1, "IGt", 112)):
                                p, bp = PS[pi % 8], bPS[pi % 8]
                                pi += 1
                                MM(p[:, 0:n], LW[:, dr * 4 + gt * 2 + c2, :], tl["XC"][:, t0:t0 + n], True, True, [bLW, bf["XC"]], [bp])
                                ACT(tl[dst][:, t0:t0 + n], p[:, 0:n], AF.Sigmoid, [bp, bPP], [bf[dst]], bias=PP[:, bcol + col:bcol + col + 1])
                        ACT(tl["Mt"][:], tl["Rt"][:], AF.Exp, [bf["Rt"], bSC], [bf["Mt"]], scale=SC[:, 4 + col:5 + col])
                        ACT(tl["Rt"][:], tl["Rt"][:], AF.Exp, [bf["Rt"], bSC], [bf["Rt"]], scale=SC[:, col:col + 1])
                        ACT(tl["Mt"][:], tl["Mt"][:], AF.Sqrt, [bf["Mt"]], [bf["Mt"]], bias=1.0, scale=-1.0)
                        TT("pool", tl["IGt"][:], tl["IGt"][:], tl["Mt"][:], ALU.mult, [bf["IGt"], bf["Mt"]], [bf["IGt"]])
                        TT("pool", tl["IGt"][:], tl["IGt"][:], tl["XC"][:], ALU.mult, [bf["IGt"], bf["XC"]], [bf["IGt"]])
                        H = tl["HF" if dr == 0 else "HB"]
                        bH = bf["HF" if dr == 0 else "HB"]
                        A_, B_ = tl["Rt"], tl["IGt"]
                        rw = ([bf["Rt"], bf["IGt"], bH], [bH])

                        def scan(s0, L, init, reverse):
                            o, a, b = H[:, s0:s0 + L], A_[:, s0:s0 + L], B_[:, s0:s0 + L]
                            if reverse:
                                o, a, b = rev_ap(o), rev_ap(a), rev_ap(b)
                            S.op("dve", lambda e: e.tensor_tensor_scan(out=o, data0=a, data1=b, initial=init, op0=ALU.mult, op1=ALU.add), rw[0], rw[1])

                        PIECE = 1024
                        if dr == 0:
                            scan(0, LC, 0.0, False)
                            for s0 in range(LC, T, PIECE):
                                scan(s0, PIECE, H[:, s0 - 1:s0], False)
                        else:
                            scan(0, LC, 0.0, True)
                            prev = H[:, 0:1]
                            for s0 in range(T - PIECE, LC - 1, -PIECE):
                                scan(s0, PIECE, prev, True)
                                prev = H[:, s0:s0 + 1]
                    TT("pool", tl["HF"][:], tl["HF"][:], tl["HB"][:], ALU.add, [bf["HF"], bf["HB"]], [bf["HF"]])
                    ACT(tl["XR"][:], tl["XR"][:], AF.Gelu, [bf["XR"]], [bf["XR"]])
                    TT("dve", YB[:], tl["HF"][:], tl["XR"][:], ALU.mult, [bf["HF"], bf["XR"]], [bYB])
                    DMA(YT_d[2 + c2], YB[:], [bYB], ())
            S.barrier()

        def phase_C(l):
            with ExitStack() as st:
                BTm = [sb(st, "BTm%d" % g, [128, T], BF16) for g in range(2)]
                CTm = [sb(st, "CTm%d" % g, [128, T], BF16) for g in range(2)]
                bBC = Buf("BCT")
                XTK = sb(st, "XTK", [128, NCH, 256])
                XTKb = sb(st, "XTKb", [128, NCH, 256], BF16)
                BTK = sb(st, "BTK", [128, NCH, 256], BF16)
                bTK = Buf("TK")
                EX = sb(st, "EX", [128, NCH, 40])
                DTt = sb(st, "DTt", [128, NCH, 8])
                LDT = sb(st, "LDT", [128, NCH, 8])
                DTA = sb(st, "DTA", [128, NCH, 8])
                WE = sb(st, "WE", [128, NCH, 8])
                A8 = sb(st, "A8", [128, 8])
                bSM = Buf("ssd_small")
                with ExitStack() as st2:
                    XR = [sb(st2, "cXR%d" % i, [128, T]) for i in range(2)]
                    bXR = [Buf("cXR%d" % i) for i in range(2)]
                    XC = sb(st2, "cXC", [128, T])
                    bXC = Buf("cXC")
                    XS = [sb(st2, "cXS%d" % i, [128, T]) for i in range(2)]
                    bXS = Buf("cXS")
                    for ci in range(6):
                        xr, bxr = XR[ci % 2], bXR[ci % 2]
                        DMA(xr[:], PTf_d[I_X + ci], (), [bxr])
                        conv_full(XC, xr, 64 + ci * 4, 88 + ci, [bxr, bPP], [bXC])
                        if ci < 2:
                            ACT(XS[ci][:], XC[:], AF.Silu, [bXC], [bXS])
                        elif ci < 4:
                            ACT(BTm[ci - 2][:], XC[:], AF.Silu, [bXC], [bBC])
                        else:
                            ACT(CTm[ci - 4][:], XC[:], AF.Silu, [bXC], [bBC])
                    for c in range(NCH):
                        tt = c * 128
                        p, bp = PS[(2 * c) % 8], bPS[(2 * c) % 8]
                        pb, bpb = PS[(2 * c + 1) % 8], bPS[(2 * c + 1) % 8]
                        for ci in range(2):
                            TR(p[:, ci * 128:(ci + 1) * 128], XS[ci][:, tt:tt + 128], ident, [bXS, bCST], [bp])
                        CP("act", XTK[:, c, :], p[:, 0:256], [bp], [bTK])
                        CP("dve", XTKb[:, c, :], p[:, 0:256], [bp], [bTK])
                        pbv = pb[:].bitcast(BF16)
                        for g in range(2):
                            TR(pbv[:, g * 128:(g + 1) * 128], BTm[g][:, tt:tt + 128], identb, [bBC, bCSTb], [bpb])
                        CP("dve", BTK[:, c, :], pbv[:, 0:256], [bpb], [bTK])
                        DMA(DTt[:, c, :], PTt_d[tt:tt + 128, 256:264], (), [bSM])
                S.barrier()
                TT("dve", DTt[:], DTt[:], RP[:, 260:268].unsqueeze(1).to_broadcast([128, NCH, 8]), ALU.add, [bSM, bRP], [bSM])
                ACT(DTt[:], DTt[:], AF.Exp, [bSM], [bSM])
                ACT(DTt[:], DTt[:], AF.Ln, [bSM], [bSM], bias=1.0)
                ACT(LDT[:], DTt[:], AF.Ln, [bSM], [bSM])
                ACT(A8[:], RP[:, 268:276], AF.Exp, [bRP], [bSM])
                STT(DTA[:], DTt[:], -1.0, A8[:].unsqueeze(1).to_broadcast([128, NCH, 8]), ALU.mult, ALU.mult, [bSM], [bSM])
                for c0 in range(0, NCH, 12):
                    nb = min(12, NCH - c0)
                    p, bp = PS[(c0 // 12) % 8], bPS[(c0 // 12) % 8]
                    for cc in range(nb):
                        for k, cm in enumerate((C_LT, C_UT, C_LE, C_GE, C_ONE)):
                            MM(p[:, cc * 40 + k * 8:cc * 40 + k * 8 + 8], CST[:, cm, :], DTA[:, c0 + cc, :], True, True,
                               [bCST, bSM], [bp], inc=(cc == nb - 1 and k == 4))
                    ACT(EX[:, c0:c0 + nb, :], p[:, 0:nb * 40].rearrange("p (c k) -> p c k", k=40), AF.Exp, [bp], [bSM])
                TT("dve", WE[:, :, 0:4], EX[:, :, 0:4], DTt[:, :, 0:4], ALU.mult, [bSM], [bSM])
                TT("dve", WE[:, :, 4:8], EX[:, :, 12:16], DTt[:, :, 4:8], ALU.mult, [bSM], [bSM])

                def bc4(ap):
                    return ap.unsqueeze(2).to_broadcast([128, 4, 64])

                def v4(ap):
                    return ap.rearrange("p (h d) -> p h d", h=4)

                HBall = sb(st, "HBall", [128, NCH, 256], BF16)
                bHBall = Buf("HBall")
                Hs = sb(st, "Hs", [128, 256])
                bHs = Buf("Hs")
                XSB = [sb(st, "XSB%d" % i, [128, 256], BF16) for i in range(2)]
                bXSB = [Buf("XSB%d" % i) for i in range(2)]
                MEMSET("pool", Hs[:], 0.0, [bHs])
                order = [1, 0] + list(range(NCH - 1, 1, -1))
                for it, c in enumerate(order):
                    xsb, bxsb = XSB[it % 2], bXSB[it % 2]
                    CP("pool", HBall[:, c, :], Hs[:], [bHs], [bHBall])
                    TT("dve", v4(xsb[:]), v4(XTK[:, c, :]), bc4(WE[:, c, 4:8]), ALU.mult, [bTK, bSM], [bxsb])
                    p, bp = PS[it % 4], bPS[it % 4]
                    for g in range(2):
                        MM(p[:, g * 128:(g + 1) * 128], BTK[:, c, g * 128:(g + 1) * 128], xsb[:, g * 128:(g + 1) * 128], True, True,
                           [bTK, bxsb], [bp], inc=(g == 1))
                    TT("pool", v4(Hs[:]), v4(Hs[:]), bc4(EX[:, c, 36:40]), ALU.mult, [bHs, bSM], [bHs])
                    TT("dve", Hs[:], Hs[:], p[:, 0:256], ALU.add, [bHs, bp], [bHs])
                RFB = [sb(st, "RFB%d" % i, [128, 8, 128]) for i in range(2)]
                bRFB = [Buf("RFB%d" % i) for i in range(2)]
                Dx = sb(st, "Dx", [128, 8, 128])
                bDx = Buf("Dx")
                GTs = sb(st, "GTs", [128, 2, 128])
                bGTs = Buf("GTs")
                DS = sb(st, "DS", [128, 4, 128])
                bDS = Buf("DS")
                MTb = sb(st, "MTb", [128, 4, 128], BF16)
                bMTb = Buf("MTb")
                Hbf = sb(st, "Hbf", [128, 256], BF16)
                bHbf = Buf("Hbf")
                t1 = sb(st, "ct1", [128, 256])
                t2 = sb(st, "ct2", [128, 256])
                t3 = sb(st, "ct3", [128, 256])
                bt1, bt2, bt3 = Buf("ct1"), Buf("ct2"), Buf("ct3")
                ZD = [sb(st, "ZD%d" % i, [128, 256]) for i in range(2)]
                bZD = [Buf("ZD%d" % i) for i in range(2)]
                ssum = sb(st, "ssum", [128, 2])
                bss = Buf("ssum")
                YN = sb(st, "YN", [128, 256])
                bYN = Buf("YN")
                YAT = sb(st, "YAT", [128, 2, T], BF16)
                bYAT = Buf("YAT")
                MEMSET("pool", Hs[:], 0.0, [bHs])
                for c in range(NCH):
                    tt = c * 128
                    rfb, brfb = RFB[c % 2], bRFB[c % 2]
                    zd, bzd = ZD[c % 2], bZD[c % 2]
                    DMA(zd[:], PTt_d[tt:tt + 128, 0:256], (), [bzd])
                    CP("pool", Hbf[:], Hs[:], [bHs], [bHbf])
                    for g in range(2):
                        MM(PS[0][:, g * 128:(g + 1) * 128], BTm[g][:, tt:tt + 128], CTm[g][:, tt:tt + 128], True, True, [bBC], [bPS[0]], inc=(g == 1))
                    CP("act", GTs[:], PS[0][:, 0:256].rearrange("p (g i) -> p g i", g=2), [bPS[0]], [bGTs])
                    for h in range(4):
                        TS("pool", rfb[:, h, :], CST[:, C_LE, :], DTA[:, c, h:h + 1], 1.0, ALU.mult, ALU.mult, [bCST, bSM], [brfb])
                        TS("pool", rfb[:, 4 + h, :], CST[:, C_GE, :], DTA[:, c, 4 + h:5 + h], 1.0, ALU.mult, ALU.mult, [bCST, bSM], [brfb])
                    for h in range(4):
                        MM(PS[1][:, h * 128:(h + 1) * 128], CST[:, C_LT, :], rfb[:, h, :], True, False, [bCST, brfb], [bPS[1]], inc=False)
                        MM(PS[1][:, h * 128:(h + 1) * 128], ident, CST[:, C_NEGF, :], False, True, [bCST], [bPS[1]], inc=(h == 3))
                    for h in range(4):
                        MM(PS[2][:, h * 128:(h + 1) * 128], CST[:, C_UT, :], rfb[:, 4 + h, :], True, False, [bCST, brfb], [bPS[2]], inc=False)
                        MM(PS[2][:, h * 128:(h + 1) * 128], ident, CST[:, C_NEGB, :], False, True, [bCST], [bPS[2]], inc=(h == 3))
                    for dh in range(8):
                        src = PS[1 + dh // 4]
                        h = dh % 4
                        ACT(Dx[:, dh, :], src[:, h * 128:(h + 1) * 128], AF.Exp, [bPS[1 + dh // 4], bSM], [bDx], bias=LDT[:, c, dh:dh + 1])
                    TT("pool", DS[:], Dx[:, 0:4, :], Dx[:, 4:8, :], ALU.add, [bDx], [bDS])
                    for g in range(2):
                        TT("dve", MTb[:, 2 * g:2 * g + 2, :], DS[:, 2 * g:2 * g + 2, :], GTs[:, g, :].unsqueeze(1).to_broadcast([128, 2, 128]),
                           ALU.mult, [bDS, bGTs], [bMTb])
                    for h in range(4):
                        MM(PS[3][:, h * 64:(h + 1) * 64], MTb[:, h, :], XTKb[:, c, h * 64:(h + 1) * 64], True, True, [bMTb, bTK], [bPS[3]], inc=(h == 3))
                    for g in range(2):
                        MM(PS[4][:, g * 128:(g + 1) * 128], CTm[g][:, tt:tt + 128], Hbf[:, g * 128:(g + 1) * 128], True, True, [bBC, bHbf], [bPS[4]], inc=False)
                        MM(PS[4][:, 256 + g * 128:256 + (g + 1) * 128], CTm[g][:, tt:tt + 128], HBall[:, c, g * 128:(g + 1) * 128], True, True,
                           [bBC, bHBall], [bPS[4]], inc=(g == 1))
                    TT("dve", v4(t1[:]), v4(PS[4][:, 0:256]), bc4(EX[:, c, 16:20]), ALU.mult, [bPS[4], bSM], [bt1])
                    TT("dve", v4(t2[:]), v4(PS[4][:, 256:512]), bc4(EX[:, c, 28:32]), ALU.mult, [bPS[4], bSM], [bt2])
                    TT("pool", t1[:], t1[:], t2[:], ALU.add, [bt1, bt2], [bt1])
                    TT("dve", t1[:], t1[:], PS[3][:, 0:256], ALU.add, [bt1, bPS[3]], [bt1])
                    TT("pool", v4(t3[:]), v4(XTK[:, c, :]), bc4(RP[:, 256:260]), ALU.mult, [bTK, bRP], [bt3])
                    TT("pool", t1[:], t1[:], t3[:], ALU.add, [bt1, bt3], [bt1])
                    ACT(t2[:], zd[:], AF.Silu, [bzd], [bt2])
                    TT("dve", t1[:], t1[:], t2[:], ALU.mult, [bt1, bt2], [bt1])
                    TT("pool", t3[:], t1[:], t1[:], ALU.mult, [bt1], [bt3])
                    S.op("dve", lambda e: e.tensor_reduce(out=ssum[:, 0:1], in_=t3[:], axis=AX.X, op=ALU.add), [bt3], [bss])
                    ACT(ssum[:, 1:2], ssum[:, 0:1], AF.Sqrt, [bss], [bss], bias=EPS, scale=1.0 / 256)
                    RECIP(ssum[:, 1:2], ssum[:, 1:2], [bss], [bss])
                    STT(YN[:], t1[:], ssum[:, 1:2], RP[:, 0:256], ALU.mult, ALU.mult, [bt1, bss, bRP], [bYN])
                    for ci in range(2):
                        TR(PS[6][:, ci * 128:(ci + 1) * 128], YN[:, ci * 128:(ci + 1) * 128], ident, [bYN, bCST], [bPS[6]])
                    CP("act", YAT[:, :, tt:tt + 128], PS[6][:, 0:256].rearrange("p (j t) -> p j t", j=2), [bPS[6]], [bYAT])
                    xsb, bxsb = XSB[c % 2], bXSB[c % 2]
                    TT("dve", v4(xsb[:]), v4(XTK[:, c, :]), bc4(WE[:, c, 0:4]), ALU.mult, [bTK, bSM], [bxsb])
                    for g in range(2):
                        MM(PS[5][:, g * 128:(g + 1) * 128], BTK[:, c, g * 128:(g + 1) * 128], xsb[:, g * 128:(g + 1) * 128], True, True,
                           [bTK, bxsb], [bPS[5]], inc=(g == 1))
                    TT("pool", v4(Hs[:]), v4(Hs[:]), bc4(EX[:, c, 32:36]), ALU.mult, [bHs, bSM], [bHs])
                    TT("dve", Hs[:], Hs[:], PS[5][:, 0:256], ALU.add, [bHs, bPS[5]], [bHs])
                DMA(YT_d[0:2].rearrange("j p t -> p j t"), YAT[:], [bYAT], ())
            S.barrier()


        def phase_D(l, need_ctx):
            with ExitStack() as st:
                QT = [[sb(st, "QT%d%d" % (ty, k), [128, T], BF16) for k in range(2)] for ty in range(2)]
                KT = [[sb(st, "KT%d%d" % (ty, k), [128, T], BF16) for k in range(2)] for ty in range(2)]
                bQK = Buf("QK")
                VA = [[sb(st, "VA%d%d" % (ty, k), [128, NCH, 128], BF16) for k in range(2)] for ty in range(2)]
                bVA = Buf("VA")
                QG = sb(st, "QG", [128, 4])
                ESK = sb(st, "ESK", [128, 4])
                bQG = Buf("QG")
                TS("dve", QG[:, 0:1], PP[:, 116:117], 0.125, None, ALU.mult, None, [bPP], [bQG])
                TS("dve", QG[:, 2:3], PP[:, 118:119], 0.125, None, ALU.mult, None, [bPP], [bQG])
                CP("dve", QG[:, 1:2], PP[:, 117:118], [bPP], [bQG])
                CP("dve", QG[:, 3:4], PP[:, 119:120], [bPP], [bQG])
                ACT(ESK[:], RP[:, 276:280], AF.Exp, [bRP], [bQG])
                pi = 0
                with ExitStack() as st2:
                    COS = sb(st2, "COS", [128, LL])
                    SIN = sb(st2, "SIN", [128, LL])
                    bROPE = Buf("ROPE")
                    DMA(COS[:], rope_d[0], (), [bROPE])
                    DMA(SIN[:], rope_d[1], (), [bROPE])
                    XR = [sb(st2, "dXR%d" % i, [128, T]) for i in range(2)]
                    bXR = [Buf("dXR%d" % i) for i in range(2)]
                    SQ = sb(st2, "dSQ", [128, 512], BF16)
                    bSQ = Buf("dSQ")
                    RSq = sb(st2, "dRS", [128, 512])
                    bRSq = Buf("dRS")
                    QN = sb(st2, "dQN", [128, 512])
                    bQN = Buf("dQN")
                    TA = sb(st2, "dTA", [128, 512])
                    TB = sb(st2, "dTB", [128, 512])
                    bTA, bTB = Buf("dTA"), Buf("dTB")
                    VS = sb(st2, "dVS", [128, NCH, 64])
                    bVS = Buf("dVS")
                    it = 0
                    for ty in range(2):
                        base = I_CQ if ty == 0 else I_DQ
                        for idx in range(4):
                            dest = (QT[ty][idx] if idx < 2 else KT[ty][idx - 2])
                            gcol = ty * 2 + (0 if idx < 2 else 1)
                            xr, bxr = XR[it % 2], bXR[it % 2]
                            it += 1
                            DMA(xr[:], PTf_d[base + idx], (), [bxr])
                            for (t0, n, which) in GROUPS:
                                ACT(SQ[:, 0:n], xr[:, t0:t0 + n], AF.Square, [bxr], [bSQ])
                                p, bp = PS[pi % 8], bPS[pi % 8]
                                pi += 1
                                MM(p[:, 0:n], CSTb[:, C_BD, :], SQ[:, 0:n], True, True, [bCSTb, bSQ], [bp])
                                ACT(RSq[:, 0:n], p[:, 0:n], AF.Sqrt, [bp], [bRSq], bias=EPS, scale=1.0 / 64)
                                RECIP(RSq[:, 0:n], RSq[:, 0:n], [bRSq], [bRSq])
                                if which:
                                    STT(dest[:, t0:t0 + n], xr[:, t0:t0 + n], QG[:, gcol:gcol + 1], RSq[:, 0:n], ALU.mult, ALU.mult,
                                        [bxr, bQG, bRSq], [bQK])
                                else:
                                    lt0 = t0 - LC
                                    STT(QN[:, 0:n], xr[:, t0:t0 + n], QG[:, gcol:gcol + 1], RSq[:, 0:n], ALU.mult, ALU.mult,
                                        [bxr, bQG, bRSq], [bQN])
                                    p2, bp2 = PS[pi % 8], bPS[pi % 8]
                                    pi += 1
                                    MM(p2[:, 0:n], CST[:, C_PERM, :], QN[:, 0:n], True, True, [bCST, bQN], [bp2])
                                    TT("pool", TA[:, 0:n], QN[:, 0:n], COS[:, lt0:lt0 + n], ALU.mult, [bQN, bROPE], [bTA])
                                    TT("dve", TB[:, 0:n], p2[:, 0:n], SIN[:, lt0:lt0 + n], ALU.mult, [bp2, bROPE], [bTB])
                                    TT("pool", dest[:, t0:t0 + n], TA[:, 0:n], TB[:, 0:n], ALU.add, [bTA, bTB], [bQK])
                        for hk in range(2):
                            col0 = 264 + ty * 128 + hk * 64
                            src = PTt_d[:, col0:col0 + 64].rearrange("(c p) f -> p c f", p=128)
                            DMA(VS[:, 0:17, :], src[:, 0:17, :], (), [bVS])
                            DMA(VS[:, 17:34, :], src[:, 17:34, :], (), [bVS])
                            CP("dve", VA[ty][hk][:, :, 0:64], VS[:], [bVS], [bVA])
                            MEMSET("pool", VA[ty][hk][:, :, 64:128], 1.0, [bVA])
                S.barrier()
                YO = [sb(st, "YO%d" % ty, [128, 2, T], BF16) for ty in range(2)]
                bYO = [Buf("YO0"), Buf("YO1")]
                PTs = [sb(st, "PTs%d" % i, [128, 640], BF16) for i in range(3)]
                bPTs = [Buf("PTs%d" % i) for i in range(3)]
                RD = sb(st, "RD", [128, 512])
                bRD = Buf("RD")
                LA = 2

                def run_pipe(steps):
                    n_ = len(steps)
                    for i in range(n_ + LA):
                        if i < n_:
                            steps[i][0]()
                        if i >= LA:
                            steps[i - LA][1]()

                oi = [0]
                qgroups = ([(0, LC, [0, 1])] if need_ctx else []) + [(t0, n, list(range(NCH))) for (t0, n, w_) in GROUPS[1:]]
                steps = []
                for hk in range(2):
                    for g in range(2):
                        rows = slice(g * 64, (g + 1) * 64)
                        for (q0, nq, keys) in qgroups:
                            O, bO = PS[6 + oi[0] % 2], bPS[6 + oi[0] % 2]
                            oi[0] += 1
                            for ki, kc in enumerate(keys):
                                si = len(steps)
                                Sp, bSp = PS[si % 3], bPS[si % 3]
                                pt, bpt = PTs[si % 3], bPTs[si % 3]

                                def front(Sp=Sp, bSp=bSp, pt=pt, bpt=bpt, hk=hk, rows=rows, kc=kc, q0=q0, nq=nq):
                                    MM(Sp[:, 0:nq], KT[0][hk][rows, kc * 128:(kc + 1) * 128], QT[0][hk][rows, q0:q0 + nq], True, True, [bQK], [bSp])
                                    ACT(pt[:, 0:nq], Sp[:, 0:nq], AF.Exp, [bSp], [bpt])

                                def back(O=O, bO=bO, pt=pt, bpt=bpt, hk=hk, rows=rows, kc=kc, q0=q0, nq=nq, ki=ki, nk=len(keys)):
                                    MM(O[:, 0:nq], VA[0][hk][:, kc, :], pt[:, 0:nq], ki == 0, ki == nk - 1, [bVA, bpt], [bO], inc=(ki == nk - 1))
                                    if ki == nk - 1:
                                        RECIP(RD[64:128, 0:nq], O[64:128, 0:nq], [bO], [bRD])
                                        TT("dve", YO[0][rows, hk, q0:q0 + nq], O[0:64, 0:nq], RD[64:128, 0:nq], ALU.mult, [bO, bRD], [bYO[0]])

                                steps.append((front, back))
                run_pipe(steps)
                DMA(YT_d[4:6].rearrange("j p t -> p j t"), YO[0][:], [bYO[0]], ())

                def norm_d(O, bO, rows, hk, h, q0, nq):
                    TS("dve", RD[64:128, 0:nq], O[64:128, 0:nq], ESK[64:128, h:h + 1], None, ALU.add, None, [bO, bQG], [bRD])
                    RECIP(RD[64:128, 0:nq], RD[64:128, 0:nq], [bRD], [bRD])
                    TT("dve", YO[1][rows, hk, q0:q0 + nq], O[0:64, 0:nq], RD[64:128, 0:nq], ALU.mult, [bO, bRD], [bYO[1]])

                steps = []
                for hk in range(2):
                    for g in range(2):
                        h = 2 * hk + g
                        rows = slice(g * 64, (g + 1) * 64)
                        if need_ctx:
                            O, bO = PS[6 + oi[0] % 2], bPS[6 + oi[0] % 2]
                            oi[0] += 1
                            si = len(steps)
                            SA, bSA = PS[(2 * si) % 6], bPS[(2 * si) % 6]
                            SB, bSB = PS[(2 * si + 1) % 6], bPS[(2 * si + 1) % 6]
                            pt, bpt = PTs[si % 3], bPTs[si % 3]

                            def front(SA=SA, bSA=bSA, SB=SB, bSB=bSB, pt=pt, bpt=bpt, hk=hk, rows=rows):
                                for kc, (Sp, bSp) in enumerate(((SA, bSA), (SB, bSB))):
                                    MM(Sp[:, 0:LC], KT[1][hk][rows, kc * 128:(kc + 1) * 128], QT[1][hk][rows, 0:LC], True, True, [bQK], [bSp])
                                    ACT(pt[:, kc * 256:(kc + 1) * 256], Sp[:, 0:LC], AF.Exp, [bSp], [bpt])

                            def back(O=O, bO=bO, pt=pt, bpt=bpt, hk=hk, rows=rows, h=h):
                                for kc in range(2):
                                    MM(O[:, 0:LC], VA[1][hk][:, kc, :], pt[:, kc * 256:(kc + 1) * 256], kc == 0, kc == 1, [bVA, bpt], [bO], inc=(kc == 1))
                                norm_d(O, bO, rows, hk, h, 0, LC)

                            steps.append((front, back))
                        for (t0, n, w_) in GROUPS[1:]:
                            O, bO = PS[6 + oi[0] % 2], bPS[6 + oi[0] % 2]
                            oi[0] += 1
                            for blk in range(4):
                                nb = (t0 - LC) // 128 + blk
                                qb = t0 + blk * 128
                                lat = []
                                if nb > 0:
                                    lat.append((2 + nb - 1, C_NEGB))
                                lat.append((2 + nb, None))
                                if nb < 31:
                                    lat.append((2 + nb + 1, C_NEGF))
                                si = len(steps)
                                SA, bSA = PS[(2 * si) % 6], bPS[(2 * si) % 6]
                                SB, bSB = PS[(2 * si + 1) % 6], bPS[(2 * si + 1) % 6]
                                pt, bpt = PTs[si % 3], bPTs[si % 3]

                                def front(SA=SA, bSA=bSA, SB=SB, bSB=bSB, pt=pt, bpt=bpt, hk=hk, rows=rows, lat=lat, qb=qb):
                                    for ii, (kc, mk) in enumerate(lat):
                                        last = (ii == len(lat) - 1)
                                        MM(SA[:, ii * 128:(ii + 1) * 128], KT[1][hk][rows, kc * 128:(kc + 1) * 128], QT[1][hk][rows, qb:qb + 128],
                                           True, mk is None, [bQK], [bSA], inc=(last and mk is None))
                                        if mk is not None:
                                            MM(SA[:, ii * 128:(ii + 1) * 128], identb, CSTb[:, mk, :], False, True, [bCSTb], [bSA], inc=last)
                                    for kc in range(2):
                                        MM(SB[:, kc * 128:(kc + 1) * 128], KT[1][hk][rows, kc * 128:(kc + 1) * 128], QT[1][hk][rows, qb:qb + 128],
                                           True, True, [bQK], [bSB], inc=(kc == 1))
                                    nl = len(lat)
                                    ACT(pt[:, 0:nl * 128], SA[:, 0:nl * 128], AF.Exp, [bSA], [bpt])
                                    ACT(pt[:, 384:640], SB[:, 0:256], AF.Exp, [bSB], [bpt])

                                def back(O=O, bO=bO, pt=pt, bpt=bpt, hk=hk, rows=rows, lat=lat, blk=blk, h=h, t0=t0, n=n):
                                    kvs = [(kc, ii * 128) for ii, (kc, mk) in enumerate(lat)] + [(0, 384), (1, 512)]
                                    for ii, (kc, off) in enumerate(kvs):
                                        MM(O[:, blk * 128:(blk + 1) * 128], VA[1][hk][:, kc, :], pt[:, off:off + 128], ii == 0, ii == len(kvs) - 1,
                                           [bVA, bpt], [bO], inc=(ii == len(kvs) - 1))
                                    if blk == 3:
                                        norm_d(O, bO, rows, hk, h, t0, n)

                                steps.append((front, back))
                run_pipe(steps)
                DMA(YT_d[6:8].rearrange("j p t -> p j t"), YO[1][:], [bYO[1]], ())
            S.barrier()

        def phase_E(l, need_ctx):
            with ExitStack() as st:
                WO = sb(st, "WO", [128, 8, 1024], BF16)
                W1 = sb(st, "W1", [128, 8, 4096], BF16)
                W2 = sb(st, "W2", [128, 32, 1024], BF16)
                bWO, bW1, bW2 = Buf("WO"), Buf("W1"), Buf("W2")
                DMA(WO[:], wout_d[l].rearrange("(j p) c -> p j c", p=128), (), [bWO], q="pool")
                for q4 in range(4):
                    DMA(W1[:, :, q4 * 1024:(q4 + 1) * 1024], w1_d[l, :, q4 * 1024:(q4 + 1) * 1024].rearrange("(j p) c -> p j c", p=128), (), [bW1], q="pool")
                for q4 in range(4):
                    DMA(W2[:, q4 * 8:(q4 + 1) * 8, :], w2_d[l, q4 * 1024:(q4 + 1) * 1024, :].rearrange("(k p) c -> p k c", p=128), (), [bW2], q="pool")
                NE = 256
                _xt = sb(st, "eXT", [128, 8, NE])
                _bxt = Buf("eXT")
                XT = [_xt, _xt]
                bXT = [_bxt, _bxt]
                _yt = sb(st, "eYT", [128, 8, NE], BF16)
                _byt = Buf("eYT")
                YTs = [_yt, _yt]
                bYTs = [_byt, _byt]
                X1 = [sb(st, "eX1%d" % i, [128, 8, NE]) for i in range(2)]
                bX1 = [Buf("eX1%d" % i) for i in range(2)]
                SQ = sb(st, "eSQ", [128, 8, NE], BF16)
                bSQ = Buf("eSQ")
                RS = sb(st, "eRS", [128, NE])
                bRS = Buf("eRS")
                H2 = [sb(st, "eH2%d" % i, [128, 8, NE], BF16) for i in range(2)]
                bH2 = [Buf("eH2%d" % i) for i in range(2)]
                RL = [sb(st, "eRL%d" % i, [128, NE]) for i in range(2)]
                bRL = [Buf("eRL%d" % i) for i in range(2)]
                AK = [sb(st, "eAK%d" % i, [128, NE], BF16) for i in range(3)]
                bAK = [Buf("eAK%d" % i) for i in range(3)]
                starts = list(range(0 if need_ctx else LC, T, NE))

                def load(gi):
                    t0 = starts[gi]
                    DMA(XT[gi % 2][:], xT_d[:, :, t0:t0 + NE].rearrange("j p t -> p j t"), (), [bXT[gi % 2]])
                    DMA(YTs[gi % 2][:], YT_d[:, :, t0:t0 + NE].rearrange("j p t -> p j t"), (), [bYTs[gi % 2]])

                def head(gi):
                    t0 = starts[gi]
                    w = 1 if t0 < LC else 0
                    xt, bxt = XT[gi % 2], bXT[gi % 2]
                    yt, byt = YTs[gi % 2], bYTs[gi % 2]
                    x1, bx1 = X1[gi % 2], bX1[gi % 2]
                    h2, bh2 = H2[gi % 2], bH2[gi % 2]
                    p, bp = PS[3], bPS[3]
                    for fo in range(8):
                        for k in range(8):
                            MM(p[:, (fo % 2) * NE:(fo % 2 + 1) * NE], WO[:, k, fo * 128:(fo + 1) * 128], yt[:, k, :], k == 0, k == 7, [bWO, byt], [bp], inc=(k == 7))
                        STT(x1[:, fo, :], p[:, (fo % 2) * NE:(fo % 2 + 1) * NE], MODS[:, 16 + fo, w:w + 1], xt[:, fo, :], ALU.mult, ALU.add, [bp, bMODS, bxt], [bx1])
                    ACT(SQ[:], x1[:], AF.Square, [bx1], [bSQ])
                    for j in range(8):
                        MM(p[:, 0:NE], CSTb[:, C_ONE, :], SQ[:, j, :], j == 0, j == 7, [bSQ, bCSTb], [bp], inc=(j == 7))
                    ACT(RS[:], p[:, 0:NE], AF.Sqrt, [bp], [bRS], bias=EPS, scale=1.0 / D)
                    RECIP(RS[:], RS[:], [bRS], [bRS])
                    TT("dve", xt[:], x1[:], RS[:].unsqueeze(1).to_broadcast([128, 8, NE]), ALU.mult, [bx1, bRS], [bxt])
                    for j in range(8):
                        ACT(h2[:, j, :], xt[:, j, :], AF.Identity, [bxt, bG, bMODS], [bh2],
                            bias=MODS[:, 24 + j, w:w + 1], scale=G2[:, j, w:w + 1])

                steps = []
                for gi, t0 in enumerate(starts):
                    w = 1 if t0 < LC else 0
                    for kh in range(32):
                        ki = len(steps)
                        pu, bpu = PS[ki % 3], bPS[ki % 3]
                        rl, brl = RL[ki % 2], bRL[ki % 2]
                        ak, bak = AK[ki % 3], bAK[ki % 3]

                        def front(gi=gi, kh=kh, pu=pu, bpu=bpu, rl=rl, brl=brl, ak=ak, bak=bak):
                            h2, bh2 = H2[gi % 2], bH2[gi % 2]
                            for j in range(8):
                                MM(pu[:, 0:NE], W1[:, j, kh * 128:(kh + 1) * 128], h2[:, j, :], j == 0, j == 7, [bW1, bh2], [bpu], inc=(j == 7))
                            ACT(rl[:], pu[:, 0:NE], AF.Relu, [bpu], [brl])
                            TT("dve", ak[:], rl[:], pu[:, 0:NE], ALU.mult, [brl, bpu], [bak])
                            if kh == 12 and gi + 1 < len(starts):
                                load(gi + 1)
                            if kh == 16 and gi + 1 < len(starts):
                                head(gi + 1)

                        def back(gi=gi, kh=kh, ak=ak, bak=bak, t0=t0, w=w):
                            x1, bx1 = X1[gi % 2], bX1[gi % 2]
                            for fo in range(8):
                                pd, bpd = PS[4 + fo // 2], bPS[4 + fo // 2]
                                MM(pd[:, (fo % 2) * NE:(fo % 2 + 1) * NE], W2[:, kh, fo * 128:(fo + 1) * 128], ak[:], (kh == 0 and fo % 2 == 0), kh == 31,
                                   [bW2, bak], [bpd], inc=(kh == 31 or fo == 7))
                            if kh == 31:
                                for fo in range(8):
                                    pd, bpd = PS[4 + fo // 2], bPS[4 + fo // 2]
                                    STT(x1[:, fo, :], pd[:, (fo % 2) * NE:(fo % 2 + 1) * NE], MODS[:, 40 + fo, w:w + 1], x1[:, fo, :], ALU.mult, ALU.add,
                                        [bpd, bMODS, bx1], [bx1])
                                DMA(xT_d[:, :, t0:t0 + NE].rearrange("j p t -> p j t"), x1[:], [bx1], ())

                        steps.append((front, back))
                load(0)
                head(0)
                LA = 2
                for i in range(len(steps) + LA):
                    if i < len(steps):
                        steps[i][0]()
                    if i >= LA:
                        steps[i - LA][1]()
            S.barrier()

        def phase_U():
            with ExitStack() as st:
                XT = [sb(st, "uXT%d" % i, [128, 8, 512]) for i in range(2)]
                bXT = [Buf("uXT%d" % i) for i in range(2)]
                OS = [sb(st, "uOS%d" % i, [128, D]) for i in range(2)]
                bOS = [Buf("uOS%d" % i) for i in range(2)]
                it = 0
                for gi, (t0, n, which) in enumerate(GROUPS[1:]):
                    xt, bxt = XT[gi % 2], bXT[gi % 2]
                    DMA(xt[:], xT_d[:, :, t0:t0 + n].rearrange("j p t -> p j t"), (), [bxt])
                    for s in range(4):
                        os_, bos = OS[it % 2], bOS[it % 2]
                        for half in range(2):
                            p, bp = PS[(it * 2 + half) % 8], bPS[(it * 2 + half) % 8]
                            for jj in range(4):
                                j = half * 4 + jj
                                TR(p[:, jj * 128:(jj + 1) * 128], xt[:, j, s * 128:(s + 1) * 128], ident, [bxt, bCST], [bp])
                            CP("act" if half else "dve", os_[:, half * 512:(half + 1) * 512], p[:, 0:512], [bp], [bos])
                        it += 1
                        tt = t0 - LC + s * 128
                        final_toks.append(DMA(out_d[tt:tt + 128, :], os_[:], [bos], ()))
            S.barrier()

        phase_T()
        done = False
        for l in range(n_layers):
            need_ctx = l < n_layers - 1
            for nm, fn in (("0", lambda: phase_0(l)), ("A", lambda: phase_A(l)), ("B", lambda: phase_B(l)), ("C", lambda: phase_C(l)),
                           ("D", lambda: phase_D(l, need_ctx)), ("E", lambda: phase_E(l, need_ctx))):
                if stop_after is not None and isinstance(stop_after[0], tuple):
                    if (nm, l) not in stop_after:
                        continue
                fn()
                if stop_after == (nm, l):
                    done = True
                    break
            if done:
                break
        if stop_after is None:
            phase_U()
        else:
            with ExitStack() as st:
                Z = sb(st, "ZZ", [128, D])
                bZ = Buf("ZZ")
                MEMSET("dve", Z[:], 0.0, [bZ])
                final_toks.append(DMA(out_d[0:128, :], Z[:], [bZ], ()))
        S.barrier()
        with nc.Block() as block:
            S.emit(block)
    return nc, S


def _consts():
    p = np.arange(128)[:, None]
    f = np.arange(128)[None, :]
    cst = np.zeros((NCST, 128, 128), np.float32)
    cst[C_ID] = (p == f)
    cst[C_LT] = (p > f)
    cst[C_UT] = (p < f)
    cst[C_LE] = (p <= f)
    cst[C_GE] = (p >= f)
    cst[C_ONE] = 1.0
    cst[C_BD] = (p // 64 == f // 64)
    cst[C_PERM] = (p == (f ^ 16))
    cst[C_NEGF] = NEG * (p > f)
    cst[C_NEGB] = NEG * (p < f)
    return cst


def _rope():
    t = np.arange(LL)
    row = (t // 64).astype(np.float32)
    col = (t % 64).astype(np.float32)
    inv = (np.float32(10000.0) ** (-np.arange(0, 32, 2, dtype=np.float32) / np.float32(32))).astype(np.float32)
    tab = np.zeros((2, 128, LL), np.float32)
    for d in range(128):
        dd = d % 64
        axis, half, fi = dd // 32, (dd % 32) // 16, dd % 16
        ang = ((row if axis == 0 else col) * inv[fi]).astype(np.float32)
        tab[0, d] = np.cos(ang)
        tab[1, d] = (-np.sin(ang)) if half == 0 else np.sin(ang)
    return tab


def _pack(inputs):
    f = lambda a: np.ascontiguousarray(np.asarray(a, dtype=np.float32))
    pp = np.zeros((2, 128, NPP), np.float32)
    rp = np.zeros((2, NRP), np.float32)
    lruw = np.zeros((2, 2, 2, 2, 128, 128), np.float32)
    for l in range(2):
        pp[l, :, 0:8] = f(inputs["g_mix"])[l].reshape(8, 128).T
        pp[l, :, 8:16] = f(inputs["g_ffn"])[l].reshape(8, 128).T
        pp[l, :, 16:64] = f(inputs["b_mod"])[l].reshape(48, 128).T
        cw = f(inputs["ssd_conv_w"])[l]
        for ci in range(6):
            pp[l, :, 64 + ci * 4:64 + ci * 4 + 4] = cw[:, ci * 128:(ci + 1) * 128].T
        pp[l, :, 88:94] = f(inputs["ssd_conv_b"])[l].reshape(6, 128).T
        lw = f(inputs["lru_conv_w"])[l]
        for c2 in range(2):
            pp[l, :, 94 + c2 * 4:94 + c2 * 4 + 4] = lw[:, c2 * 128:(c2 + 1) * 128].T
        pp[l, :, 102:104] = f(inputs["lru_conv_b"])[l].reshape(2, 128).T
        pp[l, :, 104:108] = f(inputs["lru_lambda"])[l].reshape(4, 128).T
        pp[l, :, 108:112] = f(inputs["lru_b_a"])[l].reshape(4, 128).T
        pp[l, :, 112:116] = f(inputs["lru_b_i"])[l].reshape(4, 128).T
        for k, nm in enumerate(("gqa_q_norm", "gqa_k_norm", "swa_q_norm", "swa_k_norm")):
            pp[l, :, 116 + k] = np.tile(f(inputs[nm])[l], 2)
        rp[l, 0:256] = f(inputs["ssd_norm_g"])[l]
        rp[l, 256:260] = f(inputs["ssd_d"])[l]
        rp[l, 260:268] = f(inputs["ssd_dt_bias"])[l].reshape(8)
        rp[l, 268:276] = f(inputs["ssd_a_log"])[l].reshape(8)
        rp[l, 276:280] = f(inputs["swa_sink"])[l]
        for dr in range(2):
            for gt, nm in enumerate(("lru_w_a", "lru_w_i")):
                w = f(inputs[nm])[l, dr]
                for c2 in range(2):
                    for bl in range(2):
                        lruw[l, dr, gt, c2, bl * 64:(bl + 1) * 64, bl * 64:(bl + 1) * 64] = w[c2 * 2 + bl]
    return pp, rp, lruw


def make_in_maps(inputs, cores):
    f = lambda a: np.ascontiguousarray(np.asarray(a, dtype=np.float32))
    pp, rp, lruw = _pack(inputs)
    cst = _consts()
    rope = _rope()
    shared = {"w_mod": f(inputs["w_mod"]), "w_in": f(inputs["w_in"]), "w_out": f(inputs["w_out"]),
              "w_ffn1": f(inputs["w_ffn1"]), "w_ffn2": f(inputs["w_ffn2"]), "pp": pp, "rp": rp, "lruw": lruw,
              "cst": cst, "rope": rope}
    maps = []
    for b in cores:
        c2 = np.stack([f(inputs["c"])[b], f(inputs["c_ctx"])], axis=0)
        c2 = np.ascontiguousarray(c2.reshape(2, 8, 128).transpose(2, 1, 0))
        m = dict(shared)
        m["x"] = f(inputs["x"])[b]
        m["ctx"] = f(inputs["ctx"])[b]
        m["c2"] = c2
        maps.append(m)
    return maps


_NC_CACHE = {}


def kernel(**inputs):
    if "nc" not in _NC_CACHE:
        _NC_CACHE["nc"] = build()[0]
    nc = _NC_CACHE["nc"]
    maps = make_in_maps(inputs, range(4))
    res = run_bass_kernel_spmd(nc, maps, core_ids=list(range(4)))
    return np.stack([np.asarray(r["out"], dtype=np.float32) for r in res.results], axis=0)
```

```python
import numpy as np
import concourse.bass as bass
import concourse.mybir as mybir
from concourse.bass_utils import run_bass_kernel_spmd
from contextlib import ExitStack

F32 = mybir.dt.float32
BF16 = mybir.dt.bfloat16
ALU = mybir.AluOpType
AF = mybir.ActivationFunctionType
AX = mybir.AxisListType

T = 4352
LC = 256
LL = 4096
D = 1024
NCH = 34
EPS = 1e-6
NEG = -30000.0
GROUPS = [(0, 256, 1)] + [(256 + i * 512, 512, 0) for i in range(8)]

FM = [(256, 128, 0), (384, 128, 0), (512, 128, 0), (640, 128, 0), (768, 128, 0), (896, 128, 0),
      (1032, 128, 0), (1160, 128, 0), (1288, 128, 0), (1416, 128, 0),
      (1544, 128, 0), (1672, 128, 0), (1800, 64, 1), (1864, 64, 1),
      (2056, 128, 0), (2184, 128, 0), (2312, 64, 1), (2376, 64, 1)]
I_X, I_B, I_C, I_G, I_R, I_CQ, I_CK, I_DQ, I_DK = 0, 2, 4, 6, 8, 10, 12, 14, 16
TMC = [(0, 256), (1024, 8), (1928, 128), (2440, 128)]
NTM = 520
NPP = 120
NRP = 280
(C_ID, C_LT, C_UT, C_LE, C_GE, C_ONE, C_BD, C_PERM, C_NEGF, C_NEGB) = range(10)
NCST = 10


class Buf:
    __slots__ = ("name", "w", "r")

    def __init__(self, name):
        self.name = name
        self.w = None
        self.r = {}


class Sched:
    COMPUTE = ("pe", "act", "dve", "pool")
    ALL = ("pe", "act", "dve", "pool", "sp")

    def __init__(self, nc, es, n_dma_sems=32, same_engine_sync=True):
        self.nc = nc
        self.streams = {e: [] for e in self.ALL}
        self.sems = {}
        for e in self.COMPUTE:
            self.sems[e] = es.enter_context(nc.semaphore("s_" + e))
        self.cnt = {e: 0 for e in self.COMPUTE}
        self.pending = {e: False for e in self.COMPUTE}
        self.dsem = [es.enter_context(nc.semaphore("d%d" % i)) for i in range(n_dma_sems)]
        self.dcnt = [0] * n_dma_sems
        self.dnext = 0
        self.waited = {e: {} for e in self.ALL}
        self.same_engine_sync = same_engine_sync
        self.nwaits = 0

    def _semof(self, k):
        if isinstance(k, tuple):
            return self.dsem[k[1]]
        return self.sems[k]

    def _deps(self, eng, reads, writes, extra=()):
        deps = {}

        def add(tok):
            if tok is None:
                return
            k, v = tok
            if deps.get(k, 0) < v:
                deps[k] = v

        for b in reads:
            add(b.w)
        for b in writes:
            add(b.w)
            for k, v in b.r.items():
                add((k, v))
        for t in extra:
            add(t)
        out = []
        for k, v in deps.items():
            if k == eng:
                if eng == "pe" or not self.same_engine_sync:
                    continue
                if v > self.cnt[eng]:
                    continue
            if self.waited[eng].get(k, 0) >= v:
                continue
            self.waited[eng][k] = v
            out.append((k, v))
        self.nwaits += len(out)
        return out

    def op(self, eng, fn, reads=(), writes=(), inc=True):
        waits = self._deps(eng, reads, writes)
        if inc:
            self.cnt[eng] += 1
            tok = (eng, self.cnt[eng])
            self.pending[eng] = False
        else:
            tok = (eng, self.cnt[eng] + 1)
            self.pending[eng] = True
        self.streams[eng].append((waits, fn, tok, 1 if inc else 0))
        for b in writes:
            b.w = tok
            b.r = {}
        for b in reads:
            if b.w is not tok:
                if b.r.get(eng, 0) < tok[1]:
                    b.r[eng] = tok[1]
        return tok

    def dma(self, fn, reads=(), writes=(), q="sp"):
        j = self.dnext
        self.dnext = (self.dnext + 1) % len(self.dsem)
        prev = (("d", j), 16 * self.dcnt[j])
        waits = self._deps(q, reads, writes, extra=(prev,) if self.dcnt[j] else ())
        self.dcnt[j] += 1
        tok = (("d", j), 16 * self.dcnt[j])
        self.streams[q].append((waits, fn, tok, 16))
        for b in writes:
            b.w = tok
            b.r = {}
        for b in reads:
            if b.w is not tok:
                b.r[tok[0]] = tok[1]
        return tok

    def wait_all(self, eng, toks):
        waits = self._deps(eng, (), (), extra=toks)
        if waits:
            self.streams[eng].append((waits, None, None, 0))

    def barrier(self):
        for e in self.COMPUTE:
            assert not self.pending[e], "pending un-incremented op on " + e
        toks = [(e, self.cnt[e]) for e in self.COMPUTE if self.cnt[e]]
        toks += [(("d", j), 16 * self.dcnt[j]) for j in range(len(self.dsem)) if self.dcnt[j]]
        for e in self.ALL:
            self.wait_all(e, toks)

    def emit(self, block):
        def mk(ename):
            def body(eng):
                for waits, fn, tok, inc in self.streams[ename]:
                    for k, v in waits:
                        eng.wait_ge(self._semof(k), v)
                    if fn is None:
                        continue
                    inst = fn(eng)
                    if inc:
                        inst.then_inc(self._semof(tok[0]), inc)
            return body

        block.tensor(mk("pe"))
        block.scalar(mk("act"))
        block.vector(mk("dve"))
        block.gpsimd(mk("pool"))
        block.sync(mk("sp"))


def rev_ap(ap):
    dims = [list(d) for d in ap.ap]
    fs, fc = dims[-1]
    dims[-1] = [-fs, fc]
    return bass.AP(ap.tensor, ap.offset + (fc - 1) * fs, dims)


def build(dbg=(), stop_after=None, n_layers=2):
    nc = bass.Bass("TRN2", target_bir_lowering=False)

    def din(name, shape, dt=F32):
        return nc.dram_tensor(name, list(shape), dt, kind="ExternalInput").ap()

    x_d = din("x", [LL, D])
    ctx_d = din("ctx", [LC, D])
    c2_d = din("c2", [128, 8, 2])
    wmod_d = din("w_mod", [2, D, 6144])
    win_d = din("w_in", [2, D, 2568])
    wout_d = din("w_out", [2, D, D])
    w1_d = din("w_ffn1", [2, D, 4096])
    w2_d = din("w_ffn2", [2, 4096, D])
    pp_d = din("pp", [2, 128, NPP])
    rp_d = din("rp", [2, NRP])
    lruw_d = din("lruw", [2, 2, 2, 2, 128, 128])
    cst_d = din("cst", [NCST, 128, 128])
    rope_d = din("rope", [2, 128, LL])
    out_d = nc.dram_tensor("out", [LL, D], F32, kind="ExternalOutput").ap()

    def scratch(name, shape, dt):
        kind = "ExternalOutput" if name in dbg else "Internal"
        return nc.dram_tensor(name, list(shape), dt, kind=kind).ap()

    xT_d = scratch("xT", [8, 128, T], F32)
    PTf_d = scratch("PTf", [18, 128, T], F32)
    PTt_d = scratch("PTt", [T, NTM], F32)
    YT_d = scratch("YT", [8, 128, T], BF16)
    MOD_d = scratch("MODd", [2, 128, 96], F32) if "MODd" in dbg else None

    es = ExitStack()
    with es:
        S = Sched(nc, es)

        _uid = [0]

        def sb(st, name, shape, dt=F32):
            _uid[0] += 1
            return st.enter_context(nc.sbuf_tensor("%s_%d" % (name, _uid[0]), list(shape), dt))

        def MM(out, lhsT, rhs, start, stop, r, w, inc=True):
            S.op("pe", lambda e: e.matmul(out, lhsT, rhs, start=start, stop=stop), r, w, inc=inc)

        def TR(out, in_, ident, r, w):
            S.op("pe", lambda e: e.transpose(out, in_, ident), r, w)

        def ACT(out, in_, func, r, w, bias=None, scale=None):
            kw = {}
            if bias is not None:
                kw["bias"] = bias
            if scale is not None:
                kw["scale"] = scale
            S.op("act", lambda e: e.activation(out=out, in_=in_, func=func, **kw), r, w)

        def TT(eng, out, in0, in1, op, r, w):
            S.op(eng, lambda e: e.tensor_tensor(out=out, in0=in0, in1=in1, op=op), r, w)

        def TS(eng, out, in0, s1, s2, op0, op1, r, w):
            if s2 is None:
                S.op(eng, lambda e: e.tensor_scalar(out=out, in0=in0, scalar1=s1, scalar2=None, op0=op0), r, w)
            else:
                S.op(eng, lambda e: e.tensor_scalar(out=out, in0=in0, scalar1=s1, scalar2=s2, op0=op0, op1=op1), r, w)

        def STT(out, in0, scalar, in1, op0, op1, r, w):
            S.op("dve", lambda e: e.scalar_tensor_tensor(out=out, in0=in0, scalar=scalar, in1=in1, op0=op0, op1=op1), r, w)

        def CP(eng, out, in_, r, w):
            if eng == "act":
                S.op("act", lambda e: e.activation(out=out, in_=in_, func=AF.Copy), r, w)
            else:
                S.op(eng, lambda e: e.tensor_copy(out=out, in_=in_), r, w)

        def RECIP(out, in_, r, w):
            S.op("dve", lambda e: e.reciprocal(out=out, in_=in_), r, w)

        def MEMSET(eng, ap, val, w):
            S.op(eng, lambda e: e.memset(ap, val), (), w)

        def DMA(out, in_, r, w, q="sp"):
            return S.dma(lambda e: e.dma_start(out=out, in_=in_), r, w, q=q)

        PS = [es.enter_context(nc.psum_tensor("ps%d" % i, [128, 512], F32)) for i in range(8)]
        bPS = [Buf("ps%d" % i) for i in range(8)]
        CST = sb(es, "CST", [128, NCST, 128])
        bCST = Buf("CST")
        DMA(CST[:], cst_d.rearrange("c p f -> p c f"), (), [bCST])
        CSTb = sb(es, "CSTb", [128, NCST, 128], BF16)
        bCSTb = Buf("CSTb")
        CP("dve", CSTb[:], CST[:], [bCST], [bCSTb])
        ident = CST[:, C_ID, :]
        identb = CSTb[:, C_ID, :]
        CS = sb(es, "CS", [128, 8, 2])
        bCS = Buf("CS")
        DMA(CS[:], c2_d, (), [bCS])
        ACT(CS[:], CS[:], AF.Silu, [bCS], [bCS])
        MODS = sb(es, "MODS", [128, 48, 2])
        bMODS = Buf("MODS")
        G1 = sb(es, "G1", [128, 8, 2])
        G2 = sb(es, "G2", [128, 8, 2])
        bG = Buf("G12")
        PP = sb(es, "PP", [128, NPP])
        bPP = Buf("PP")
        RP = sb(es, "RP", [128, NRP])
        bRP = Buf("RP")
        final_toks = []

        def phase_T():
            with ExitStack() as st:
                XK = [sb(st, "XK%d" % i, [128, D]) for i in range(2)]
                bXK = [Buf("XK%d" % i) for i in range(2)]
                XS = [sb(st, "XS%d" % i, [128, 8, 512]) for i in range(2)]
                bXS = [Buf("XSs%d" % i) for i in range(2)]
                it = 0
                for gi, (t0, n, which) in enumerate(GROUPS):
                    xs, bxs = XS[gi % 2], bXS[gi % 2]
                    for s in range(n // 128):
                        xk, bxk = XK[it % 2], bXK[it % 2]
                        tt = t0 + s * 128
                        src = ctx_d[tt:tt + 128, :] if which else x_d[tt - LC:tt - LC + 128, :]
                        DMA(xk[:], src, (), [bxk])
                        for half in range(2):
                            p, bp = PS[(it * 2 + half) % 8], bPS[(it * 2 + half) % 8]
                            for jj in range(4):
                                j = half * 4 + jj
                                TR(p[:, jj * 128:(jj + 1) * 128], xk[:, j * 128:(j + 1) * 128], ident, [bxk, bCST], [bp])
                            CP("act" if half else "dve", xs[:, half * 4:half * 4 + 4, s * 128:(s + 1) * 128],
                               p[:].rearrange("p (j t) -> p j t", j=4), [bp], [bxs])
                        it += 1
                    DMA(xT_d[:, :, t0:t0 + n].rearrange("j p t -> p j t"), xs[:, :, 0:n], [bxs], ())
            S.barrier()

        def phase_0(l):
            with ExitStack() as st:
                DMA(PP[:], pp_d[l], (), [bPP])
                DMA(RP[:], rp_d[l:l + 1, :].partition_broadcast(128) if False else rp_d[l].partition_broadcast(128), (), [bRP])
                WM = [sb(st, "WM%d" % i, [128, 8, 1024]) for i in range(2)]
                bWM = [Buf("WM%d" % i) for i in range(2)]
                pm, bpm = PS[0], bPS[0]
                for sl in range(6):
                    wm, bwm = WM[sl % 2], bWM[sl % 2]
                    DMA(wm[:], wmod_d[l, :, sl * 1024:(sl + 1) * 1024].rearrange("(j p) c -> p j c", p=128), (), [bwm])
                    for oc8 in range(8):
                        oc = sl * 8 + oc8
                        for j in range(8):
                            MM(pm[:, oc * 2:oc * 2 + 2], wm[:, j, oc8 * 128:(oc8 + 1) * 128], CS[:, j, :],
                               j == 0, j == 7, [bwm, bCS], [bpm], inc=(j == 7))
                TT("dve", MODS[:], pm[:, 0:96].rearrange("p (o w) -> p o w", w=2),
                   PP[:, 16:64].unsqueeze(2).to_broadcast([128, 48, 2]), ALU.add, [bpm, bPP], [bMODS])
                for (G, gcol, sccol) in ((G1, 0, 8), (G2, 8, 32)):
                    TS("dve", G[:], MODS[:, sccol:sccol + 8, :], 1.0, None, ALU.add, None, [bMODS], [bG])
                    TT("dve", G[:], G[:], PP[:, gcol:gcol + 8].unsqueeze(2).to_broadcast([128, 8, 2]), ALU.mult, [bG, bPP], [bG])
                if MOD_d is not None:
                    final_toks.append(DMA(MOD_d[l], MODS[:].rearrange("p o w -> p (o w)"), [bMODS], ()))
            S.barrier()

        def phase_A(l):
            with ExitStack() as st:
                WF = sb(st, "WF", [128, 8, 18 * 128], BF16)
                WK = sb(st, "WK", [128, 8, NTM], BF16)
                bW = Buf("Win")
                for ci, (c0, ncol, dup) in enumerate(FM):
                    src = win_d[l, :, c0:c0 + ncol].rearrange("(j p) c -> p j c", p=128)
                    DMA(WF[:, :, ci * 128:ci * 128 + ncol], src, (), [bW], q="pool")
                    if dup:
                        DMA(WF[:, :, ci * 128 + 64:ci * 128 + 128], src, (), [bW], q="pool")
                off = 0
                for (c0, ncol) in TMC:
                    DMA(WK[:, :, off:off + ncol], win_d[l, :, c0:c0 + ncol].rearrange("(j p) c -> p j c", p=128), (), [bW], q="pool")
                    off += ncol
                XT = [sb(st, "XT%d" % i, [128, 8, 512]) for i in range(2)]
                bXT = [Buf("XT%d" % i) for i in range(2)]
                SQ = sb(st, "SQ", [128, 8, 512], BF16)
                bSQ = Buf("SQ")
                RS = sb(st, "RS", [128, 512])
                bRS = Buf("RS")
                HT = [sb(st, "HT%d" % i, [128, 8, 512], BF16) for i in range(2)]
                bHT = [Buf("HT%d" % i) for i in range(2)]
                FS = [sb(st, "FS%d" % i, [128, 512]) for i in range(4)]
                bFS = [Buf("FS%d" % i) for i in range(4)]
                ZS = [sb(st, "ZS%d" % i, [128, NTM]) for i in range(2)]
                bZS = [Buf("ZS%d" % i) for i in range(2)]
                fsi = 0
                zsi = 0
                pi = 0
                for gi, (t0, n, which) in enumerate(GROUPS):
                    xt, bxt = XT[gi % 2], bXT[gi % 2]
                    ht, bht = HT[gi % 2], bHT[gi % 2]
                    DMA(xt[:, :, 0:n], xT_d[:, :, t0:t0 + n].rearrange("j p t -> p j t"), (), [bxt])
                    ACT(SQ[:, :, 0:n], xt[:, :, 0:n], AF.Square, [bxt], [bSQ])
                    pss, bpss = PS[pi % 8], bPS[pi % 8]
                    pi += 1
                    for j in range(8):
                        MM(pss[:, 0:n], CSTb[:, C_ONE, :], SQ[:, j, 0:n], j == 0, j == 7, [bSQ, bCSTb], [bpss], inc=(j == 7))
                    ACT(RS[:, 0:n], pss[:, 0:n], AF.Sqrt, [bpss], [bRS], bias=EPS, scale=1.0 / D)
                    RECIP(RS[:, 0:n], RS[:, 0:n], [bRS], [bRS])
                    TT("dve", xt[:, :, 0:n], xt[:, :, 0:n], RS[:, 0:n].unsqueeze(1).to_broadcast([128, 8, n]), ALU.mult, [bxt, bRS], [bxt])
                    for j in range(8):
                        ACT(ht[:, j, 0:n], xt[:, j, 0:n], AF.Identity, [bxt, bG, bMODS], [bht],
                            bias=MODS[:, j, which:which + 1], scale=G1[:, j, which:which + 1])
                    for ci in range(18):
                        p, bp = PS[pi % 8], bPS[pi % 8]
                        pi += 1
                        for j in range(8):
                            MM(p[:, 0:n], WF[:, j, ci * 128:(ci + 1) * 128], ht[:, j, 0:n], j == 0, j == 7, [bW, bht], [bp], inc=(j == 7))
                        fs, bfs = FS[fsi % 4], bFS[fsi % 4]
                        CP("act" if fsi % 2 else "dve", fs[:, 0:n], p[:, 0:n], [bp], [bfs])
                        fsi += 1
                        DMA(PTf_d[ci, :, t0:t0 + n], fs[:, 0:n], [bfs], ())
                    for s in range(n // 128):
                        p, bp = PS[pi % 8], bPS[pi % 8]
                        p2, bp2 = PS[(pi + 1) % 8], bPS[(pi + 1) % 8]
                        pi += 2
                        for j in range(8):
                            MM(p[:, 0:512], ht[:, j, s * 128:(s + 1) * 128], WK[:, j, 0:512], j == 0, j == 7, [bW, bht], [bp], inc=(j == 7))
                        for j in range(8):
                            MM(p2[:, 0:8], ht[:, j, s * 128:(s + 1) * 128], WK[:, j, 512:520], j == 0, j == 7, [bW, bht], [bp2], inc=(j == 7))
                        zs, bzs = ZS[zsi % 2], bZS[zsi % 2]
                        zsi += 1
                        CP("act", zs[:, 0:512], p[:, 0:512], [bp], [bzs])
                        CP("dve", zs[:, 512:520], p2[:, 0:8], [bp2], [bzs])
                        tt = t0 + s * 128
                        DMA(PTt_d[tt:tt + 128, :], zs[:], [bzs], ())
            S.barrier()


        def conv_seg(out, inp, wc, bc, s0, L, r, w):
            S.op("dve", lambda e: e.tensor_scalar(out=out[:, s0:s0 + L], in0=inp[:, s0:s0 + L], scalar1=PP[:, wc + 2:wc + 3],
                                                   scalar2=PP[:, bc:bc + 1], op0=ALU.mult, op1=ALU.add), r, w)
            STT(out[:, s0 + 2:s0 + L], inp[:, s0:s0 + L - 2], PP[:, wc:wc + 1], out[:, s0 + 2:s0 + L], ALU.mult, ALU.add, r + w, w)
            STT(out[:, s0 + 1:s0 + L], inp[:, s0:s0 + L - 1], PP[:, wc + 1:wc + 2], out[:, s0 + 1:s0 + L], ALU.mult, ALU.add, r + w, w)
            STT(out[:, s0:s0 + L - 1], inp[:, s0 + 1:s0 + L], PP[:, wc + 3:wc + 4], out[:, s0:s0 + L - 1], ALU.mult, ALU.add, r + w, w)

        def conv_full(out, inp, wc, bc, r, w):
            conv_seg(out, inp, wc, bc, 0, LC, r, w)
            conv_seg(out, inp, wc, bc, LC, LL, r, w)

        def phase_B(l):
            with ExitStack() as st:
                LW = sb(st, "LW", [128, 8, 128])
                bLW = Buf("LW")
                DMA(LW[:], lruw_d[l].rearrange("d g c p f -> p (d g c) f"), (), [bLW])
                SC = sb(st, "SCl", [128, 8])
                bSC = Buf("SCl")
                ACT(SC[:, 0:4], PP[:, 104:108], AF.Exp, [bPP], [bSC], scale=-1.0)
                ACT(SC[:, 0:4], SC[:, 0:4], AF.Ln, [bSC], [bSC], bias=1.0)
                TS("dve", SC[:, 4:8], SC[:, 0:4], -16.0, None, ALU.mult, None, [bSC], [bSC])
                TS("dve", SC[:, 0:4], SC[:, 0:4], -8.0, None, ALU.mult, None, [bSC], [bSC])
                names = ["XR", "XC", "Rt", "IGt", "Mt", "HF", "HB"]
                tl = {n_: sb(st, "lru_" + n_, [128, T]) for n_ in names}
                bf = {n_: Buf("lru_" + n_) for n_ in names}
                YB = sb(st, "lru_YB", [128, T], BF16)
                bYB = Buf("lru_YB")
                pi = 0
                for c2 in range(2):
                    DMA(tl["XR"][:], PTf_d[I_R + c2], (), [bf["XR"]])
                    conv_full(tl["XC"], tl["XR"], 94 + c2 * 4, 102 + c2, [bf["XR"], bPP], [bf["XC"]])
                    DMA(tl["XR"][:], PTf_d[I_G + c2], (), [bf["XR"]])
                    for dr in range(2):
                        col = dr * 2 + c2
                        for (t0, n, which) in GROUPS:
                            for gt, dst, bcol in ((0, "Rt", 108), (1, "IGt", 112)):
                                p, bp = PS[pi % 8], bPS[pi % 8]
                                pi += 1
                                MM(p[:, 0:n], LW[:, dr * 4 + gt * 2 + c2, :], tl["XC"][:, t0:t0 + n], True, True, [bLW, bf["XC"]], [bp])
                                ACT(tl[dst][:, t0:t0 + n], p[:, 0:n], AF.Sigmoid, [bp, bPP], [bf[dst]], bias=PP[:, bcol + col:bcol + col + 1])
                        ACT(tl["Mt"][:], tl["Rt"][:], AF.Exp, [bf["Rt"], bSC], [bf["Mt"]], scale=SC[:, 4 + col:5 + col])
                        ACT(tl["Rt"][:], tl["Rt"][:], AF.Exp, [bf["Rt"], bSC], [bf["Rt"]], scale=SC[:, col:col + 1])
                        ACT(tl["Mt"][:], tl["Mt"][:], AF.Sqrt, [bf["Mt"]], [bf["Mt"]], bias=1.0, scale=-1.0)
                        TT("pool", tl["IGt"][:], tl["IGt"][:], tl["Mt"][:], ALU.mult, [bf["IGt"], bf["Mt"]], [bf["IGt"]])
                        TT("pool", tl["IGt"][:], tl["IGt"][:], tl["XC"][:], ALU.mult, [bf["IGt"], bf["XC"]], [bf["IGt"]])
                        H = tl["HF" if dr == 0 else "HB"]
                        bH = bf["HF" if dr == 0 else "HB"]
                        A_, B_ = tl["Rt"], tl["IGt"]
                        rw = ([bf["Rt"], bf["IGt"], bH], [bH])

                        def scan(s0, L, init, reverse):
                            o, a, b = H[:, s0:s0 + L], A_[:, s0:s0 + L], B_[:, s0:s0 + L]
                            if reverse:
                                o, a, b = rev_ap(o), rev_ap(a), rev_ap(b)
                            S.op("dve", lambda e: e.tensor_tensor_scan(out=o, data0=a, data1=b, initial=init, op0=ALU.mult, op1=ALU.add), rw[0], rw[1])

                        PIECE = 1024
                        if dr == 0:
                            scan(0, LC, 0.0, False)
                            for s0 in range(LC, T, PIECE):
                                scan(s0, PIECE, H[:, s0 - 1:s0], False)
                        else:
                            scan(0, LC, 0.0, True)
                            prev = H[:, 0:1]
                            for s0 in range(T - PIECE, LC - 1, -PIECE):
                                scan(s0, PIECE, prev, True)
                                prev = H[:, s0:s0 + 1]
                    TT("pool", tl["HF"][:], tl["HF"][:], tl["HB"][:], ALU.add, [bf["HF"], bf["HB"]], [bf["HF"]])
                    ACT(tl["XR"][:], tl["XR"][:], AF.Gelu, [bf["XR"]], [bf["XR"]])
                    TT("dve", YB[:], tl["HF"][:], tl["XR"][:], ALU.mult, [bf["HF"], bf["XR"]], [bYB])
                    DMA(YT_d[2 + c2], YB[:], [bYB], ())
            S.barrier()

        def phase_C(l):
            with ExitStack() as st:
                BTm = [sb(st, "BTm%d" % g, [128, T], BF16) for g in range(2)]
                CTm = [sb(st, "CTm%d" % g, [128, T], BF16) for g in range(2)]
                bBC = Buf("BCT")
                XTK = sb(st, "XTK", [128, NCH, 256])
                XTKb = sb(st, "XTKb", [128, NCH, 256], BF16)
                BTK = sb(st, "BTK", [128, NCH, 256], BF16)
                bTK = Buf("TK")
                EX = sb(st, "EX", [128, NCH, 40])
                DTt = sb(st, "DTt", [128, NCH, 8])
                LDT = sb(st, "LDT", [128, NCH, 8])
                DTA = sb(st, "DTA", [128, NCH, 8])
                WE = sb(st, "WE", [128, NCH, 8])
                A8 = sb(st, "A8", [128, 8])
                bSM = Buf("ssd_small")
                with ExitStack() as st2:
                    XR = [sb(st2, "cXR%d" % i, [128, T]) for i in range(2)]
                    bXR = [Buf("cXR%d" % i) for i in range(2)]
                    XC = sb(st2, "cXC", [128, T])
                    bXC = Buf("cXC")
                    XS = [sb(st2, "cXS%d" % i, [128, T]) for i in range(2)]
                    bXS = Buf("cXS")
                    for ci in range(6):
                        xr, bxr = XR[ci % 2], bXR[ci % 2]
                        DMA(xr[:], PTf_d[I_X + ci], (), [bxr])
                        conv_full(XC, xr, 64 + ci * 4, 88 + ci, [bxr, bPP], [bXC])
                        if ci < 2:
                            ACT(XS[ci][:], XC[:], AF.Silu, [bXC], [bXS])
                        elif ci < 4:
                            ACT(BTm[ci - 2][:], XC[:], AF.Silu, [bXC], [bBC])
                        else:
                            ACT(CTm[ci - 4][:], XC[:], AF.Silu, [bXC], [bBC])
                    for c in range(NCH):
                        tt = c * 128
                        p, bp = PS[(2 * c) % 8], bPS[(2 * c) % 8]
                        pb, bpb = PS[(2 * c + 1) % 8], bPS[(2 * c + 1) % 8]
                        for ci in range(2):
                            TR(p[:, ci * 128:(ci + 1) * 128], XS[ci][:, tt:tt + 128], ident, [bXS, bCST], [bp])
                        CP("act", XTK[:, c, :], p[:, 0:256], [bp], [bTK])
                        CP("dve", XTKb[:, c, :], p[:, 0:256], [bp], [bTK])
                        pbv = pb[:].bitcast(BF16)
                        for g in range(2):
                            TR(pbv[:, g * 128:(g + 1) * 128], BTm[g][:, tt:tt + 128], identb, [bBC, bCSTb], [bpb])
                        CP("dve", BTK[:, c, :], pbv[:, 0:256], [bpb], [bTK])
                        DMA(DTt[:, c, :], PTt_d[tt:tt + 128, 256:264], (), [bSM])
                S.barrier()
                TT("dve", DTt[:], DTt[:], RP[:, 260:268].unsqueeze(1).to_broadcast([128, NCH, 8]), ALU.add, [bSM, bRP], [bSM])
                ACT(DTt[:], DTt[:], AF.Exp, [bSM], [bSM])
                ACT(DTt[:], DTt[:], AF.Ln, [bSM], [bSM], bias=1.0)
                ACT(LDT[:], DTt[:], AF.Ln, [bSM], [bSM])
                ACT(A8[:], RP[:, 268:276], AF.Exp, [bRP], [bSM])
                STT(DTA[:], DTt[:], -1.0, A8[:].unsqueeze(1).to_broadcast([128, NCH, 8]), ALU.mult, ALU.mult, [bSM], [bSM])
                for c0 in range(0, NCH, 12):
                    nb = min(12, NCH - c0)
                    p, bp = PS[(c0 // 12) % 8], bPS[(c0 // 12) % 8]
                    for cc in range(nb):
                        for k, cm in enumerate((C_LT, C_UT, C_LE, C_GE, C_ONE)):
                            MM(p[:, cc * 40 + k * 8:cc * 40 + k * 8 + 8], CST[:, cm, :], DTA[:, c0 + cc, :], True, True,
                               [bCST, bSM], [bp], inc=(cc == nb - 1 and k == 4))
                    ACT(EX[:, c0:c0 + nb, :], p[:, 0:nb * 40].rearrange("p (c k) -> p c k", k=40), AF.Exp, [bp], [bSM])
                TT("dve", WE[:, :, 0:4], EX[:, :, 0:4], DTt[:, :, 0:4], ALU.mult, [bSM], [bSM])
                TT("dve", WE[:, :, 4:8], EX[:, :, 12:16], DTt[:, :, 4:8], ALU.mult, [bSM], [bSM])

                def bc4(ap):
                    return ap.unsqueeze(2).to_broadcast([128, 4, 64])

                def v4(ap):
                    return ap.rearrange("p (h d) -> p h d", h=4)

                HBall = sb(st, "HBall", [128, NCH, 256], BF16)
                bHBall = Buf("HBall")
                Hs = sb(st, "Hs", [128, 256])
                bHs = Buf("Hs")
                XSB = [sb(st, "XSB%d" % i, [128, 256], BF16) for i in range(2)]
                bXSB = [Buf("XSB%d" % i) for i in range(2)]
                MEMSET("pool", Hs[:], 0.0, [bHs])
                order = [1, 0] + list(range(NCH - 1, 1, -1))
                for it, c in enumerate(order):
                    xsb, bxsb = XSB[it % 2], bXSB[it % 2]
                    CP("pool", HBall[:, c, :], Hs[:], [bHs], [bHBall])
                    TT("dve", v4(xsb[:]), v4(XTK[:, c, :]), bc4(WE[:, c, 4:8]), ALU.mult, [bTK, bSM], [bxsb])
                    p, bp = PS[it % 4], bPS[it % 4]
                    for g in range(2):
                        MM(p[:, g * 128:(g + 1) * 128], BTK[:, c, g * 128:(g + 1) * 128], xsb[:, g * 128:(g + 1) * 128], True, True,
                           [bTK, bxsb], [bp], inc=(g == 1))
                    TT("pool", v4(Hs[:]), v4(Hs[:]), bc4(EX[:, c, 36:40]), ALU.mult, [bHs, bSM], [bHs])
                    TT("dve", Hs[:], Hs[:], p[:, 0:256], ALU.add, [bHs, bp], [bHs])
                RFB = [sb(st, "RFB%d" % i, [128, 8, 128]) for i in range(2)]
                bRFB = [Buf("RFB%d" % i) for i in range(2)]
                Dx = sb(st, "Dx", [128, 8, 128])
                bDx = Buf("Dx")
                GTs = sb(st, "GTs", [128, 2, 128])
                bGTs = Buf("GTs")
                DS = sb(st, "DS", [128, 4, 128])
                bDS = Buf("DS")
                MTb = sb(st, "MTb", [128, 4, 128], BF16)
                bMTb = Buf("MTb")
                Hbf = sb(st, "Hbf", [128, 256], BF16)
                bHbf = Buf("Hbf")
                t1 = sb(st, "ct1", [128, 256])
                t2 = sb(st, "ct2", [128, 256])
                t3 = sb(st, "ct3", [128, 256])
                bt1, bt2, bt3 = Buf("ct1"), Buf("ct2"), Buf("ct3")
                ZD = [sb(st, "ZD%d" % i, [128, 256]) for i in range(2)]
                bZD = [Buf("ZD%d" % i) for i in range(2)]
                ssum = sb(st, "ssum", [128, 2])
                bss = Buf("ssum")
                YN = sb(st, "YN", [128, 256])
                bYN = Buf("YN")
                YAT = sb(st, "YAT", [128, 2, T], BF16)
                bYAT = Buf("YAT")
                MEMSET("pool", Hs[:], 0.0, [bHs])
                for c in range(NCH):
                    tt = c * 128
                    rfb, brfb = RFB[c % 2], bRFB[c % 2]
                    zd, bzd = ZD[c % 2], bZD[c % 2]
                    DMA(zd[:], PTt_d[tt:tt + 128, 0:256], (), [bzd])
                    CP("pool", Hbf[:], Hs[:], [bHs], [bHbf])
                    for g in range(2):
                        MM(PS[0][:, g * 128:(g + 1) * 128], BTm[g][:, tt:tt + 128], CTm[g][:, tt:tt + 128], True, True, [bBC], [bPS[0]], inc=(g == 1))
                    CP("act", GTs[:], PS[0][:, 0:256].rearrange("p (g i) -> p g i", g=2), [bPS[0]], [bGTs])
                    for h in range(4):
                        TS("pool", rfb[:, h, :], CST[:, C_LE, :], DTA[:, c, h:h + 1], 1.0, ALU.mult, ALU.mult, [bCST, bSM], [brfb])
                        TS("pool", rfb[:, 4 + h, :], CST[:, C_GE, :], DTA[:, c, 4 + h:5 + h], 1.0, ALU.mult, ALU.mult, [bCST, bSM], [brfb])
                    for h in range(4):
                        MM(PS[1][:, h * 128:(h + 1) * 128], CST[:, C_LT, :], rfb[:, h, :], True, False, [bCST, brfb], [bPS[1]], inc=False)
                        MM(PS[1][:, h * 128:(h + 1) * 128], ident, CST[:, C_NEGF, :], False, True, [bCST], [bPS[1]], inc=(h == 3))
                    for h in range(4):
                        MM(PS[2][:, h * 128:(h + 1) * 128], CST[:, C_UT, :], rfb[:, 4 + h, :], True, False, [bCST, brfb], [bPS[2]], inc=False)
                        MM(PS[2][:, h * 128:(h + 1) * 128], ident, CST[:, C_NEGB, :], False, True, [bCST], [bPS[2]], inc=(h == 3))
                    for dh in range(8):
                        src = PS[1 + dh // 4]
                        h = dh % 4
                        ACT(Dx[:, dh, :], src[:, h * 128:(h + 1) * 128], AF.Exp, [bPS[1 + dh // 4], bSM], [bDx], bias=LDT[:, c, dh:dh + 1])
                    TT("pool", DS[:], Dx[:, 0:4, :], Dx[:, 4:8, :], ALU.add, [bDx], [bDS])
                    for g in range(2):
                        TT("dve", MTb[:, 2 * g:2 * g + 2, :], DS[:, 2 * g:2 * g + 2, :], GTs[:, g, :].unsqueeze(1).to_broadcast([128, 2, 128]),
                           ALU.mult, [bDS, bGTs], [bMTb])
                    for h in range(4):
                        MM(PS[3][:, h * 64:(h + 1) * 64], MTb[:, h, :], XTKb[:, c, h * 64:(h + 1) * 64], True, True, [bMTb, bTK], [bPS[3]], inc=(h == 3))
                    for g in range(2):
                        MM(PS[4][:, g * 128:(g + 1) * 128], CTm[g][:, tt:tt + 128], Hbf[:, g * 128:(g + 1) * 128], True, True, [bBC, bHbf], [bPS[4]], inc=False)
                        MM(PS[4][:, 256 + g * 128:256 + (g + 1) * 128], CTm[g][:, tt:tt + 128], HBall[:, c, g * 128:(g + 1) * 128], True, True,
                           [bBC, bHBall], [bPS[4]], inc=(g == 1))
                    TT("dve", v4(t1[:]), v4(PS[4][:, 0:256]), bc4(EX[:, c, 16:20]), ALU.mult, [bPS[4], bSM], [bt1])
                    TT("dve", v4(t2[:]), v4(PS[4][:, 256:512]), bc4(EX[:, c, 28:32]), ALU.mult, [bPS[4], bSM], [bt2])
                    TT("pool", t1[:], t1[:], t2[:], ALU.add, [bt1, bt2], [bt1])
                    TT("dve", t1[:], t1[:], PS[3][:, 0:256], ALU.add, [bt1, bPS[3]], [bt1])
                    TT("pool", v4(t3[:]), v4(XTK[:, c, :]), bc4(RP[:, 256:260]), ALU.mult, [bTK, bRP], [bt3])
                    TT("pool", t1[:], t1[:], t3[:], ALU.add, [bt1, bt3], [bt1])
                    ACT(t2[:], zd[:], AF.Silu, [bzd], [bt2])
                    TT("dve", t1[:], t1[:], t2[:], ALU.mult, [bt1, bt2], [bt1])
                    TT("pool", t3[:], t1[:], t1[:], ALU.mult, [bt1], [bt3])
                    S.op("dve", lambda e: e.tensor_reduce(out=ssum[:, 0:1], in_=t3[:], axis=AX.X, op=ALU.add), [bt3], [bss])
                    ACT(ssum[:, 1:2], ssum[:, 0:1], AF.Sqrt, [bss], [bss], bias=EPS, scale=1.0 / 256)
                    RECIP(ssum[:, 1:2], ssum[:, 1:2], [bss], [bss])
                    STT(YN[:], t1[:], ssum[:, 1:2], RP[:, 0:256], ALU.mult, ALU.mult, [bt1, bss, bRP], [bYN])
                    for ci in range(2):
                        TR(PS[6][:, ci * 128:(ci + 1) * 128], YN[:, ci * 128:(ci + 1) * 128], ident, [bYN, bCST], [bPS[6]])
                    CP("act", YAT[:, :, tt:tt + 128], PS[6][:, 0:256].rearrange("p (j t) -> p j t", j=2), [bPS[6]], [bYAT])
                    xsb, bxsb = XSB[c % 2], bXSB[c % 2]
                    TT("dve", v4(xsb[:]), v4(XTK[:, c, :]), bc4(WE[:, c, 0:4]), ALU.mult, [bTK, bSM], [bxsb])
                    for g in range(2):
                        MM(PS[5][:, g * 128:(g + 1) * 128], BTK[:, c, g * 128:(g + 1) * 128], xsb[:, g * 128:(g + 1) * 128], True, True,
                           [bTK, bxsb], [bPS[5]], inc=(g == 1))
                    TT("pool", v4(Hs[:]), v4(Hs[:]), bc4(EX[:, c, 32:36]), ALU.mult, [bHs, bSM], [bHs])
                    TT("dve", Hs[:], Hs[:], PS[5][:, 0:256], ALU.add, [bHs, bPS[5]], [bHs])
                DMA(YT_d[0:2].rearrange("j p t -> p j t"), YAT[:], [bYAT], ())
            S.barrier()


        def phase_D(l, need_ctx, units=((0, 0), (0, 1), (1, 0), (1, 1))):
            with ExitStack() as st:
                QG = sb(st, "QG", [128, 4])
                ESK = sb(st, "ESK", [128, 4])
                bQG = Buf("QG")
                TS("dve", QG[:, 0:1], PP[:, 116:117], 0.125, None, ALU.mult, None, [bPP], [bQG])
                TS("dve", QG[:, 2:3], PP[:, 118:119], 0.125, None, ALU.mult, None, [bPP], [bQG])
                CP("dve", QG[:, 1:2], PP[:, 117:118], [bPP], [bQG])
                CP("dve", QG[:, 3:4], PP[:, 119:120], [bPP], [bQG])
                ACT(ESK[:], RP[:, 276:280], AF.Exp, [bRP], [bQG])
                NSET = 2
                QTs = [sb(st, "QT%d" % i, [128, T], BF16) for i in range(NSET)]
                KZs = [[sb(st, "KZ%d%d" % (i, g), [128, T], BF16) for g in range(2)] for i in range(NSET)]
                VAs = [sb(st, "VA%d" % i, [128, NCH, 128], BF16) for i in range(NSET)]
                YOs = [sb(st, "YO%d" % i, [128, T], BF16) for i in range(NSET)]
                bQKs = [Buf("QK%d" % i) for i in range(NSET)]
                bVAs = [Buf("VA%d" % i) for i in range(NSET)]
                bYOs = [Buf("YO%d" % i) for i in range(NSET)]
                for i in range(NSET):
                    for g in range(2):
                        MEMSET("pool", KZs[i][g][:], 0.0, [bQKs[i]])
                    MEMSET("pool", VAs[i][:, :, 64:128], 1.0, [bVAs[i]])
                XR = [sb(st, "dXR%d" % i, [128, 512]) for i in range(2)]
                bXR = [Buf("dXR%d" % i) for i in range(2)]
                CSs = [sb(st, "dCS%d" % i, [128, 2, 512]) for i in range(2)]
                bCSs = [Buf("dCS%d" % i) for i in range(2)]
                SQ = sb(st, "dSQ", [128, 512], BF16)
                bSQ = Buf("dSQ")
                RSq = sb(st, "dRS", [128, 512])
                bRSq = Buf("dRS")
                QN = sb(st, "dQN", [128, 512])
                bQN = Buf("dQN")
                TA = sb(st, "dTA", [128, 512])
                TB = sb(st, "dTB", [128, 512])
                bTA, bTB = Buf("dTA"), Buf("dTB")
                VS = sb(st, "dVS", [128, NCH, 64])
                bVS = Buf("dVS")
                PTs = [sb(st, "PTs%d" % i, [128, 640], BF16) for i in range(3)]
                bPTs = [Buf("PTs%d" % i) for i in range(3)]
                RD = sb(st, "RD", [128, 512])
                bRD = Buf("RD")
                cnt = {"pi": 0, "xi": 0, "oi": 0}
                LA = 2

                def run_pipe(steps):
                    n_ = len(steps)
                    for i in range(n_ + LA):
                        if i < n_:
                            steps[i][0]()
                        if i >= LA:
                            steps[i - LA][1]()

                def prep(ty, hk, QT, KZ, VA, bQK, bVA):
                    base = I_CQ if ty == 0 else I_DQ
                    for kind in range(2):
                        ci = base + (hk if kind == 0 else 2 + hk)
                        gcol = ty * 2 + kind
                        for (t0, n, which) in GROUPS:
                            xr, bxr = XR[cnt["xi"] % 2], bXR[cnt["xi"] % 2]
                            cs, bcs = CSs[cnt["xi"] % 2], bCSs[cnt["xi"] % 2]
                            cnt["xi"] += 1
                            DMA(xr[:, 0:n], PTf_d[ci, :, t0:t0 + n], (), [bxr])
                            if not which:
                                DMA(cs[:, :, 0:n], rope_d[:, :, t0 - LC:t0 - LC + n].rearrange("c p t -> p c t"), (), [bcs])
                            ACT(SQ[:, 0:n], xr[:, 0:n], AF.Square, [bxr], [bSQ])
                            p, bp = PS[cnt["pi"] % 6], bPS[cnt["pi"] % 6]
                            cnt["pi"] += 1
                            MM(p[:, 0:n], CSTb[:, C_BD, :], SQ[:, 0:n], True, True, [bCSTb, bSQ], [bp])
                            ACT(RSq[:, 0:n], p[:, 0:n], AF.Sqrt, [bp], [bRSq], bias=EPS, scale=1.0 / 64)
                            RECIP(RSq[:, 0:n], RSq[:, 0:n], [bRSq], [bRSq])

                            def emit_out(fn):
                                if kind == 0:
                                    fn(lambda rs: QT[rs, t0:t0 + n], slice(0, 128))
                                else:
                                    fn(lambda rs: KZ[0][rs, t0:t0 + n], slice(0, 64))
                                    fn(lambda rs: KZ[1][rs, t0:t0 + n], slice(64, 128))

                            if which:
                                emit_out(lambda dst, rs: STT(dst(rs), xr[rs, 0:n], QG[rs, gcol:gcol + 1], RSq[rs, 0:n], ALU.mult, ALU.mult,
                                                             [bxr, bQG, bRSq], [bQK]))
                            else:
                                STT(QN[:, 0:n], xr[:, 0:n], QG[:, gcol:gcol + 1], RSq[:, 0:n], ALU.mult, ALU.mult, [bxr, bQG, bRSq], [bQN])
                                p2, bp2 = PS[cnt["pi"] % 6], bPS[cnt["pi"] % 6]
                                cnt["pi"] += 1
                                MM(p2[:, 0:n], CST[:, C_PERM, :], QN[:, 0:n], True, True, [bCST, bQN], [bp2])
                                TT("pool", TA[:, 0:n], QN[:, 0:n], cs[:, 0, 0:n], ALU.mult, [bQN, bcs], [bTA])
                                TT("dve", TB[:, 0:n], p2[:, 0:n], cs[:, 1, 0:n], ALU.mult, [bp2, bcs], [bTB])
                                emit_out(lambda dst, rs: TT("pool", dst(rs), TA[rs, 0:n], TB[rs, 0:n], ALU.add, [bTA, bTB], [bQK]))
                    col0 = 264 + ty * 128 + hk * 64
                    src = PTt_d[:, col0:col0 + 64].rearrange("(c p) f -> p c f", p=128)
                    DMA(VS[:, 0:17, :], src[:, 0:17, :], (), [bVS])
                    DMA(VS[:, 17:34, :], src[:, 17:34, :], (), [bVS])
                    CP("dve", VA[:, :, 0:64], VS[:], [bVS], [bVA])

                def attend_global(hk, QT, KZ, VA, YO, bQK, bVA, bYO):
                    qgroups = ([(0, LC, [0, 1])] if need_ctx else []) + [(t0, n, list(range(NCH))) for (t0, n, w_) in GROUPS[1:]]
                    steps = []
                    for g in range(2):
                        rows = slice(g * 64, (g + 1) * 64)
                        for (q0, nq, keys) in qgroups:
                            O, bO = PS[6 + cnt["oi"] % 2], bPS[6 + cnt["oi"] % 2]
                            cnt["oi"] += 1
                            for ki, kc in enumerate(keys):
                                si = len(steps)
                                Sp, bSp = PS[si % 3], bPS[si % 3]
                                pt, bpt = PTs[si % 3], bPTs[si % 3]

                                def front(Sp=Sp, bSp=bSp, pt=pt, bpt=bpt, g=g, kc=kc, q0=q0, nq=nq):
                                    MM(Sp[:, 0:nq], KZ[g][:, kc * 128:(kc + 1) * 128], QT[:, q0:q0 + nq], True, True, [bQK], [bSp])
                                    ACT(pt[:, 0:nq], Sp[:, 0:nq], AF.Exp, [bSp], [bpt])

                                def back(O=O, bO=bO, pt=pt, bpt=bpt, rows=rows, kc=kc, q0=q0, nq=nq, ki=ki, nk=len(keys)):
                                    MM(O[:, 0:nq], VA[:, kc, :], pt[:, 0:nq], ki == 0, ki == nk - 1, [bVA, bpt], [bO], inc=(ki == nk - 1))
                                    if ki == nk - 1:
                                        RECIP(RD[64:128, 0:nq], O[64:128, 0:nq], [bO], [bRD])
                                        TT("dve", YO[rows, q0:q0 + nq], O[0:64, 0:nq], RD[64:128, 0:nq], ALU.mult, [bO, bRD], [bYO])

                                steps.append((front, back))
                    run_pipe(steps)

                def attend_window(hk, QT, KZ, VA, YO, bQK, bVA, bYO):
                    def norm_d(O, bO, rows, h, q0, nq):
                        TS("dve", RD[64:128, 0:nq], O[64:128, 0:nq], ESK[64:128, h:h + 1], None, ALU.add, None, [bO, bQG], [bRD])
                        RECIP(RD[64:128, 0:nq], RD[64:128, 0:nq], [bRD], [bRD])
                        TT("dve", YO[rows, q0:q0 + nq], O[0:64, 0:nq], RD[64:128, 0:nq], ALU.mult, [bO, bRD], [bYO])

                    steps = []
                    for g in range(2):
                        h = 2 * hk + g
                        rows = slice(g * 64, (g + 1) * 64)
                        if need_ctx:
                            O, bO = PS[6 + cnt["oi"] % 2], bPS[6 + cnt["oi"] % 2]
                            cnt["oi"] += 1
                            si = len(steps)
                            SA, bSA = PS[(2 * si) % 6], bPS[(2 * si) % 6]
                            SB, bSB = PS[(2 * si + 1) % 6], bPS[(2 * si + 1) % 6]
                            pt, bpt = PTs[si % 3], bPTs[si % 3]

                            def front(SA=SA, bSA=bSA, SB=SB, bSB=bSB, pt=pt, bpt=bpt, g=g):
                                for kc, (Sp, bSp) in enumerate(((SA, bSA), (SB, bSB))):
                                    MM(Sp[:, 0:LC], KZ[g][:, kc * 128:(kc + 1) * 128], QT[:, 0:LC], True, True, [bQK], [bSp])
                                    ACT(pt[:, kc * 256:(kc + 1) * 256], Sp[:, 0:LC], AF.Exp, [bSp], [bpt])

                            def back(O=O, bO=bO, pt=pt, bpt=bpt, rows=rows, h=h):
                                for kc in range(2):
                                    MM(O[:, 0:LC], VA[:, kc, :], pt[:, kc * 256:(kc + 1) * 256], kc == 0, kc == 1, [bVA, bpt], [bO], inc=(kc == 1))
                                norm_d(O, bO, rows, h, 0, LC)

                            steps.append((front, back))
                        for (t0, n, w_) in GROUPS[1:]:
                            O, bO = PS[6 + cnt["oi"] % 2], bPS[6 + cnt["oi"] % 2]
                            cnt["oi"] += 1
                            for blk in range(4):
                                nb = (t0 - LC) // 128 + blk
                                qb = t0 + blk * 128
                                lat = []
                                if nb > 0:
                                    lat.append((2 + nb - 1, C_NEGB))
                                lat.append((2 + nb, None))
                                if nb < 31:
                                    lat.append((2 + nb + 1, C_NEGF))
                                si = len(steps)
                                SA, bSA = PS[(2 * si) % 6], bPS[(2 * si) % 6]
                                SB, bSB = PS[(2 * si + 1) % 6], bPS[(2 * si + 1) % 6]
                                pt, bpt = PTs[si % 3], bPTs[si % 3]

                                def front(SA=SA, bSA=bSA, SB=SB, bSB=bSB, pt=pt, bpt=bpt, g=g, lat=lat, qb=qb):
                                    for ii, (kc, mk) in enumerate(lat):
                                        last = (ii == len(lat) - 1)
                                        MM(SA[:, ii * 128:(ii + 1) * 128], KZ[g][:, kc * 128:(kc + 1) * 128], QT[:, qb:qb + 128],
                                           True, mk is None, [bQK], [bSA], inc=(last and mk is None))
                                        if mk is not None:
                                            MM(SA[:, ii * 128:(ii + 1) * 128], identb, CSTb[:, mk, :], False, True, [bCSTb], [bSA], inc=last)
                                    for kc in range(2):
                                        MM(SB[:, kc * 128:(kc + 1) * 128], KZ[g][:, kc * 128:(kc + 1) * 128], QT[:, qb:qb + 128],
                                           True, True, [bQK], [bSB], inc=(kc == 1))
                                    nl = len(lat)
                                    ACT(pt[:, 0:nl * 128], SA[:, 0:nl * 128], AF.Exp, [bSA], [bpt])
                                    ACT(pt[:, 384:640], SB[:, 0:256], AF.Exp, [bSB], [bpt])

                                def back(O=O, bO=bO, pt=pt, bpt=bpt, rows=rows, lat=lat, blk=blk, h=h, t0=t0, n=n):
                                    kvs = [(kc, ii * 128) for ii, (kc, mk) in enumerate(lat)] + [(0, 384), (1, 512)]
                                    for ii, (kc, off) in enumerate(kvs):
                                        MM(O[:, blk * 128:(blk + 1) * 128], VA[:, kc, :], pt[:, off:off + 128], ii == 0, ii == len(kvs) - 1,
                                           [bVA, bpt], [bO], inc=(ii == len(kvs) - 1))
                                    if blk == 3:
                                        norm_d(O, bO, rows, h, t0, n)

                                steps.append((front, back))
                    run_pipe(steps)

                for ui, (ty, hk) in enumerate(units):
                    k = ui % NSET
                    prep(ty, hk, QTs[k], KZs[k], VAs[k], bQKs[k], bVAs[k])
                    (attend_global if ty == 0 else attend_window)(hk, QTs[k], KZs[k], VAs[k], YOs[k], bQKs[k], bVAs[k], bYOs[k])
                    DMA(YT_d[4 + 2 * ty + hk], YOs[k][:], [bYOs[k]], ())
            S.barrier()

        def phase_E(l, need_ctx):
            with ExitStack() as st:
                WO = sb(st, "WO", [128, 8, 1024], BF16)
                W1 = sb(st, "W1", [128, 8, 4096], BF16)
                W2 = sb(st, "W2", [128, 32, 1024], BF16)
                bWO, bW1, bW2 = Buf("WO"), Buf("W1"), Buf("W2")
                DMA(WO[:], wout_d[l].rearrange("(j p) c -> p j c", p=128), (), [bWO], q="pool")
                for q4 in range(4):
                    DMA(W1[:, :, q4 * 1024:(q4 + 1) * 1024], w1_d[l, :, q4 * 1024:(q4 + 1) * 1024].rearrange("(j p) c -> p j c", p=128), (), [bW1], q="pool")
                for q4 in range(4):
                    DMA(W2[:, q4 * 8:(q4 + 1) * 8, :], w2_d[l, q4 * 1024:(q4 + 1) * 1024, :].rearrange("(k p) c -> p k c", p=128), (), [bW2], q="pool")
                NE = 256
                _xt = sb(st, "eXT", [128, 8, NE])
                _bxt = Buf("eXT")
                XT = [_xt, _xt]
                bXT = [_bxt, _bxt]
                _yt = sb(st, "eYT", [128, 8, NE], BF16)
                _byt = Buf("eYT")
                YTs = [_yt, _yt]
                bYTs = [_byt, _byt]
                X1 = [sb(st, "eX1%d" % i, [128, 8, NE]) for i in range(2)]
                bX1 = [Buf("eX1%d" % i) for i in range(2)]
                SQ = sb(st, "eSQ", [128, 8, NE], BF16)
                bSQ = Buf("eSQ")
                RS = sb(st, "eRS", [128, NE])
                bRS = Buf("eRS")
                H2 = [sb(st, "eH2%d" % i, [128, 8, NE], BF16) for i in range(2)]
                bH2 = [Buf("eH2%d" % i) for i in range(2)]
                RL = [sb(st, "eRL%d" % i, [128, NE]) for i in range(2)]
                bRL = [Buf("eRL%d" % i) for i in range(2)]
                AK = [sb(st, "eAK%d" % i, [128, NE], BF16) for i in range(3)]
                bAK = [Buf("eAK%d" % i) for i in range(3)]
                starts = list(range(0 if need_ctx else LC, T, NE))

                def load(gi):
                    t0 = starts[gi]
                    DMA(XT[gi % 2][:], xT_d[:, :, t0:t0 + NE].rearrange("j p t -> p j t"), (), [bXT[gi % 2]])
                    DMA(YTs[gi % 2][:], YT_d[:, :, t0:t0 + NE].rearrange("j p t -> p j t"), (), [bYTs[gi % 2]])

                def head(gi):
                    t0 = starts[gi]
                    w = 1 if t0 < LC else 0
                    xt, bxt = XT[gi % 2], bXT[gi % 2]
                    yt, byt = YTs[gi % 2], bYTs[gi % 2]
                    x1, bx1 = X1[gi % 2], bX1[gi % 2]
                    h2, bh2 = H2[gi % 2], bH2[gi % 2]
                    p, bp = PS[3], bPS[3]
                    for fo in range(8):
                        for k in range(8):
                            MM(p[:, (fo % 2) * NE:(fo % 2 + 1) * NE], WO[:, k, fo * 128:(fo + 1) * 128], yt[:, k, :], k == 0, k == 7, [bWO, byt], [bp], inc=(k == 7))
                        STT(x1[:, fo, :], p[:, (fo % 2) * NE:(fo % 2 + 1) * NE], MODS[:, 16 + fo, w:w + 1], xt[:, fo, :], ALU.mult, ALU.add, [bp, bMODS, bxt], [bx1])
                    ACT(SQ[:], x1[:], AF.Square, [bx1], [bSQ])
                    for j in range(8):
                        MM(p[:, 0:NE], CSTb[:, C_ONE, :], SQ[:, j, :], j == 0, j == 7, [bSQ, bCSTb], [bp], inc=(j == 7))
                    ACT(RS[:], p[:, 0:NE], AF.Sqrt, [bp], [bRS], bias=EPS, scale=1.0 / D)
                    RECIP(RS[:], RS[:], [bRS], [bRS])
                    TT("dve", xt[:], x1[:], RS[:].unsqueeze(1).to_broadcast([128, 8, NE]), ALU.mult, [bx1, bRS], [bxt])
                    for j in range(8):
                        ACT(h2[:, j, :], xt[:, j, :], AF.Identity, [bxt, bG, bMODS], [bh2],
                            bias=MODS[:, 24 + j, w:w + 1], scale=G2[:, j, w:w + 1])

                steps = []
                for gi, t0 in enumerate(starts):
                    w = 1 if t0 < LC else 0
                    for kh in range(32):
                        ki = len(steps)
                        pu, bpu = PS[ki % 3], bPS[ki % 3]
                        rl, brl = RL[ki % 2], bRL[ki % 2]
                        ak, bak = AK[ki % 3], bAK[ki % 3]

                        def front(gi=gi, kh=kh, pu=pu, bpu=bpu, rl=rl, brl=brl, ak=ak, bak=bak):
                            h2, bh2 = H2[gi % 2], bH2[gi % 2]
                            for j in range(8):
                                MM(pu[:, 0:NE], W1[:, j, kh * 128:(kh + 1) * 128], h2[:, j, :], j == 0, j == 7, [bW1, bh2], [bpu], inc=(j == 7))
                            ACT(rl[:], pu[:, 0:NE], AF.Relu, [bpu], [brl])
                            TT("dve", ak[:], rl[:], pu[:, 0:NE], ALU.mult, [brl, bpu], [bak])
                            if kh == 12 and gi + 1 < len(starts):
                                load(gi + 1)
                            if kh == 16 and gi + 1 < len(starts):
                                head(gi + 1)

                        def back(gi=gi, kh=kh, ak=ak, bak=bak, t0=t0, w=w):
                            x1, bx1 = X1[gi % 2], bX1[gi % 2]
                            for fo in range(8):
                                pd, bpd = PS[4 + fo // 2], bPS[4 + fo // 2]
                                MM(pd[:, (fo % 2) * NE:(fo % 2 + 1) * NE], W2[:, kh, fo * 128:(fo + 1) * 128], ak[:], (kh == 0 and fo % 2 == 0), kh == 31,
                                   [bW2, bak], [bpd], inc=(kh == 31 or fo == 7))
                            if kh == 31:
                                for fo in range(8):
                                    pd, bpd = PS[4 + fo // 2], bPS[4 + fo // 2]
                                    STT(x1[:, fo, :], pd[:, (fo % 2) * NE:(fo % 2 + 1) * NE], MODS[:, 40 + fo, w:w + 1], x1[:, fo, :], ALU.mult, ALU.add,
                                        [bpd, bMODS, bx1], [bx1])
                                DMA(xT_d[:, :, t0:t0 + NE].rearrange("j p t -> p j t"), x1[:], [bx1], ())

                        steps.append((front, back))
                load(0)
                head(0)
                LA = 2
                for i in range(len(steps) + LA):
                    if i < len(steps):
                        steps[i][0]()
                    if i >= LA:
                        steps[i - LA][1]()
            S.barrier()

        def phase_U():
            with ExitStack() as st:
                XT = [sb(st, "uXT%d" % i, [128, 8, 512]) for i in range(2)]
                bXT = [Buf("uXT%d" % i) for i in range(2)]
                OS = [sb(st, "uOS%d" % i, [128, D]) for i in range(2)]
                bOS = [Buf("uOS%d" % i) for i in range(2)]
                it = 0
                for gi, (t0, n, which) in enumerate(GROUPS[1:]):
                    xt, bxt = XT[gi % 2], bXT[gi % 2]
                    DMA(xt[:], xT_d[:, :, t0:t0 + n].rearrange("j p t -> p j t"), (), [bxt])
                    for s in range(4):
                        os_, bos = OS[it % 2], bOS[it % 2]
                        for half in range(2):
                            p, bp = PS[(it * 2 + half) % 8], bPS[(it * 2 + half) % 8]
                            for jj in range(4):
                                j = half * 4 + jj
                                TR(p[:, jj * 128:(jj + 1) * 128], xt[:, j, s * 128:(s + 1) * 128], ident, [bxt, bCST], [bp])
                            CP("act" if half else "dve", os_[:, half * 512:(half + 1) * 512], p[:, 0:512], [bp], [bos])
                        it += 1
                        tt = t0 - LC + s * 128
                        final_toks.append(DMA(out_d[tt:tt + 128, :], os_[:], [bos], ()))
            S.barrier()

        phase_T()
        done = False
        for l in range(n_layers):
            need_ctx = l < n_layers - 1
            for nm, fn in (("0", lambda: phase_0(l)), ("A", lambda: phase_A(l)), ("B", lambda: phase_B(l)), ("C", lambda: phase_C(l)),
                           ("D", lambda: phase_D(l, need_ctx)), ("E", lambda: phase_E(l, need_ctx))):
                if stop_after is not None and isinstance(stop_after[0], tuple):
                    if (nm, l) not in stop_after:
                        continue
                fn()
                if stop_after == (nm, l):
                    done = True
                    break
            if done:
                break
        if stop_after is None:
            phase_U()
        else:
            with ExitStack() as st:
                Z = sb(st, "ZZ", [128, D])
                bZ = Buf("ZZ")
                MEMSET("dve", Z[:], 0.0, [bZ])
                final_toks.append(DMA(out_d[0:128, :], Z[:], [bZ], ()))
        S.barrier()
        with nc.Block() as block:
            S.emit(block)
    return nc, S


def _consts():
    p = np.arange(128)[:, None]
    f = np.arange(128)[None, :]
    cst = np.zeros((NCST, 128, 128), np.float32)
    cst[C_ID] = (p == f)
    cst[C_LT] = (p > f)
    cst[C_UT] = (p < f)
    cst[C_LE] = (p <= f)
    cst[C_GE] = (p >= f)
    cst[C_ONE] = 1.0
    cst[C_BD] = (p // 64 == f // 64)
    cst[C_PERM] = (p == (f ^ 16))
    cst[C_NEGF] = NEG * (p > f)
    cst[C_NEGB] = NEG * (p < f)
    return cst


def _rope():
    t = np.arange(LL)
    row = (t // 64).astype(np.float32)
    col = (t % 64).astype(np.float32)
    inv = (np.float32(10000.0) ** (-np.arange(0, 32, 2, dtype=np.float32) / np.float32(32))).astype(np.float32)
    tab = np.zeros((2, 128, LL), np.float32)
    for d in range(128):
        dd = d % 64
        axis, half, fi = dd // 32, (dd % 32) // 16, dd % 16
        ang = ((row if axis == 0 else col) * inv[fi]).astype(np.float32)
        tab[0, d] = np.cos(ang)
        tab[1, d] = (-np.sin(ang)) if half == 0 else np.sin(ang)
    return tab


def _pack(inputs):
    f = lambda a: np.ascontiguousarray(np.asarray(a, dtype=np.float32))
    pp = np.zeros((2, 128, NPP), np.float32)
    rp = np.zeros((2, NRP), np.float32)
    lruw = np.zeros((2, 2, 2, 2, 128, 128), np.float32)
    for l in range(2):
        pp[l, :, 0:8] = f(inputs["g_mix"])[l].reshape(8, 128).T
        pp[l, :, 8:16] = f(inputs["g_ffn"])[l].reshape(8, 128).T
        pp[l, :, 16:64] = f(inputs["b_mod"])[l].reshape(48, 128).T
        cw = f(inputs["ssd_conv_w"])[l]
        for ci in range(6):
            pp[l, :, 64 + ci * 4:64 + ci * 4 + 4] = cw[:, ci * 128:(ci + 1) * 128].T
        pp[l, :, 88:94] = f(inputs["ssd_conv_b"])[l].reshape(6, 128).T
        lw = f(inputs["lru_conv_w"])[l]
        for c2 in range(2):
            pp[l, :, 94 + c2 * 4:94 + c2 * 4 + 4] = lw[:, c2 * 128:(c2 + 1) * 128].T
        pp[l, :, 102:104] = f(inputs["lru_conv_b"])[l].reshape(2, 128).T
        pp[l, :, 104:108] = f(inputs["lru_lambda"])[l].reshape(4, 128).T
        pp[l, :, 108:112] = f(inputs["lru_b_a"])[l].reshape(4, 128).T
        pp[l, :, 112:116] = f(inputs["lru_b_i"])[l].reshape(4, 128).T
        for k, nm in enumerate(("gqa_q_norm", "gqa_k_norm", "swa_q_norm", "swa_k_norm")):
            pp[l, :, 116 + k] = np.tile(f(inputs[nm])[l], 2)
        rp[l, 0:256] = f(inputs["ssd_norm_g"])[l]
        rp[l, 256:260] = f(inputs["ssd_d"])[l]
        rp[l, 260:268] = f(inputs["ssd_dt_bias"])[l].reshape(8)
        rp[l, 268:276] = f(inputs["ssd_a_log"])[l].reshape(8)
        rp[l, 276:280] = f(inputs["swa_sink"])[l]
        for dr in range(2):
            for gt, nm in enumerate(("lru_w_a", "lru_w_i")):
                w = f(inputs[nm])[l, dr]
                for c2 in range(2):
                    for bl in range(2):
                        lruw[l, dr, gt, c2, bl * 64:(bl + 1) * 64, bl * 64:(bl + 1) * 64] = w[c2 * 2 + bl]
    return pp, rp, lruw


def make_in_maps(inputs, cores):
    f = lambda a: np.ascontiguousarray(np.asarray(a, dtype=np.float32))
    pp, rp, lruw = _pack(inputs)
    cst = _consts()
    rope = _rope()
    shared = {"w_mod": f(inputs["w_mod"]), "w_in": f(inputs["w_in"]), "w_out": f(inputs["w_out"]),
              "w_ffn1": f(inputs["w_ffn1"]), "w_ffn2": f(inputs["w_ffn2"]), "pp": pp, "rp": rp, "lruw": lruw,
              "cst": cst, "rope": rope}
    maps = []
    for b in cores:
        c2 = np.stack([f(inputs["c"])[b], f(inputs["c_ctx"])], axis=0)
        c2 = np.ascontiguousarray(c2.reshape(2, 8, 128).transpose(2, 1, 0))
        m = dict(shared)
        m["x"] = f(inputs["x"])[b]
        m["ctx"] = f(inputs["ctx"])[b]
        m["c2"] = c2
        maps.append(m)
    return maps


_NC_CACHE = {}


def kernel(**inputs):
    if "nc" not in _NC_CACHE:
        _NC_CACHE["nc"] = build()[0]
    nc = _NC_CACHE["nc"]
    maps = make_in_maps(inputs, range(4))
    res = run_bass_kernel_spmd(nc, maps, core_ids=list(range(4)))
    return np.stack([np.asarray(r["out"], dtype=np.float32) for r in res.results], axis=0)
```

```python
import numpy as np
import concourse.bass as bass
import concourse.mybir as mybir
from concourse.bass_utils import run_bass_kernel_spmd
from contextlib import ExitStack

F32 = mybir.dt.float32
BF16 = mybir.dt.bfloat16
ALU = mybir.AluOpType
AF = mybir.ActivationFunctionType
AX = mybir.AxisListType

T = 4352
LC = 256
LL = 4096
D = 1024
NCH = 34
EPS = 1e-6
NEG = -30000.0
GROUPS = [(0, 256, 1)] + [(256 + i * 512, 512, 0) for i in range(8)]

FM = [(256, 128, 0), (384, 128, 0), (512, 128, 0), (640, 128, 0), (768, 128, 0), (896, 128, 0),
      (1032, 128, 0), (1160, 128, 0), (1288, 128, 0), (1416, 128, 0),
      (1544, 128, 0), (1672, 128, 0), (1800, 64, 1), (1864, 64, 1),
      (2056, 128, 0), (2184, 128, 0), (2312, 64, 1), (2376, 64, 1)]
I_X, I_B, I_C, I_G, I_R, I_CQ, I_CK, I_DQ, I_DK = 0, 2, 4, 6, 8, 10, 12, 14, 16
TMC = [(0, 256), (1024, 8), (1928, 128), (2440, 128)]
NTM = 520
NPP = 120
NRP = 280
(C_ID, C_LT, C_UT, C_LE, C_GE, C_ONE, C_BD, C_PERM, C_NEGF, C_NEGB) = range(10)
NCST = 10


class Buf:
    __slots__ = ("name", "w", "r")

    def __init__(self, name):
        self.name = name
        self.w = None
        self.r = {}


class Sched:
    COMPUTE = ("pe", "act", "dve", "pool")
    ALL = ("pe", "act", "dve", "pool", "sp")

    def __init__(self, nc, es, n_dma_sems=32, same_engine_sync=True):
        self.nc = nc
        self.streams = {e: [] for e in self.ALL}
        self.sems = {}
        for e in self.COMPUTE:
            self.sems[e] = es.enter_context(nc.semaphore("s_" + e))
        self.cnt = {e: 0 for e in self.COMPUTE}
        self.pending = {e: False for e in self.COMPUTE}
        self.dsem = [es.enter_context(nc.semaphore("d%d" % i)) for i in range(n_dma_sems)]
        self.dcnt = [0] * n_dma_sems
        self.dnext = 0
        self.waited = {e: {} for e in self.ALL}
        self.same_engine_sync = same_engine_sync
        self.nwaits = 0

    def _semof(self, k):
        if isinstance(k, tuple):
            return self.dsem[k[1]]
        return self.sems[k]

    def _deps(self, eng, reads, writes, extra=()):
        deps = {}

        def add(tok):
            if tok is None:
                return
            k, v = tok
            if deps.get(k, 0) < v:
                deps[k] = v

        for b in reads:
            add(b.w)
        for b in writes:
            add(b.w)
            for k, v in b.r.items():
                add((k, v))
        for t in extra:
            add(t)
        out = []
        for k, v in deps.items():
            if k == eng:
                if eng == "pe" or not self.same_engine_sync:
                    continue
                if v > self.cnt[eng]:
                    continue
            if self.waited[eng].get(k, 0) >= v:
                continue
            self.waited[eng][k] = v
            out.append((k, v))
        self.nwaits += len(out)
        return out

    def op(self, eng, fn, reads=(), writes=(), inc=True):
        waits = self._deps(eng, reads, writes)
        if inc:
            self.cnt[eng] += 1
            tok = (eng, self.cnt[eng])
            self.pending[eng] = False
        else:
            tok = (eng, self.cnt[eng] + 1)
            self.pending[eng] = True
        self.streams[eng].append((waits, fn, tok, 1 if inc else 0))
        for b in writes:
            b.w = tok
            b.r = {}
        for b in reads:
            if b.w is not tok:
                if b.r.get(eng, 0) < tok[1]:
                    b.r[eng] = tok[1]
        return tok

    def dma(self, fn, reads=(), writes=(), q="sp"):
        j = self.dnext
        self.dnext = (self.dnext + 1) % len(self.dsem)
        prev = (("d", j), 16 * self.dcnt[j])
        waits = self._deps(q, reads, writes, extra=(prev,) if self.dcnt[j] else ())
        self.dcnt[j] += 1
        tok = (("d", j), 16 * self.dcnt[j])
        self.streams[q].append((waits, fn, tok, 16))
        for b in writes:
            b.w = tok
            b.r = {}
        for b in reads:
            if b.w is not tok:
                b.r[tok[0]] = tok[1]
        return tok

    def wait_all(self, eng, toks):
        waits = self._deps(eng, (), (), extra=toks)
        if waits:
            self.streams[eng].append((waits, None, None, 0))

    def barrier(self):
        for e in self.COMPUTE:
            assert not self.pending[e], "pending un-incremented op on " + e
        toks = [(e, self.cnt[e]) for e in self.COMPUTE if self.cnt[e]]
        toks += [(("d", j), 16 * self.dcnt[j]) for j in range(len(self.dsem)) if self.dcnt[j]]
        for e in self.ALL:
            self.wait_all(e, toks)

    def emit(self, block):
        def mk(ename):
            def body(eng):
                for waits, fn, tok, inc in self.streams[ename]:
                    for k, v in waits:
                        eng.wait_ge(self._semof(k), v)
                    if fn is None:
                        continue
                    inst = fn(eng)
                    if inc:
                        inst.then_inc(self._semof(tok[0]), inc)
            return body

        block.tensor(mk("pe"))
        block.scalar(mk("act"))
        block.vector(mk("dve"))
        block.gpsimd(mk("pool"))
        block.sync(mk("sp"))


def rev_ap(ap):
    dims = [list(d) for d in ap.ap]
    fs, fc = dims[-1]
    dims[-1] = [-fs, fc]
    return bass.AP(ap.tensor, ap.offset + (fc - 1) * fs, dims)


def build(dbg=(), stop_after=None, n_layers=2):
    nc = bass.Bass("TRN2", target_bir_lowering=False)

    def din(name, shape, dt=F32):
        return nc.dram_tensor(name, list(shape), dt, kind="ExternalInput").ap()

    x_d = din("x", [LL, D])
    ctx_d = din("ctx", [LC, D])
    c2_d = din("c2", [128, 8, 2])
    wmod_d = din("w_mod", [2, D, 6144])
    win_d = din("w_in", [2, D, 2568])
    wout_d = din("w_out", [2, D, D])
    w1_d = din("w_ffn1", [2, D, 4096])
    w2_d = din("w_ffn2", [2, 4096, D])
    pp_d = din("pp", [2, 128, NPP])
    rp_d = din("rp", [2, NRP])
    lruw_d = din("lruw", [2, 2, 2, 2, 128, 128])
    cst_d = din("cst", [NCST, 128, 128])
    rope_d = din("rope", [2, 128, LL])
    out_d = nc.dram_tensor("out", [LL, D], F32, kind="ExternalOutput").ap()

    def scratch(name, shape, dt):
        kind = "ExternalOutput" if name in dbg else "Internal"
        return nc.dram_tensor(name, list(shape), dt, kind=kind).ap()

    xT_d = scratch("xT", [8, 128, T], F32)
    PTf_d = scratch("PTf", [18, 128, T], F32)
    PTt_d = scratch("PTt", [T, NTM], F32)
    YT_d = scratch("YT", [8, 128, T], BF16)
    MOD_d = scratch("MODd", [2, 128, 96], F32) if "MODd" in dbg else None

    es = ExitStack()
    with es:
        S = Sched(nc, es)

        _uid = [0]

        def sb(st, name, shape, dt=F32):
            _uid[0] += 1
            return st.enter_context(nc.sbuf_tensor("%s_%d" % (name, _uid[0]), list(shape), dt))

        def MM(out, lhsT, rhs, start, stop, r, w, inc=True):
            S.op("pe", lambda e: e.matmul(out, lhsT, rhs, start=start, stop=stop), r, w, inc=inc)

        def TR(out, in_, ident, r, w):
            S.op("pe", lambda e: e.transpose(out, in_, ident), r, w)

        def ACT(out, in_, func, r, w, bias=None, scale=None):
            kw = {}
            if bias is not None:
                kw["bias"] = bias
            if scale is not None:
                kw["scale"] = scale
            S.op("act", lambda e: e.activation(out=out, in_=in_, func=func, **kw), r, w)

        def TT(eng, out, in0, in1, op, r, w):
            S.op(eng, lambda e: e.tensor_tensor(out=out, in0=in0, in1=in1, op=op), r, w)

        def TS(eng, out, in0, s1, s2, op0, op1, r, w):
            if s2 is None:
                S.op(eng, lambda e: e.tensor_scalar(out=out, in0=in0, scalar1=s1, scalar2=None, op0=op0), r, w)
            else:
                S.op(eng, lambda e: e.tensor_scalar(out=out, in0=in0, scalar1=s1, scalar2=s2, op0=op0, op1=op1), r, w)

        def STT(out, in0, scalar, in1, op0, op1, r, w):
            S.op("dve", lambda e: e.scalar_tensor_tensor(out=out, in0=in0, scalar=scalar, in1=in1, op0=op0, op1=op1), r, w)

        def CP(eng, out, in_, r, w):
            if eng == "act":
                S.op("act", lambda e: e.activation(out=out, in_=in_, func=AF.Copy), r, w)
            else:
                S.op(eng, lambda e: e.tensor_copy(out=out, in_=in_), r, w)

        def RECIP(out, in_, r, w):
            S.op("dve", lambda e: e.reciprocal(out=out, in_=in_), r, w)

        def MEMSET(eng, ap, val, w):
            S.op(eng, lambda e: e.memset(ap, val), (), w)

        def DMA(out, in_, r, w, q="sp"):
            return S.dma(lambda e: e.dma_start(out=out, in_=in_), r, w, q=q)

        PS = [es.enter_context(nc.psum_tensor("ps%d" % i, [128, 512], F32)) for i in range(8)]
        bPS = [Buf("ps%d" % i) for i in range(8)]
        CST = sb(es, "CST", [128, NCST, 128])
        bCST = Buf("CST")
        DMA(CST[:], cst_d.rearrange("c p f -> p c f"), (), [bCST])
        CSTb = sb(es, "CSTb", [128, NCST, 128], BF16)
        bCSTb = Buf("CSTb")
        CP("dve", CSTb[:], CST[:], [bCST], [bCSTb])
        ident = CST[:, C_ID, :]
        identb = CSTb[:, C_ID, :]
        CS = sb(es, "CS", [128, 8, 2])
        bCS = Buf("CS")
        DMA(CS[:], c2_d, (), [bCS])
        ACT(CS[:], CS[:], AF.Silu, [bCS], [bCS])
        MODS = sb(es, "MODS", [128, 48, 2])
        bMODS = Buf("MODS")
        G1 = sb(es, "G1", [128, 8, 2])
        G2 = sb(es, "G2", [128, 8, 2])
        bG = Buf("G12")
        PP = sb(es, "PP", [128, NPP])
        bPP = Buf("PP")
        RP = sb(es, "RP", [128, NRP])
        bRP = Buf("RP")
        final_toks = []

        def phase_T():
            with ExitStack() as st:
                XK = [sb(st, "XK%d" % i, [128, D]) for i in range(2)]
                bXK = [Buf("XK%d" % i) for i in range(2)]
                XS = [sb(st, "XS%d" % i, [128, 8, 512]) for i in range(2)]
                bXS = [Buf("XSs%d" % i) for i in range(2)]
                it = 0
                for gi, (t0, n, which) in enumerate(GROUPS):
                    xs, bxs = XS[gi % 2], bXS[gi % 2]
                    for s in range(n // 128):
                        xk, bxk = XK[it % 2], bXK[it % 2]
                        tt = t0 + s * 128
                        src = ctx_d[tt:tt + 128, :] if which else x_d[tt - LC:tt - LC + 128, :]
                        DMA(xk[:], src, (), [bxk])
                        for half in range(2):
                            p, bp = PS[(it * 2 + half) % 8], bPS[(it * 2 + half) % 8]
                            for jj in range(4):
                                j = half * 4 + jj
                                TR(p[:, jj * 128:(jj + 1) * 128], xk[:, j * 128:(j + 1) * 128], ident, [bxk, bCST], [bp])
                            CP("act" if half else "dve", xs[:, half * 4:half * 4 + 4, s * 128:(s + 1) * 128],
                               p[:].rearrange("p (j t) -> p j t", j=4), [bp], [bxs])
                        it += 1
                    DMA(xT_d[:, :, t0:t0 + n].rearrange("j p t -> p j t"), xs[:, :, 0:n], [bxs], ())
            S.barrier()

        def phase_0(l):
            with ExitStack() as st:
                DMA(PP[:], pp_d[l], (), [bPP])
                DMA(RP[:], rp_d[l:l + 1, :].partition_broadcast(128) if False else rp_d[l].partition_broadcast(128), (), [bRP])
                WM = [sb(st, "WM%d" % i, [128, 8, 1024]) for i in range(2)]
                bWM = [Buf("WM%d" % i) for i in range(2)]
                pm, bpm = PS[0], bPS[0]
                for sl in range(6):
                    wm, bwm = WM[sl % 2], bWM[sl % 2]
                    DMA(wm[:], wmod_d[l, :, sl * 1024:(sl + 1) * 1024].rearrange("(j p) c -> p j c", p=128), (), [bwm])
                    for oc8 in range(8):
                        oc = sl * 8 + oc8
                        for j in range(8):
                            MM(pm[:, oc * 2:oc * 2 + 2], wm[:, j, oc8 * 128:(oc8 + 1) * 128], CS[:, j, :],
                               j == 0, j == 7, [bwm, bCS], [bpm], inc=(j == 7))
                TT("dve", MODS[:], pm[:, 0:96].rearrange("p (o w) -> p o w", w=2),
                   PP[:, 16:64].unsqueeze(2).to_broadcast([128, 48, 2]), ALU.add, [bpm, bPP], [bMODS])
                for (G, gcol, sccol) in ((G1, 0, 8), (G2, 8, 32)):
                    TS("dve", G[:], MODS[:, sccol:sccol + 8, :], 1.0, None, ALU.add, None, [bMODS], [bG])
                    TT("dve", G[:], G[:], PP[:, gcol:gcol + 8].unsqueeze(2).to_broadcast([128, 8, 2]), ALU.mult, [bG, bPP], [bG])
                if MOD_d is not None:
                    final_toks.append(DMA(MOD_d[l], MODS[:].rearrange("p o w -> p (o w)"), [bMODS], ()))
            S.barrier()

        def phase_A(l):
            with ExitStack() as st:
                WF = sb(st, "WF", [128, 8, 18 * 128], BF16)
                WK = sb(st, "WK", [128, 8, NTM], BF16)
                bW = Buf("Win")
                for ci, (c0, ncol, dup) in enumerate(FM):
                    src = win_d[l, :, c0:c0 + ncol].rearrange("(j p) c -> p j c", p=128)
                    DMA(WF[:, :, ci * 128:ci * 128 + ncol], src, (), [bW], q="pool")
                    if dup:
                        DMA(WF[:, :, ci * 128 + 64:ci * 128 + 128], src, (), [bW], q="pool")
                off = 0
                for (c0, ncol) in TMC:
                    DMA(WK[:, :, off:off + ncol], win_d[l, :, c0:c0 + ncol].rearrange("(j p) c -> p j c", p=128), (), [bW], q="pool")
                    off += ncol
                XT = [sb(st, "XT%d" % i, [128, 8, 512]) for i in range(2)]
                bXT = [Buf("XT%d" % i) for i in range(2)]
                SQ = sb(st, "SQ", [128, 8, 512], BF16)
                bSQ = Buf("SQ")
                RS = sb(st, "RS", [128, 512])
                bRS = Buf("RS")
                HT = [sb(st, "HT%d" % i, [128, 8, 512], BF16) for i in range(2)]
                bHT = [Buf("HT%d" % i) for i in range(2)]
                FS = [sb(st, "FS%d" % i, [128, 512]) for i in range(4)]
                bFS = [Buf("FS%d" % i) for i in range(4)]
                ZS = [sb(st, "ZS%d" % i, [128, NTM]) for i in range(2)]
                bZS = [Buf("ZS%d" % i) for i in range(2)]
                fsi = 0
                zsi = 0
                pi = 0
                for gi, (t0, n, which) in enumerate(GROUPS):
                    xt, bxt = XT[gi % 2], bXT[gi % 2]
                    ht, bht = HT[gi % 2], bHT[gi % 2]
                    DMA(xt[:, :, 0:n], xT_d[:, :, t0:t0 + n].rearrange("j p t -> p j t"), (), [bxt])
                    ACT(SQ[:, :, 0:n], xt[:, :, 0:n], AF.Square, [bxt], [bSQ])
                    pss, bpss = PS[pi % 8], bPS[pi % 8]
                    pi += 1
                    for j in range(8):
                        MM(pss[:, 0:n], CSTb[:, C_ONE, :], SQ[:, j, 0:n], j == 0, j == 7, [bSQ, bCSTb], [bpss], inc=(j == 7))
                    ACT(RS[:, 0:n], pss[:, 0:n], AF.Sqrt, [bpss], [bRS], bias=EPS, scale=1.0 / D)
                    RECIP(RS[:, 0:n], RS[:, 0:n], [bRS], [bRS])
                    TT("dve", xt[:, :, 0:n], xt[:, :, 0:n], RS[:, 0:n].unsqueeze(1).to_broadcast([128, 8, n]), ALU.mult, [bxt, bRS], [bxt])
                    for j in range(8):
                        ACT(ht[:, j, 0:n], xt[:, j, 0:n], AF.Identity, [bxt, bG, bMODS], [bht],
                            bias=MODS[:, j, which:which + 1], scale=G1[:, j, which:which + 1])
                    for ci in range(18):
                        p, bp = PS[pi % 8], bPS[pi % 8]
                        pi += 1
                        for j in range(8):
                            MM(p[:, 0:n], WF[:, j, ci * 128:(ci + 1) * 128], ht[:, j, 0:n], j == 0, j == 7, [bW, bht], [bp], inc=(j == 7))
                        fs, bfs = FS[fsi % 4], bFS[fsi % 4]
                        CP("act" if fsi % 2 else "dve", fs[:, 0:n], p[:, 0:n], [bp], [bfs])
                        fsi += 1
                        DMA(PTf_d[ci, :, t0:t0 + n], fs[:, 0:n], [bfs], ())
                    for s in range(n // 128):
                        p, bp = PS[pi % 8], bPS[pi % 8]
                        p2, bp2 = PS[(pi + 1) % 8], bPS[(pi + 1) % 8]
                        pi += 2
                        for j in range(8):
                            MM(p[:, 0:512], ht[:, j, s * 128:(s + 1) * 128], WK[:, j, 0:512], j == 0, j == 7, [bW, bht], [bp], inc=(j == 7))
                        for j in range(8):
                            MM(p2[:, 0:8], ht[:, j, s * 128:(s + 1) * 128], WK[:, j, 512:520], j == 0, j == 7, [bW, bht], [bp2], inc=(j == 7))
                        zs, bzs = ZS[zsi % 2], bZS[zsi % 2]
                        zsi += 1
                        CP("act", zs[:, 0:512], p[:, 0:512], [bp], [bzs])
                        CP("dve", zs[:, 512:520], p2[:, 0:8], [bp2], [bzs])
                        tt = t0 + s * 128
                        DMA(PTt_d[tt:tt + 128, :], zs[:], [bzs], ())
            S.barrier()


        def conv_seg(out, inp, wc, bc, s0, L, r, w):
            S.op("dve", lambda e: e.tensor_scalar(out=out[:, s0:s0 + L], in0=inp[:, s0:s0 + L], scalar1=PP[:, wc + 2:wc + 3],
                                                   scalar2=PP[:, bc:bc + 1], op0=ALU.mult, op1=ALU.add), r, w)
            STT(out[:, s0 + 2:s0 + L], inp[:, s0:s0 + L - 2], PP[:, wc:wc + 1], out[:, s0 + 2:s0 + L], ALU.mult, ALU.add, r + w, w)
            STT(out[:, s0 + 1:s0 + L], inp[:, s0:s0 + L - 1], PP[:, wc + 1:wc + 2], out[:, s0 + 1:s0 + L], ALU.mult, ALU.add, r + w, w)
            STT(out[:, s0:s0 + L - 1], inp[:, s0 + 1:s0 + L], PP[:, wc + 3:wc + 4], out[:, s0:s0 + L - 1], ALU.mult, ALU.add, r + w, w)

        def conv_full(out, inp, wc, bc, r, w):
            conv_seg(out, inp, wc, bc, 0, LC, r, w)
            conv_seg(out, inp, wc, bc, LC, LL, r, w)

        def phase_B(l):
            with ExitStack() as st:
                LW = sb(st, "LW", [128, 8, 128])
                bLW = Buf("LW")
                DMA(LW[:], lruw_d[l].rearrange("d g c p f -> p (d g c) f"), (), [bLW])
                SC = sb(st, "SCl", [128, 8])
                bSC = Buf("SCl")
                ACT(SC[:, 0:4], PP[:, 104:108], AF.Exp, [bPP], [bSC], scale=-1.0)
                ACT(SC[:, 0:4], SC[:, 0:4], AF.Ln, [bSC], [bSC], bias=1.0)
                TS("dve", SC[:, 4:8], SC[:, 0:4], -16.0, None, ALU.mult, None, [bSC], [bSC])
                TS("dve", SC[:, 0:4], SC[:, 0:4], -8.0, None, ALU.mult, None, [bSC], [bSC])
                names = ["XR", "XC", "Rt", "IGt", "Mt", "HF", "HB"]
                tl = {n_: sb(st, "lru_" + n_, [128, T]) for n_ in names}
                bf = {n_: Buf("lru_" + n_) for n_ in names}
                YB = sb(st, "lru_YB", [128, T], BF16)
                bYB = Buf("lru_YB")
                pi = 0
                for c2 in range(2):
                    DMA(tl["XR"][:], PTf_d[I_R + c2], (), [bf["XR"]])
                    conv_full(tl["XC"], tl["XR"], 94 + c2 * 4, 102 + c2, [bf["XR"], bPP], [bf["XC"]])
                    DMA(tl["XR"][:], PTf_d[I_G + c2], (), [bf["XR"]])
                    for dr in range(2):
                        col = dr * 2 + c2
                        for (t0, n, which) in GROUPS:
                            for gt, dst, bcol in ((0, "Rt", 108), (1, "IGt", 112)):
                                p, bp = PS[pi % 8], bPS[pi % 8]
                                pi += 1
                                MM(p[:, 0:n], LW[:, dr * 4 + gt * 2 + c2, :], tl["XC"][:, t0:t0 + n], True, True, [bLW, bf["XC"]], [bp])
                                ACT(tl[dst][:, t0:t0 + n], p[:, 0:n], AF.Sigmoid, [bp, bPP], [bf[dst]], bias=PP[:, bcol + col:bcol + col + 1])
                        ACT(tl["Mt"][:], tl["Rt"][:], AF.Exp, [bf["Rt"], bSC], [bf["Mt"]], scale=SC[:, 4 + col:5 + col])
                        ACT(tl["Rt"][:], tl["Rt"][:], AF.Exp, [bf["Rt"], bSC], [bf["Rt"]], scale=SC[:, col:col + 1])
                        ACT(tl["Mt"][:], tl["Mt"][:], AF.Sqrt, [bf["Mt"]], [bf["Mt"]], bias=1.0, scale=-1.0)
                        TT("pool", tl["IGt"][:], tl["IGt"][:], tl["Mt"][:], ALU.mult, [bf["IGt"], bf["Mt"]], [bf["IGt"]])
                        TT("pool", tl["IGt"][:], tl["IGt"][:], tl["XC"][:], ALU.mult, [bf["IGt"], bf["XC"]], [bf["IGt"]])
                        H = tl["HF" if dr == 0 else "HB"]
                        bH = bf["HF" if dr == 0 else "HB"]
                        A_, B_ = tl["Rt"], tl["IGt"]
                        rw = ([bf["Rt"], bf["IGt"], bH], [bH])

                        def scan(s0, L, init, reverse):
                            o, a, b = H[:, s0:s0 + L], A_[:, s0:s0 + L], B_[:, s0:s0 + L]
                            if reverse:
                                o, a, b = rev_ap(o), rev_ap(a), rev_ap(b)
                            S.op("dve", lambda e: e.tensor_tensor_scan(out=o, data0=a, data1=b, initial=init, op0=ALU.mult, op1=ALU.add), rw[0], rw[1])

                        PIECE = 1024
                        if dr == 0:
                            scan(0, LC, 0.0, False)
                            for s0 in range(LC, T, PIECE):
                                scan(s0, PIECE, H[:, s0 - 1:s0], False)
                        else:
                            scan(0, LC, 0.0, True)
                            prev = H[:, 0:1]
                            for s0 in range(T - PIECE, LC - 1, -PIECE):
                                scan(s0, PIECE, prev, True)
                                prev = H[:, s0:s0 + 1]
                    TT("pool", tl["HF"][:], tl["HF"][:], tl["HB"][:], ALU.add, [bf["HF"], bf["HB"]], [bf["HF"]])
                    ACT(tl["XR"][:], tl["XR"][:], AF.Gelu, [bf["XR"]], [bf["XR"]])
                    TT("dve", YB[:], tl["HF"][:], tl["XR"][:], ALU.mult, [bf["HF"], bf["XR"]], [bYB])
                    DMA(YT_d[2 + c2], YB[:], [bYB], ())
            S.barrier()

        def phase_C(l):
            with ExitStack() as st:
                BTm = [sb(st, "BTm%d" % g, [128, T], BF16) for g in range(2)]
                CTm = [sb(st, "CTm%d" % g, [128, T], BF16) for g in range(2)]
                bBC = Buf("BCT")
                XTK = sb(st, "XTK", [128, NCH, 256])
                XTKb = sb(st, "XTKb", [128, NCH, 256], BF16)
                BTK = sb(st, "BTK", [128, NCH, 256], BF16)
                bTK = Buf("TK")
                EX = sb(st, "EX", [128, NCH, 40])
                DTt = sb(st, "DTt", [128, NCH, 8])
                LDT = sb(st, "LDT", [128, NCH, 8])
                DTA = sb(st, "DTA", [128, NCH, 8])
                WE = sb(st, "WE", [128, NCH, 8])
                A8 = sb(st, "A8", [128, 8])
                bSM = Buf("ssd_small")
                with ExitStack() as st2:
                    XR = [sb(st2, "cXR%d" % i, [128, T]) for i in range(2)]
                    bXR = [Buf("cXR%d" % i) for i in range(2)]
                    XC = sb(st2, "cXC", [128, T])
                    bXC = Buf("cXC")
                    XS = [sb(st2, "cXS%d" % i, [128, T]) for i in range(2)]
                    bXS = Buf("cXS")
                    for ci in range(6):
                        xr, bxr = XR[ci % 2], bXR[ci % 2]
                        DMA(xr[:], PTf_d[I_X + ci], (), [bxr])
                        conv_full(XC, xr, 64 + ci * 4, 88 + ci, [bxr, bPP], [bXC])
                        if ci < 2:
                            ACT(XS[ci][:], XC[:], AF.Silu, [bXC], [bXS])
                        elif ci < 4:
                            ACT(BTm[ci - 2][:], XC[:], AF.Silu, [bXC], [bBC])
                        else:
                            ACT(CTm[ci - 4][:], XC[:], AF.Silu, [bXC], [bBC])
                    for c in range(NCH):
                        tt = c * 128
                        p, bp = PS[(2 * c) % 8], bPS[(2 * c) % 8]
                        pb, bpb = PS[(2 * c + 1) % 8], bPS[(2 * c + 1) % 8]
                        for ci in range(2):
                            TR(p[:, ci * 128:(ci + 1) * 128], XS[ci][:, tt:tt + 128], ident, [bXS, bCST], [bp])
                        CP("act", XTK[:, c, :], p[:, 0:256], [bp], [bTK])
                        CP("dve", XTKb[:, c, :], p[:, 0:256], [bp], [bTK])
                        pbv = pb[:].bitcast(BF16)
                        for g in range(2):
                            TR(pbv[:, g * 128:(g + 1) * 128], BTm[g][:, tt:tt + 128], identb, [bBC, bCSTb], [bpb])
                        CP("dve", BTK[:, c, :], pbv[:, 0:256], [bpb], [bTK])
                        DMA(DTt[:, c, :], PTt_d[tt:tt + 128, 256:264], (), [bSM])
                S.barrier()
                TT("dve", DTt[:], DTt[:], RP[:, 260:268].unsqueeze(1).to_broadcast([128, NCH, 8]), ALU.add, [bSM, bRP], [bSM])
                ACT(DTt[:], DTt[:], AF.Exp, [bSM], [bSM])
                ACT(DTt[:], DTt[:], AF.Ln, [bSM], [bSM], bias=1.0)
                ACT(LDT[:], DTt[:], AF.Ln, [bSM], [bSM])
                ACT(A8[:], RP[:, 268:276], AF.Exp, [bRP], [bSM])
                STT(DTA[:], DTt[:], -1.0, A8[:].unsqueeze(1).to_broadcast([128, NCH, 8]), ALU.mult, ALU.mult, [bSM], [bSM])
                for c0 in range(0, NCH, 12):
                    nb = min(12, NCH - c0)
                    p, bp = PS[(c0 // 12) % 8], bPS[(c0 // 12) % 8]
                    for cc in range(nb):
                        for k, cm in enumerate((C_LT, C_UT, C_LE, C_GE, C_ONE)):
                            MM(p[:, cc * 40 + k * 8:cc * 40 + k * 8 + 8], CST[:, cm, :], DTA[:, c0 + cc, :], True, True,
                               [bCST, bSM], [bp], inc=(cc == nb - 1 and k == 4))
                    ACT(EX[:, c0:c0 + nb, :], p[:, 0:nb * 40].rearrange("p (c k) -> p c k", k=40), AF.Exp, [bp], [bSM])
                TT("dve", WE[:, :, 0:4], EX[:, :, 0:4], DTt[:, :, 0:4], ALU.mult, [bSM], [bSM])
                TT("dve", WE[:, :, 4:8], EX[:, :, 12:16], DTt[:, :, 4:8], ALU.mult, [bSM], [bSM])

                def bc4(ap):
                    return ap.unsqueeze(2).to_broadcast([128, 4, 64])

                def v4(ap):
                    return ap.rearrange("p (h d) -> p h d", h=4)

                HBall = sb(st, "HBall", [128, NCH, 256], BF16)
                bHBall = Buf("HBall")
                Hs = sb(st, "Hs", [128, 256])
                bHs = Buf("Hs")
                XSB = [sb(st, "XSB%d" % i, [128, 256], BF16) for i in range(2)]
                bXSB = [Buf("XSB%d" % i) for i in range(2)]
                MEMSET("pool", Hs[:], 0.0, [bHs])
                order = [1, 0] + list(range(NCH - 1, 1, -1))
                for it, c in enumerate(order):
                    xsb, bxsb = XSB[it % 2], bXSB[it % 2]
                    CP("pool", HBall[:, c, :], Hs[:], [bHs], [bHBall])
                    TT("dve", v4(xsb[:]), v4(XTK[:, c, :]), bc4(WE[:, c, 4:8]), ALU.mult, [bTK, bSM], [bxsb])
                    p, bp = PS[it % 4], bPS[it % 4]
                    for g in range(2):
                        MM(p[:, g * 128:(g + 1) * 128], BTK[:, c, g * 128:(g + 1) * 128], xsb[:, g * 128:(g + 1) * 128], True, True,
                           [bTK, bxsb], [bp], inc=(g == 1))
                    TT("pool", v4(Hs[:]), v4(Hs[:]), bc4(EX[:, c, 36:40]), ALU.mult, [bHs, bSM], [bHs])
                    TT("dve", Hs[:], Hs[:], p[:, 0:256], ALU.add, [bHs, bp], [bHs])
                RFB = [sb(st, "RFB%d" % i, [128, 8, 128]) for i in range(2)]
                bRFB = [Buf("RFB%d" % i) for i in range(2)]
                Dx = sb(st, "Dx", [128, 8, 128])
                bDx = Buf("Dx")
                GTs = sb(st, "GTs", [128, 2, 128])
                bGTs = Buf("GTs")
                DS = sb(st, "DS", [128, 4, 128])
                bDS = Buf("DS")
                MTb = sb(st, "MTb", [128, 4, 128], BF16)
                bMTb = Buf("MTb")
                Hbf = sb(st, "Hbf", [128, 256], BF16)
                bHbf = Buf("Hbf")
                t1 = sb(st, "ct1", [128, 256])
                t2 = sb(st, "ct2", [128, 256])
                t3 = sb(st, "ct3", [128, 256])
                bt1, bt2, bt3 = Buf("ct1"), Buf("ct2"), Buf("ct3")
                ZD = [sb(st, "ZD%d" % i, [128, 256]) for i in range(2)]
                bZD = [Buf("ZD%d" % i) for i in range(2)]
                ssum = sb(st, "ssum", [128, 2])
                bss = Buf("ssum")
                YN = sb(st, "YN", [128, 256])
                bYN = Buf("YN")
                YAT = sb(st, "YAT", [128, 2, T], BF16)
                bYAT = Buf("YAT")
                MEMSET("pool", Hs[:], 0.0, [bHs])

                def build_rfb(c):
                    rfb, brfb = RFB[c % 2], bRFB[c % 2]
                    for h in range(4):
                        TS("pool", rfb[:, h, :], CST[:, C_LE, :], DTA[:, c, h:h + 1], 1.0, ALU.mult, ALU.mult, [bCST, bSM], [brfb])
                        TS("pool", rfb[:, 4 + h, :], CST[:, C_GE, :], DTA[:, c, 4 + h:5 + h], 1.0, ALU.mult, ALU.mult, [bCST, bSM], [brfb])

                for c in range(NCH):
                    tt = c * 128
                    rfb, brfb = RFB[c % 2], bRFB[c % 2]
                    zd, bzd = ZD[c % 2], bZD[c % 2]
                    DMA(zd[:], PTt_d[tt:tt + 128, 0:256], (), [bzd])
                    CP("pool", Hbf[:], Hs[:], [bHs], [bHbf])
                    for g in range(2):
                        MM(PS[0][:, g * 128:(g + 1) * 128], BTm[g][:, tt:tt + 128], CTm[g][:, tt:tt + 128], True, True, [bBC], [bPS[0]], inc=(g == 1))
                    CP("act", GTs[:], PS[0][:, 0:256].rearrange("p (g i) -> p g i", g=2), [bPS[0]], [bGTs])
                    if c == 0:
                        build_rfb(0)
                    for h in range(4):
                        MM(PS[1][:, h * 128:(h + 1) * 128], CST[:, C_LT, :], rfb[:, h, :], True, False, [bCST, brfb], [bPS[1]], inc=False)
                        MM(PS[1][:, h * 128:(h + 1) * 128], ident, CST[:, C_NEGF, :], False, True, [bCST], [bPS[1]], inc=(h == 3))
                    for h in range(4):
                        MM(PS[2][:, h * 128:(h + 1) * 128], CST[:, C_UT, :], rfb[:, 4 + h, :], True, False, [bCST, brfb], [bPS[2]], inc=False)
                        MM(PS[2][:, h * 128:(h + 1) * 128], ident, CST[:, C_NEGB, :], False, True, [bCST], [bPS[2]], inc=(h == 3))
                    if c + 1 < NCH:
                        build_rfb(c + 1)
                    for dh in range(8):
                        src = PS[1 + dh // 4]
                        h = dh % 4
                        ACT(Dx[:, dh, :], src[:, h * 128:(h + 1) * 128], AF.Exp, [bPS[1 + dh // 4], bSM], [bDx], bias=LDT[:, c, dh:dh + 1])
                    TT("pool", DS[:], Dx[:, 0:4, :], Dx[:, 4:8, :], ALU.add, [bDx], [bDS])
                    for g in range(2):
                        TT("dve", MTb[:, 2 * g:2 * g + 2, :], DS[:, 2 * g:2 * g + 2, :], GTs[:, g, :].unsqueeze(1).to_broadcast([128, 2, 128]),
                           ALU.mult, [bDS, bGTs], [bMTb])
                    for h in range(4):
                        MM(PS[3][:, h * 64:(h + 1) * 64], MTb[:, h, :], XTKb[:, c, h * 64:(h + 1) * 64], True, True, [bMTb, bTK], [bPS[3]], inc=(h == 3))
                    for g in range(2):
                        MM(PS[4][:, g * 128:(g + 1) * 128], CTm[g][:, tt:tt + 128], Hbf[:, g * 128:(g + 1) * 128], True, True, [bBC, bHbf], [bPS[4]], inc=False)
                        MM(PS[4][:, 256 + g * 128:256 + (g + 1) * 128], CTm[g][:, tt:tt + 128], HBall[:, c, g * 128:(g + 1) * 128], True, True,
                           [bBC, bHBall], [bPS[4]], inc=(g == 1))
                    TT("dve", v4(t1[:]), v4(PS[4][:, 0:256]), bc4(EX[:, c, 16:20]), ALU.mult, [bPS[4], bSM], [bt1])
                    TT("dve", v4(t2[:]), v4(PS[4][:, 256:512]), bc4(EX[:, c, 28:32]), ALU.mult, [bPS[4], bSM], [bt2])
                    TT("pool", t1[:], t1[:], t2[:], ALU.add, [bt1, bt2], [bt1])
                    TT("dve", t1[:], t1[:], PS[3][:, 0:256], ALU.add, [bt1, bPS[3]], [bt1])
                    TT("pool", v4(t3[:]), v4(XTK[:, c, :]), bc4(RP[:, 256:260]), ALU.mult, [bTK, bRP], [bt3])
                    TT("pool", t1[:], t1[:], t3[:], ALU.add, [bt1, bt3], [bt1])
                    ACT(t2[:], zd[:], AF.Silu, [bzd], [bt2])
                    TT("dve", t1[:], t1[:], t2[:], ALU.mult, [bt1, bt2], [bt1])
                    TT("pool", t3[:], t1[:], t1[:], ALU.mult, [bt1], [bt3])
                    S.op("dve", lambda e: e.tensor_reduce(out=ssum[:, 0:1], in_=t3[:], axis=AX.X, op=ALU.add), [bt3], [bss])
                    ACT(ssum[:, 1:2], ssum[:, 0:1], AF.Sqrt, [bss], [bss], bias=EPS, scale=1.0 / 256)
                    RECIP(ssum[:, 1:2], ssum[:, 1:2], [bss], [bss])
                    STT(YN[:], t1[:], ssum[:, 1:2], RP[:, 0:256], ALU.mult, ALU.mult, [bt1, bss, bRP], [bYN])
                    for ci in range(2):
                        TR(PS[6][:, ci * 128:(ci + 1) * 128], YN[:, ci * 128:(ci + 1) * 128], ident, [bYN, bCST], [bPS[6]])
                    CP("act", YAT[:, :, tt:tt + 128], PS[6][:, 0:256].rearrange("p (j t) -> p j t", j=2), [bPS[6]], [bYAT])
                    xsb, bxsb = XSB[c % 2], bXSB[c % 2]
                    TT("dve", v4(xsb[:]), v4(XTK[:, c, :]), bc4(WE[:, c, 0:4]), ALU.mult, [bTK, bSM], [bxsb])
                    for g in range(2):
                        MM(PS[5][:, g * 128:(g + 1) * 128], BTK[:, c, g * 128:(g + 1) * 128], xsb[:, g * 128:(g + 1) * 128], True, True,
                           [bTK, bxsb], [bPS[5]], inc=(g == 1))
                    TT("pool", v4(Hs[:]), v4(Hs[:]), bc4(EX[:, c, 32:36]), ALU.mult, [bHs, bSM], [bHs])
                    TT("dve", Hs[:], Hs[:], PS[5][:, 0:256], ALU.add, [bHs, bPS[5]], [bHs])
                DMA(YT_d[0:2].rearrange("j p t -> p j t"), YAT[:], [bYAT], ())
            S.barrier()


        def phase_D(l, need_ctx, units=((0, 0), (0, 1), (1, 0), (1, 1))):
            with ExitStack() as st:
                QG = sb(st, "QG", [128, 4])
                ESK = sb(st, "ESK", [128, 4])
                bQG = Buf("QG")
                TS("dve", QG[:, 0:1], PP[:, 116:117], 0.125, None, ALU.mult, None, [bPP], [bQG])
                TS("dve", QG[:, 2:3], PP[:, 118:119], 0.125, None, ALU.mult, None, [bPP], [bQG])
                CP("dve", QG[:, 1:2], PP[:, 117:118], [bPP], [bQG])
                CP("dve", QG[:, 3:4], PP[:, 119:120], [bPP], [bQG])
                ACT(ESK[:], RP[:, 276:280], AF.Exp, [bRP], [bQG])
                NSET = 2
                QTs = [sb(st, "QT%d" % i, [128, T], BF16) for i in range(NSET)]
                KZs = [[sb(st, "KZ%d%d" % (i, g), [128, T], BF16) for g in range(2)] for i in range(NSET)]
                VAs = [sb(st, "VA%d" % i, [128, NCH, 128], BF16) for i in range(NSET)]
                YOs = [sb(st, "YO%d" % i, [128, T], BF16) for i in range(NSET)]
                bQKs = [Buf("QK%d" % i) for i in range(NSET)]
                bVAs = [Buf("VA%d" % i) for i in range(NSET)]
                bYOs = [Buf("YO%d" % i) for i in range(NSET)]
                for i in range(NSET):
                    for g in range(2):
                        MEMSET("pool", KZs[i][g][:], 0.0, [bQKs[i]])
                    MEMSET("pool", VAs[i][:, :, 64:128], 1.0, [bVAs[i]])
                XR = [sb(st, "dXR%d" % i, [128, 512]) for i in range(2)]
                bXR = [Buf("dXR%d" % i) for i in range(2)]
                CSs = [sb(st, "dCS%d" % i, [128, 2, 512]) for i in range(2)]
                bCSs = [Buf("dCS%d" % i) for i in range(2)]
                SQ = sb(st, "dSQ", [128, 512], BF16)
                bSQ = Buf("dSQ")
                RSq = sb(st, "dRS", [128, 512])
                bRSq = Buf("dRS")
                QN = sb(st, "dQN", [128, 512])
                bQN = Buf("dQN")
                TA = sb(st, "dTA", [128, 512])
                TB = sb(st, "dTB", [128, 512])
                bTA, bTB = Buf("dTA"), Buf("dTB")
                VS = sb(st, "dVS", [128, NCH, 64])
                bVS = Buf("dVS")
                PTs = [sb(st, "PTs%d" % i, [128, 640], BF16) for i in range(3)]
                bPTs = [Buf("PTs%d" % i) for i in range(3)]
                RD = sb(st, "RD", [128, 512])
                bRD = Buf("RD")
                cnt = {"pi": 0, "xi": 0, "oi": 0}
                LA = 2

                def run_pipe(steps):
                    n_ = len(steps)
                    for i in range(n_ + LA):
                        if i < n_:
                            steps[i][0]()
                        if i >= LA:
                            steps[i - LA][1]()

                def prep(ty, hk, QT, KZ, VA, bQK, bVA):
                    base = I_CQ if ty == 0 else I_DQ
                    for kind in range(2):
                        ci = base + (hk if kind == 0 else 2 + hk)
                        gcol = ty * 2 + kind
                        for (t0, n, which) in GROUPS:
                            xr, bxr = XR[cnt["xi"] % 2], bXR[cnt["xi"] % 2]
                            cs, bcs = CSs[cnt["xi"] % 2], bCSs[cnt["xi"] % 2]
                            cnt["xi"] += 1
                            DMA(xr[:, 0:n], PTf_d[ci, :, t0:t0 + n], (), [bxr])
                            if not which:
                                DMA(cs[:, :, 0:n], rope_d[:, :, t0 - LC:t0 - LC + n].rearrange("c p t -> p c t"), (), [bcs])
                            ACT(SQ[:, 0:n], xr[:, 0:n], AF.Square, [bxr], [bSQ])
                            p, bp = PS[cnt["pi"] % 6], bPS[cnt["pi"] % 6]
                            cnt["pi"] += 1
                            MM(p[:, 0:n], CSTb[:, C_BD, :], SQ[:, 0:n], True, True, [bCSTb, bSQ], [bp])
                            ACT(RSq[:, 0:n], p[:, 0:n], AF.Sqrt, [bp], [bRSq], bias=EPS, scale=1.0 / 64)
                            RECIP(RSq[:, 0:n], RSq[:, 0:n], [bRSq], [bRSq])

                            def emit_out(fn):
                                if kind == 0:
                                    fn(lambda rs: QT[rs, t0:t0 + n], slice(0, 128))
                                else:
                                    fn(lambda rs: KZ[0][rs, t0:t0 + n], slice(0, 64))
                                    fn(lambda rs: KZ[1][rs, t0:t0 + n], slice(64, 128))

                            if which:
                                emit_out(lambda dst, rs: STT(dst(rs), xr[rs, 0:n], QG[rs, gcol:gcol + 1], RSq[rs, 0:n], ALU.mult, ALU.mult,
                                                             [bxr, bQG, bRSq], [bQK]))
                            else:
                                STT(QN[:, 0:n], xr[:, 0:n], QG[:, gcol:gcol + 1], RSq[:, 0:n], ALU.mult, ALU.mult, [bxr, bQG, bRSq], [bQN])
                                p2, bp2 = PS[cnt["pi"] % 6], bPS[cnt["pi"] % 6]
                                cnt["pi"] += 1
                                MM(p2[:, 0:n], CST[:, C_PERM, :], QN[:, 0:n], True, True, [bCST, bQN], [bp2])
                                TT("pool", TA[:, 0:n], QN[:, 0:n], cs[:, 0, 0:n], ALU.mult, [bQN, bcs], [bTA])
                                TT("dve", TB[:, 0:n], p2[:, 0:n], cs[:, 1, 0:n], ALU.mult, [bp2, bcs], [bTB])
                                emit_out(lambda dst, rs: TT("pool", dst(rs), TA[rs, 0:n], TB[rs, 0:n], ALU.add, [bTA, bTB], [bQK]))
                    col0 = 264 + ty * 128 + hk * 64
                    src = PTt_d[:, col0:col0 + 64].rearrange("(c p) f -> p c f", p=128)
                    DMA(VS[:, 0:17, :], src[:, 0:17, :], (), [bVS])
                    DMA(VS[:, 17:34, :], src[:, 17:34, :], (), [bVS])
                    CP("dve", VA[:, :, 0:64], VS[:], [bVS], [bVA])

                def attend_global(hk, QT, KZ, VA, YO, bQK, bVA, bYO):
                    qgroups = ([(0, LC, [0, 1])] if need_ctx else []) + [(t0, n, list(range(NCH))) for (t0, n, w_) in GROUPS[1:]]
                    steps = []
                    for g in range(2):
                        rows = slice(g * 64, (g + 1) * 64)
                        for (q0, nq, keys) in qgroups:
                            O, bO = PS[6 + cnt["oi"] % 2], bPS[6 + cnt["oi"] % 2]
                            cnt["oi"] += 1
                            for ki, kc in enumerate(keys):
                                si = len(steps)
                                Sp, bSp = PS[si % 3], bPS[si % 3]
                                pt, bpt = PTs[si % 3], bPTs[si % 3]

                                def front(Sp=Sp, bSp=bSp, pt=pt, bpt=bpt, g=g, kc=kc, q0=q0, nq=nq):
                                    MM(Sp[:, 0:nq], KZ[g][:, kc * 128:(kc + 1) * 128], QT[:, q0:q0 + nq], True, True, [bQK], [bSp])
                                    ACT(pt[:, 0:nq], Sp[:, 0:nq], AF.Exp, [bSp], [bpt])

                                def back(O=O, bO=bO, pt=pt, bpt=bpt, rows=rows, kc=kc, q0=q0, nq=nq, ki=ki, nk=len(keys)):
                                    MM(O[:, 0:nq], VA[:, kc, :], pt[:, 0:nq], ki == 0, ki == nk - 1, [bVA, bpt], [bO], inc=(ki == nk - 1))
                                    if ki == nk - 1:
                                        RECIP(RD[64:128, 0:nq], O[64:128, 0:nq], [bO], [bRD])
                                        TT("dve", YO[rows, q0:q0 + nq], O[0:64, 0:nq], RD[64:128, 0:nq], ALU.mult, [bO, bRD], [bYO])

                                steps.append((front, back))
                    run_pipe(steps)

                def attend_window(hk, QT, KZ, VA, YO, bQK, bVA, bYO):
                    def norm_d(O, bO, rows, h, q0, nq):
                        TS("dve", RD[64:128, 0:nq], O[64:128, 0:nq], ESK[64:128, h:h + 1], None, ALU.add, None, [bO, bQG], [bRD])
                        RECIP(RD[64:128, 0:nq], RD[64:128, 0:nq], [bRD], [bRD])
                        TT("dve", YO[rows, q0:q0 + nq], O[0:64, 0:nq], RD[64:128, 0:nq], ALU.mult, [bO, bRD], [bYO])

                    steps = []
                    for g in range(2):
                        h = 2 * hk + g
                        rows = slice(g * 64, (g + 1) * 64)
                        if need_ctx:
                            O, bO = PS[6 + cnt["oi"] % 2], bPS[6 + cnt["oi"] % 2]
                            cnt["oi"] += 1
                            si = len(steps)
                            SA, bSA = PS[(2 * si) % 6], bPS[(2 * si) % 6]
                            SB, bSB = PS[(2 * si + 1) % 6], bPS[(2 * si + 1) % 6]
                            pt, bpt = PTs[si % 3], bPTs[si % 3]

                            def front(SA=SA, bSA=bSA, SB=SB, bSB=bSB, pt=pt, bpt=bpt, g=g):
                                for kc, (Sp, bSp) in enumerate(((SA, bSA), (SB, bSB))):
                                    MM(Sp[:, 0:LC], KZ[g][:, kc * 128:(kc + 1) * 128], QT[:, 0:LC], True, True, [bQK], [bSp])
                                    ACT(pt[:, kc * 256:(kc + 1) * 256], Sp[:, 0:LC], AF.Exp, [bSp], [bpt])

                            def back(O=O, bO=bO, pt=pt, bpt=bpt, rows=rows, h=h):
                                for kc in range(2):
                                    MM(O[:, 0:LC], VA[:, kc, :], pt[:, kc * 256:(kc + 1) * 256], kc == 0, kc == 1, [bVA, bpt], [bO], inc=(kc == 1))
                                norm_d(O, bO, rows, h, 0, LC)

                            steps.append((front, back))
                        for (t0, n, w_) in GROUPS[1:]:
                            O, bO = PS[6 + cnt["oi"] % 2], bPS[6 + cnt["oi"] % 2]
                            cnt["oi"] += 1
                            for blk in range(4):
                                nb = (t0 - LC) // 128 + blk
                                qb = t0 + blk * 128
                                lat = []
                                if nb > 0:
                                    lat.append((2 + nb - 1, C_NEGB))
                                lat.append((2 + nb, None))
                                if nb < 31:
                                    lat.append((2 + nb + 1, C_NEGF))
                                si = len(steps)
                                SA, bSA = PS[(2 * si) % 6], bPS[(2 * si) % 6]
                                SB, bSB = PS[(2 * si + 1) % 6], bPS[(2 * si + 1) % 6]
                                pt, bpt = PTs[si % 3], bPTs[si % 3]

                                def front(SA=SA, bSA=bSA, SB=SB, bSB=bSB, pt=pt, bpt=bpt, g=g, lat=lat, qb=qb):
                                    for ii, (kc, mk) in enumerate(lat):
                                        last = (ii == len(lat) - 1)
                                        MM(SA[:, ii * 128:(ii + 1) * 128], KZ[g][:, kc * 128:(kc + 1) * 128], QT[:, qb:qb + 128],
                                           True, mk is None, [bQK], [bSA], inc=(last and mk is None))
                                        if mk is not None:
                                            MM(SA[:, ii * 128:(ii + 1) * 128], identb, CSTb[:, mk, :], False, True, [bCSTb], [bSA], inc=last)
                                    for kc in range(2):
                                        MM(SB[:, kc * 128:(kc + 1) * 128], KZ[g][:, kc * 128:(kc + 1) * 128], QT[:, qb:qb + 128],
                                           True, True, [bQK], [bSB], inc=(kc == 1))
                                    nl = len(lat)
                                    ACT(pt[:, 0:nl * 128], SA[:, 0:nl * 128], AF.Exp, [bSA], [bpt])
                                    ACT(pt[:, 384:640], SB[:, 0:256], AF.Exp, [bSB], [bpt])

                                def back(O=O, bO=bO, pt=pt, bpt=bpt, rows=rows, lat=lat, blk=blk, h=h, t0=t0, n=n):
                                    kvs = [(kc, ii * 128) for ii, (kc, mk) in enumerate(lat)] + [(0, 384), (1, 512)]
                                    for ii, (kc, off) in enumerate(kvs):
                                        MM(O[:, blk * 128:(blk + 1) * 128], VA[:, kc, :], pt[:, off:off + 128], ii == 0, ii == len(kvs) - 1,
                                           [bVA, bpt], [bO], inc=(ii == len(kvs) - 1))
                                    if blk == 3:
                                        norm_d(O, bO, rows, h, t0, n)

                                steps.append((front, back))
                    run_pipe(steps)

                for ui, (ty, hk) in enumerate(units):
                    k = ui % NSET
                    prep(ty, hk, QTs[k], KZs[k], VAs[k], bQKs[k], bVAs[k])
                    (attend_global if ty == 0 else attend_window)(hk, QTs[k], KZs[k], VAs[k], YOs[k], bQKs[k], bVAs[k], bYOs[k])
                    DMA(YT_d[4 + 2 * ty + hk], YOs[k][:], [bYOs[k]], ())
            S.barrier()

        def phase_E(l, need_ctx):
            with ExitStack() as st:
                WO = sb(st, "WO", [128, 8, 1024], BF16)
                W1 = sb(st, "W1", [128, 8, 4096], BF16)
                W2 = sb(st, "W2", [128, 32, 1024], BF16)
                bWO, bW1, bW2 = Buf("WO"), Buf("W1"), Buf("W2")
                DMA(WO[:], wout_d[l].rearrange("(j p) c -> p j c", p=128), (), [bWO], q="pool")
                for q4 in range(4):
                    DMA(W1[:, :, q4 * 1024:(q4 + 1) * 1024], w1_d[l, :, q4 * 1024:(q4 + 1) * 1024].rearrange("(j p) c -> p j c", p=128), (), [bW1], q="pool")
                for q4 in range(4):
                    DMA(W2[:, q4 * 8:(q4 + 1) * 8, :], w2_d[l, q4 * 1024:(q4 + 1) * 1024, :].rearrange("(k p) c -> p k c", p=128), (), [bW2], q="pool")
                NE = 256
                _xt = sb(st, "eXT", [128, 8, NE])
                _bxt = Buf("eXT")
                XT = [_xt, _xt]
                bXT = [_bxt, _bxt]
                _yt = sb(st, "eYT", [128, 8, NE], BF16)
                _byt = Buf("eYT")
                YTs = [_yt, _yt]
                bYTs = [_byt, _byt]
                X1 = [sb(st, "eX1%d" % i, [128, 8, NE]) for i in range(2)]
                bX1 = [Buf("eX1%d" % i) for i in range(2)]
                SQ = sb(st, "eSQ", [128, 8, NE], BF16)
                bSQ = Buf("eSQ")
                RS = sb(st, "eRS", [128, NE])
                bRS = Buf("eRS")
                H2 = [sb(st, "eH2%d" % i, [128, 8, NE], BF16) for i in range(2)]
                bH2 = [Buf("eH2%d" % i) for i in range(2)]
                RL = [sb(st, "eRL%d" % i, [128, NE]) for i in range(2)]
                bRL = [Buf("eRL%d" % i) for i in range(2)]
                AK = [sb(st, "eAK%d" % i, [128, NE], BF16) for i in range(3)]
                bAK = [Buf("eAK%d" % i) for i in range(3)]
                starts = list(range(0 if need_ctx else LC, T, NE))

                def load(gi):
                    t0 = starts[gi]
                    DMA(XT[gi % 2][:], xT_d[:, :, t0:t0 + NE].rearrange("j p t -> p j t"), (), [bXT[gi % 2]])
                    DMA(YTs[gi % 2][:], YT_d[:, :, t0:t0 + NE].rearrange("j p t -> p j t"), (), [bYTs[gi % 2]])

                def head(gi):
                    t0 = starts[gi]
                    w = 1 if t0 < LC else 0
                    xt, bxt = XT[gi % 2], bXT[gi % 2]
                    yt, byt = YTs[gi % 2], bYTs[gi % 2]
                    x1, bx1 = X1[gi % 2], bX1[gi % 2]
                    h2, bh2 = H2[gi % 2], bH2[gi % 2]
                    p, bp = PS[3], bPS[3]
                    for fo in range(8):
                        for k in range(8):
                            MM(p[:, (fo % 2) * NE:(fo % 2 + 1) * NE], WO[:, k, fo * 128:(fo + 1) * 128], yt[:, k, :], k == 0, k == 7, [bWO, byt], [bp], inc=(k == 7))
                        STT(x1[:, fo, :], p[:, (fo % 2) * NE:(fo % 2 + 1) * NE], MODS[:, 16 + fo, w:w + 1], xt[:, fo, :], ALU.mult, ALU.add, [bp, bMODS, bxt], [bx1])
                    ACT(SQ[:], x1[:], AF.Square, [bx1], [bSQ])
                    for j in range(8):
                        MM(p[:, 0:NE], CSTb[:, C_ONE, :], SQ[:, j, :], j == 0, j == 7, [bSQ, bCSTb], [bp], inc=(j == 7))
                    ACT(RS[:], p[:, 0:NE], AF.Sqrt, [bp], [bRS], bias=EPS, scale=1.0 / D)
                    RECIP(RS[:], RS[:], [bRS], [bRS])
                    TT("dve", xt[:], x1[:], RS[:].unsqueeze(1).to_broadcast([128, 8, NE]), ALU.mult, [bx1, bRS], [bxt])
                    for j in range(8):
                        ACT(h2[:, j, :], xt[:, j, :], AF.Identity, [bxt, bG, bMODS], [bh2],
                            bias=MODS[:, 24 + j, w:w + 1], scale=G2[:, j, w:w + 1])

                steps = []
                for gi, t0 in enumerate(starts):
                    w = 1 if t0 < LC else 0
                    for kh in range(32):
                        ki = len(steps)
                        pu, bpu = PS[ki % 3], bPS[ki % 3]
                        rl, brl = RL[ki % 2], bRL[ki % 2]
                        ak, bak = AK[ki % 3], bAK[ki % 3]

                        def front(gi=gi, kh=kh, pu=pu, bpu=bpu, rl=rl, brl=brl, ak=ak, bak=bak):
                            h2, bh2 = H2[gi % 2], bH2[gi % 2]
                            for j in range(8):
                                MM(pu[:, 0:NE], W1[:, j, kh * 128:(kh + 1) * 128], h2[:, j, :], j == 0, j == 7, [bW1, bh2], [bpu], inc=(j == 7))
                            ACT(rl[:], pu[:, 0:NE], AF.Relu, [bpu], [brl])
                            TT("dve", ak[:], rl[:], pu[:, 0:NE], ALU.mult, [brl, bpu], [bak])
                            if kh == 12 and gi + 1 < len(starts):
                                load(gi + 1)
                            if kh == 16 and gi + 1 < len(starts):
                                head(gi + 1)

                        def back(gi=gi, kh=kh, ak=ak, bak=bak, t0=t0, w=w):
                            x1, bx1 = X1[gi % 2], bX1[gi % 2]
                            for fo in range(8):
                                pd, bpd = PS[4 + fo // 2], bPS[4 + fo // 2]
                                MM(pd[:, (fo % 2) * NE:(fo % 2 + 1) * NE], W2[:, kh, fo * 128:(fo + 1) * 128], ak[:], (kh == 0 and fo % 2 == 0), kh == 31,
                                   [bW2, bak], [bpd], inc=(kh == 31 or fo == 7))
                            if kh == 31:
                                for fo in range(8):
                                    pd, bpd = PS[4 + fo // 2], bPS[4 + fo // 2]
                                    STT(x1[:, fo, :], pd[:, (fo % 2) * NE:(fo % 2 + 1) * NE], MODS[:, 40 + fo, w:w + 1], x1[:, fo, :], ALU.mult, ALU.add,
                                        [bpd, bMODS, bx1], [bx1])
                                DMA(xT_d[:, :, t0:t0 + NE].rearrange("j p t -> p j t"), x1[:], [bx1], ())

                        steps.append((front, back))
                load(0)
                head(0)
                LA = 2
                for i in range(len(steps) + LA):
                    if i < len(steps):
                        steps[i][0]()
                    if i >= LA:
                        steps[i - LA][1]()
            S.barrier()

        def phase_U():
            with ExitStack() as st:
                XT = [sb(st, "uXT%d" % i, [128, 8, 512]) for i in range(2)]
                bXT = [Buf("uXT%d" % i) for i in range(2)]
                OS = [sb(st, "uOS%d" % i, [128, D]) for i in range(2)]
                bOS = [Buf("uOS%d" % i) for i in range(2)]
                it = 0
                for gi, (t0, n, which) in enumerate(GROUPS[1:]):
                    xt, bxt = XT[gi % 2], bXT[gi % 2]
                    DMA(xt[:], xT_d[:, :, t0:t0 + n].rearrange("j p t -> p j t"), (), [bxt])
                    for s in range(4):
                        os_, bos = OS[it % 2], bOS[it % 2]
                        for half in range(2):
                            p, bp = PS[(it * 2 + half) % 8], bPS[(it * 2 + half) % 8]
                            for jj in range(4):
                                j = half * 4 + jj
                                TR(p[:, jj * 128:(jj + 1) * 128], xt[:, j, s * 128:(s + 1) * 128], ident, [bxt, bCST], [bp])
                            CP("act" if half else "dve", os_[:, half * 512:(half + 1) * 512], p[:, 0:512], [bp], [bos])
                        it += 1
                        tt = t0 - LC + s * 128
                        final_toks.append(DMA(out_d[tt:tt + 128, :], os_[:], [bos], ()))
            S.barrier()

        phase_T()
        done = False
        for l in range(n_layers):
            need_ctx = l < n_layers - 1
            for nm, fn in (("0", lambda: phase_0(l)), ("A", lambda: phase_A(l)), ("B", lambda: phase_B(l)), ("C", lambda: phase_C(l)),
                           ("D", lambda: phase_D(l, need_ctx)), ("E", lambda: phase_E(l, need_ctx))):
                if stop_after is not None and isinstance(stop_after[0], tuple):
                    if (nm, l) not in stop_after:
                        continue
                fn()
                if stop_after == (nm, l):
                    done = True
                    break
            if done:
                break
        if stop_after is None:
            phase_U()
        else:
            with ExitStack() as st:
                Z = sb(st, "ZZ", [128, D])
                bZ = Buf("ZZ")
                MEMSET("dve", Z[:], 0.0, [bZ])
                final_toks.append(DMA(out_d[0:128, :], Z[:], [bZ], ()))
        S.barrier()
        with nc.Block() as block:
            S.emit(block)
    return nc, S


def _consts():
    p = np.arange(128)[:, None]
    f = np.arange(128)[None, :]
    cst = np.zeros((NCST, 128, 128), np.float32)
    cst[C_ID] = (p == f)
    cst[C_LT] = (p > f)
    cst[C_UT] = (p < f)
    cst[C_LE] = (p <= f)
    cst[C_GE] = (p >= f)
    cst[C_ONE] = 1.0
    cst[C_BD] = (p // 64 == f // 64)
    cst[C_PERM] = (p == (f ^ 16))
    cst[C_NEGF] = NEG * (p > f)
    cst[C_NEGB] = NEG * (p < f)
    return cst


def _rope():
    t = np.arange(LL)
    row = (t // 64).astype(np.float32)
    col = (t % 64).astype(np.float32)
    inv = (np.float32(10000.0) ** (-np.arange(0, 32, 2, dtype=np.float32) / np.float32(32))).astype(np.float32)
    tab = np.zeros((2, 128, LL), np.float32)
    for d in range(128):
        dd = d % 64
        axis, half, fi = dd // 32, (dd % 32) // 16, dd % 16
        ang = ((row if axis == 0 else col) * inv[fi]).astype(np.float32)
        tab[0, d] = np.cos(ang)
        tab[1, d] = (-np.sin(ang)) if half == 0 else np.sin(ang)
    return tab


def _pack(inputs):
    f = lambda a: np.ascontiguousarray(np.asarray(a, dtype=np.float32))
    pp = np.zeros((2, 128, NPP), np.float32)
    rp = np.zeros((2, NRP), np.float32)
    lruw = np.zeros((2, 2, 2, 2, 128, 128), np.float32)
    for l in range(2):
        pp[l, :, 0:8] = f(inputs["g_mix"])[l].reshape(8, 128).T
        pp[l, :, 8:16] = f(inputs["g_ffn"])[l].reshape(8, 128).T
        pp[l, :, 16:64] = f(inputs["b_mod"])[l].reshape(48, 128).T
        cw = f(inputs["ssd_conv_w"])[l]
        for ci in range(6):
            pp[l, :, 64 + ci * 4:64 + ci * 4 + 4] = cw[:, ci * 128:(ci + 1) * 128].T
        pp[l, :, 88:94] = f(inputs["ssd_conv_b"])[l].reshape(6, 128).T
        lw = f(inputs["lru_conv_w"])[l]
        for c2 in range(2):
            pp[l, :, 94 + c2 * 4:94 + c2 * 4 + 4] = lw[:, c2 * 128:(c2 + 1) * 128].T
        pp[l, :, 102:104] = f(inputs["lru_conv_b"])[l].reshape(2, 128).T
        pp[l, :, 104:108] = f(inputs["lru_lambda"])[l].reshape(4, 128).T
        pp[l, :, 108:112] = f(inputs["lru_b_a"])[l].reshape(4, 128).T
        pp[l, :, 112:116] = f(inputs["lru_b_i"])[l].reshape(4, 128).T
        for k, nm in enumerate(("gqa_q_norm", "gqa_k_norm", "swa_q_norm", "swa_k_norm")):
            pp[l, :, 116 + k] = np.tile(f(inputs[nm])[l], 2)
        rp[l, 0:256] = f(inputs["ssd_norm_g"])[l]
        rp[l, 256:260] = f(inputs["ssd_d"])[l]
        rp[l, 260:268] = f(inputs["ssd_dt_bias"])[l].reshape(8)
        rp[l, 268:276] = f(inputs["ssd_a_log"])[l].reshape(8)
        rp[l, 276:280] = f(inputs["swa_sink"])[l]
        for dr in range(2):
            for gt, nm in enumerate(("lru_w_a", "lru_w_i")):
                w = f(inputs[nm])[l, dr]
                for c2 in range(2):
                    for bl in range(2):
                        lruw[l, dr, gt, c2, bl * 64:(bl + 1) * 64, bl * 64:(bl + 1) * 64] = w[c2 * 2 + bl]
    return pp, rp, lruw


def make_in_maps(inputs, cores):
    f = lambda a: np.ascontiguousarray(np.asarray(a, dtype=np.float32))
    pp, rp, lruw = _pack(inputs)
    cst = _consts()
    rope = _rope()
    shared = {"w_mod": f(inputs["w_mod"]), "w_in": f(inputs["w_in"]), "w_out": f(inputs["w_out"]),
              "w_ffn1": f(inputs["w_ffn1"]), "w_ffn2": f(inputs["w_ffn2"]), "pp": pp, "rp": rp, "lruw": lruw,
              "cst": cst, "rope": rope}
    maps = []
    for b in cores:
        c2 = np.stack([f(inputs["c"])[b], f(inputs["c_ctx"])], axis=0)
        c2 = np.ascontiguousarray(c2.reshape(2, 8, 128).transpose(2, 1, 0))
        m = dict(shared)
        m["x"] = f(inputs["x"])[b]
        m["ctx"] = f(inputs["ctx"])[b]
        m["c2"] = c2
        maps.append(m)
    return maps


_NC_CACHE = {}


def kernel(**inputs):
    if "nc" not in _NC_CACHE:
        _NC_CACHE["nc"] = build()[0]
    nc = _NC_CACHE["nc"]
    maps = make_in_maps(inputs, range(4))
    res = run_bass_kernel_spmd(nc, maps, core_ids=list(range(4)))
    return np.stack([np.asarray(r["out"], dtype=np.float32) for r in res.results], axis=0)
```
